# Optimizing a Trainium2 kernel written in Bass

```python
import math
import jax, jax.numpy as jnp
from jax import lax
import numpy as np

D_MODEL = 1024
BATCH = 2
SEQ = 8192
DEPTH = 4

GRID_W = 64
CTX_LEN = 256
HEAD_DIM = 64
ATTN_HEADS = 8
ATTN_KV_HEADS = 2
GQA_GROUP = ATTN_HEADS // ATTN_KV_HEADS
ATTN_WIDTH = ATTN_HEADS * HEAD_DIM
KV_WIDTH = ATTN_KV_HEADS * HEAD_DIM
HYENA_WIDTH = D_MODEL // 4
NA_HEADS = 4
NA_WIDTH = NA_HEADS * HEAD_DIM
MIX_WIDTH = ATTN_WIDTH + HYENA_WIDTH + NA_WIDTH
IN_WIDTH = ATTN_WIDTH + 2 * KV_WIDTH + 3 * HYENA_WIDTH + 3 * NA_WIDTH
FFN_HIDDEN = 2816
Q_BLOCK = 128
NA_ROWS = 8
NA_COLS = 16
SHORT_CONV = 3
FILTER_BANDS = 16
FILTER_EMB = 2 * FILTER_BANDS + 1
FILTER_HIDDEN = 64
DECAY_TARGET = 1e-2
FAST_DECAY_PCT = 0.3
SLOW_DECAY_PCT = 1.5
ROPE_THETA = 10000.0
EPS = 1e-6
N_MOD = 9

kernel_name = 'hybrid_attn_hyena_natten_dit'


def rms_norm(x, g):
    xf = x.astype(jnp.float32)
    y = xf * lax.rsqrt(jnp.mean(xf * xf, axis=-1, keepdims=True) + EPS)
    return (y * g.astype(jnp.float32)).astype(x.dtype)


def modulate(n, shift, scale):
    return n * (1 + scale) + shift


def adaln_params(cond, w, b):
    m = jax.nn.silu(cond) @ w + b
    return jnp.split(m, N_MOD, axis=-1)


def swiglu(x, wg, wu, wd):
    return (jax.nn.silu(x @ wg) * (x @ wu)) @ wd


def to_heads(t, n_heads):
    b, n, _ = t.shape
    return t.reshape(b, n, n_heads, HEAD_DIM).transpose(0, 2, 1, 3)


def from_heads(t):
    b, h, n, dh = t.shape
    return t.transpose(0, 2, 1, 3).reshape(b, n, h * dh)


def split_proj(p):
    sizes = [ATTN_WIDTH, KV_WIDTH, KV_WIDTH, 3 * HYENA_WIDTH, NA_WIDTH, NA_WIDTH, NA_WIDTH]
    return jnp.split(p, np.cumsum(sizes)[:-1].tolist(), axis=-1)


def grid_rope_tables(n, dtype):
    t = jnp.arange(n, dtype=jnp.int32)
    rows = (t // GRID_W).astype(jnp.float32)
    cols = (t % GRID_W).astype(jnp.float32)
    nf = HEAD_DIM // 4
    inv = ROPE_THETA ** (-jnp.arange(nf, dtype=jnp.float32) / nf)
    ang = jnp.stack([rows[:, None] * inv, cols[:, None] * inv], axis=1)
    return jnp.cos(ang).astype(dtype), jnp.sin(ang).astype(dtype)


def axial_rope(x, cos, sin):
    xs = x.reshape(x.shape[:-1] + (2, 2, HEAD_DIM // 4))
    x1 = xs[..., 0, :]
    x2 = xs[..., 1, :]
    out = jnp.stack([x1 * cos - x2 * sin, x1 * sin + x2 * cos], axis=-2)
    return out.reshape(x.shape)


def dense_attention(q, k, v):
    s = jnp.einsum('bhgqd,bhkd->bhgqk', q, k).astype(jnp.float32) * (HEAD_DIM ** -0.5)
    p = jax.nn.softmax(s, axis=-1).astype(v.dtype)
    return jnp.einsum('bhgqk,bhkd->bhgqd', p, v)


def gqa_latent(q, k, v, kc, vc):
    b, hkv, g, s, dh = q.shape
    k_all = jnp.concatenate([kc, k], axis=2)
    v_all = jnp.concatenate([vc, v], axis=2)
    nb = s // Q_BLOCK
    qb = jnp.moveaxis(q.reshape(b, hkv, g, nb, Q_BLOCK, dh), 3, 0)
    o = lax.map(lambda qblk: dense_attention(qblk, k_all, v_all), qb)
    return jnp.moveaxis(o, 0, 3).reshape(b, hkv, g, s, dh)


def neighbourhood_attention(q, k, v, kc, vc, rpb):
    b, h, s, dh = q.shape
    rows = s // GRID_W
    nr = min(NA_ROWS, rows)
    kg = k.reshape(b, h, rows, GRID_W, dh)
    vg = v.reshape(b, h, rows, GRID_W, dh)
    q_rows = jnp.moveaxis(q.reshape(b, h, rows, GRID_W, dh), 2, 0)
    cols = jnp.arange(GRID_W)
    col_start = jnp.clip(cols - NA_COLS // 2, 0, GRID_W - NA_COLS)
    col_idx = col_start[:, None] + jnp.arange(NA_COLS)[None, :]
    rpb_cols = rpb[:, :, col_idx - cols[:, None] + NA_COLS - 1]
    scale = HEAD_DIM ** -0.5
    n_nb = nr * NA_COLS

    def row_block(args):
        r, q_row = args
        rs = jnp.clip(r - NA_ROWS // 2, 0, rows - nr)
        k_sel = lax.dynamic_slice_in_dim(kg, rs, nr, axis=2)[:, :, :, col_idx]
        v_sel = lax.dynamic_slice_in_dim(vg, rs, nr, axis=2)[:, :, :, col_idx]
        row_off = rs + jnp.arange(nr) - r + NA_ROWS - 1
        bias = jnp.take(rpb_cols, row_off, axis=1).transpose(0, 2, 1, 3)
        s_nb = jnp.einsum('bhqd,bhrqcd->bhqrc', q_row, k_sel).astype(jnp.float32) * scale
        s_nb = (s_nb + bias[None].astype(jnp.float32)).reshape(b, h, GRID_W, n_nb)
        s_ctx = jnp.einsum('bhqd,bhkd->bhqk', q_row, kc).astype(jnp.float32) * scale
        p = jax.nn.softmax(jnp.concatenate([s_nb, s_ctx], axis=-1), axis=-1).astype(v.dtype)
        p_nb = p[..., :n_nb].reshape(b, h, GRID_W, nr, NA_COLS)
        return (jnp.einsum('bhqrc,bhrqcd->bhqd', p_nb, v_sel)
                + jnp.einsum('bhqk,bhkd->bhqd', p[..., n_nb:], vc))

    o = lax.map(row_block, (jnp.arange(rows, dtype=jnp.int32), q_rows))
    return jnp.moveaxis(o, 0, 2).reshape(b, h, s, dh)


def implicit_filters(n, w1, b1, w2, b2, w3, freq):
    pos = jnp.arange(n, dtype=jnp.float32)
    t = pos / max(n - 1, 1)
    bands = jnp.linspace(1e-4, FILTER_BANDS - 1, FILTER_BANDS, dtype=jnp.float32)
    ang = (2.0 * math.pi / n) * pos[:, None] * bands[None, :]
    feats = jnp.concatenate([t[:, None], jnp.cos(ang), -jnp.sin(ang)], axis=-1)
    hdn = jnp.sin(freq * (feats @ w1 + b1))
    hdn = jnp.sin(freq * (hdn @ w2 + b2))
    filt = (hdn @ w3).astype(jnp.float32).reshape(n, 2, HYENA_WIDTH)
    deltas = jnp.linspace(math.log(DECAY_TARGET) / SLOW_DECAY_PCT, math.log(DECAY_TARGET) / FAST_DECAY_PCT,
                          HYENA_WIDTH, dtype=jnp.float32)
    decay = jnp.exp(-t[:, None] * jnp.abs(deltas)[None, :])
    return filt * decay[:, None, :]


def hyena_mixer(u, conv_w, conv_b, w1, b1, w2, b2, w3, freq, skip):
    b, n, ch = u.shape
    uc = lax.conv_general_dilated(u, conv_w[:, None, :], window_strides=(1,), padding=[(1, 1)],
                                  dimension_numbers=('NWC', 'WIO', 'NWC'), feature_group_count=ch) + conv_b
    x0, x1, v = jnp.split(uc, 3, axis=-1)
    filt = implicit_filters(n, w1, b1, w2, b2, w3, freq)
    k = jnp.concatenate([filt[:, 0], jnp.zeros((1, HYENA_WIDTH), jnp.float32), filt[:0:-1, 1]], axis=0)
    k = k / jnp.sum(jnp.abs(k), axis=0, keepdims=True)
    z = (v * x1).astype(jnp.float32)
    y = jnp.fft.irfft(jnp.fft.rfft(z, n=2 * n, axis=1) * jnp.fft.rfft(k, n=2 * n, axis=0)[None],
                      n=2 * n, axis=1)[:, :n]
    y = y + z * skip.astype(jnp.float32)
    return y.astype(u.dtype) * x0


def setup_inputs(seed: int = 0) -> dict:
    key = jax.random.key(seed)
    keys = iter(jax.random.split(key, 40))
    D = D_MODEL

    def nrm(shape, s):
        return jax.random.normal(next(keys), shape, jnp.float32) * s

    return {
        'x': nrm((BATCH, SEQ, D), 1.0),
        'c': nrm((BATCH, D), 1.0),
        'ctx': nrm((BATCH, CTX_LEN, D), 1.0),
        'c_ctx': nrm((D,), 1.0),
        'w_ada': nrm((DEPTH, D, N_MOD * D), 0.5 * D ** -0.5),
        'b_ada': nrm((DEPTH, N_MOD * D), 0.01),
        'g_ffn1': 1.0 + nrm((DEPTH, D), 0.02),
        'w_ffn1_gate': nrm((DEPTH, D, FFN_HIDDEN), D ** -0.5),
        'w_ffn1_up': nrm((DEPTH, D, FFN_HIDDEN), D ** -0.5),
        'w_ffn1_down': nrm((DEPTH, FFN_HIDDEN, D), FFN_HIDDEN ** -0.5),
        'g_mix': 1.0 + nrm((DEPTH, D), 0.02),
        'w_in': nrm((DEPTH, D, IN_WIDTH), D ** -0.5),
        'w_out': nrm((DEPTH, MIX_WIDTH, D), MIX_WIDTH ** -0.5),
        'g_q_attn': 1.0 + nrm((DEPTH, HEAD_DIM), 0.02),
        'g_k_attn': 1.0 + nrm((DEPTH, HEAD_DIM), 0.02),
        'conv_w': nrm((DEPTH, SHORT_CONV, 3 * HYENA_WIDTH), SHORT_CONV ** -0.5),
        'conv_b': nrm((DEPTH, 3 * HYENA_WIDTH), 0.01),
        'filt_w1': nrm((DEPTH, FILTER_EMB, FILTER_HIDDEN), FILTER_EMB ** -0.5),
        'filt_b1': nrm((DEPTH, FILTER_HIDDEN), 0.02),
        'filt_w2': nrm((DEPTH, FILTER_HIDDEN, FILTER_HIDDEN), FILTER_HIDDEN ** -0.5),
        'filt_b2': nrm((DEPTH, FILTER_HIDDEN), 0.02),
        'filt_w3': nrm((DEPTH, FILTER_HIDDEN, 2 * HYENA_WIDTH), FILTER_HIDDEN ** -0.5),
        'filt_freq': 1.0 + nrm((DEPTH, FILTER_HIDDEN), 0.02),
        'hyena_skip': nrm((DEPTH, HYENA_WIDTH), 1.0),
        'g_q_na': 1.0 + nrm((DEPTH, HEAD_DIM), 0.02),
        'g_k_na': 1.0 + nrm((DEPTH, HEAD_DIM), 0.02),
        'na_rpb': nrm((DEPTH, NA_HEADS, 2 * NA_ROWS - 1, 2 * NA_COLS - 1), 0.1),
        'g_ffn2': 1.0 + nrm((DEPTH, D), 0.02),
        'w_ffn2_gate': nrm((DEPTH, D, FFN_HIDDEN), D ** -0.5),
        'w_ffn2_up': nrm((DEPTH, D, FFN_HIDDEN), D ** -0.5),
        'w_ffn2_down': nrm((DEPTH, FFN_HIDDEN, D), FFN_HIDDEN ** -0.5),
    }


def reference(x, c, ctx, c_ctx, w_ada, b_ada, g_ffn1, w_ffn1_gate, w_ffn1_up, w_ffn1_down, g_mix, w_in, w_out,
              g_q_attn, g_k_attn, conv_w, conv_b, filt_w1, filt_b1, filt_w2, filt_b2, filt_w3, filt_freq,
              hyena_skip, g_q_na, g_k_na, na_rpb, g_ffn2, w_ffn2_gate, w_ffn2_up, w_ffn2_down):
    b, s, _ = x.shape
    n_ctx = ctx.shape[1]
    cos, sin = grid_rope_tables(s, x.dtype)
    h, hc = x, ctx
    for l in range(DEPTH):
        last = l == DEPTH - 1
        sh1, sc1, gt1, sh2, sc2, gt2, sh3, sc3, gt3 = [m[:, None, :] for m in adaln_params(c, w_ada[l], b_ada[l])]
        csh1, csc1, cgt1, csh2, csc2, cgt2, csh3, csc3, cgt3 = adaln_params(c_ctx, w_ada[l], b_ada[l])
        ffn1 = (w_ffn1_gate[l], w_ffn1_up[l], w_ffn1_down[l])
        ffn2 = (w_ffn2_gate[l], w_ffn2_up[l], w_ffn2_down[l])
        filt = (filt_w1[l], filt_b1[l], filt_w2[l], filt_b2[l], filt_w3[l], filt_freq[l], hyena_skip[l])

        h = h + 0.5 * gt1 * swiglu(modulate(rms_norm(h, g_ffn1[l]), sh1, sc1), *ffn1)
        hc = hc + 0.5 * cgt1 * swiglu(modulate(rms_norm(hc, g_ffn1[l]), csh1, csc1), *ffn1)

        px = modulate(rms_norm(h, g_mix[l]), sh2, sc2) @ w_in[l]
        pc = modulate(rms_norm(hc, g_mix[l]), csh2, csc2) @ w_in[l]
        aq, ak, av, hy, nq, nk, nv = split_proj(px)
        caq, cak, cav, chy, cnq, cnk, cnv = split_proj(pc)

        kca = rms_norm(to_heads(cak, ATTN_KV_HEADS), g_k_attn[l])
        vca = to_heads(cav, ATTN_KV_HEADS)
        kcn = rms_norm(to_heads(cnk, NA_HEADS), g_k_na[l])
        vcn = to_heads(cnv, NA_HEADS)

        qa = axial_rope(rms_norm(to_heads(aq, ATTN_HEADS), g_q_attn[l]), cos, sin)
        qa = qa.reshape(b, ATTN_KV_HEADS, GQA_GROUP, s, HEAD_DIM)
        ka = axial_rope(rms_norm(to_heads(ak, ATTN_KV_HEADS), g_k_attn[l]), cos, sin)
        ya = gqa_latent(qa, ka, to_heads(av, ATTN_KV_HEADS), kca, vca)
        ya = from_heads(ya.reshape(b, ATTN_HEADS, s, HEAD_DIM))
        yb = hyena_mixer(hy, conv_w[l], conv_b[l], *filt)
        qn = rms_norm(to_heads(nq, NA_HEADS), g_q_na[l])
        kn = rms_norm(to_heads(nk, NA_HEADS), g_k_na[l])
        yc = from_heads(neighbourhood_attention(qn, kn, to_heads(nv, NA_HEADS), kcn, vcn, na_rpb[l]))

        h = h + gt2 * (jnp.concatenate([ya, yb, yc], axis=-1) @ w_out[l])
        h = h + 0.5 * gt3 * swiglu(modulate(rms_norm(h, g_ffn2[l]), sh3, sc3), *ffn2)

        if not last:
            qca = rms_norm(to_heads(caq, ATTN_HEADS), g_q_attn[l]).reshape(b, ATTN_KV_HEADS, GQA_GROUP, n_ctx, HEAD_DIM)
            yca = from_heads(dense_attention(qca, kca, vca).reshape(b, ATTN_HEADS, n_ctx, HEAD_DIM))
            ycb = hyena_mixer(chy, conv_w[l], conv_b[l], *filt)
            qcn = rms_norm(to_heads(cnq, NA_HEADS), g_q_na[l])[:, :, None]
            ycn = from_heads(dense_attention(qcn, kcn, vcn)[:, :, 0])
            hc = hc + cgt2 * (jnp.concatenate([yca, ycb, ycn], axis=-1) @ w_out[l])
            hc = hc + 0.5 * cgt3 * swiglu(modulate(rms_norm(hc, g_ffn2[l]), csh3, csc3), *ffn2)
    return h
```

```python
import math
import numpy as np
import ml_dtypes
import concourse.bass as bass
import concourse.mybir as mybir
from concourse.bass_utils import run_bass_kernel_spmd

F32 = mybir.dt.float32
BF16 = mybir.dt.bfloat16
AF = mybir.ActivationFunctionType
ALU = mybir.AluOpType
AX = mybir.AxisListType

D = 1024
DEPTH = 4
SEQ = 8192
NT = 2048
NC = 256
NTOT = NT + NC
FF = 2816
NFC = FF // 128
GRID_W = 64
EPS = 1e-6
BIG = 30000.0
G4 = [[0, 1, 2, 3], [4, 5, 6, 7]]

DEBUG_OUT = False
ENGS = ("tensor", "vector", "scalar", "gpsimd", "sync")


class Op:
    __slots__ = ("eng", "fn", "kind", "deps", "needed", "sem", "val", "idx", "prev")

    def __init__(self, eng, fn, kind):
        self.eng = eng
        self.fn = fn
        self.kind = kind
        self.deps = []
        self.needed = False
        self.sem = None
        self.val = None


class Prog:
    def __init__(self, nc, n_dma_sems=8):
        self.nc = nc
        self.stack = []
        self.csem = {}
        self.ccount = {}
        self.csem_pool = {}
        for e in ENGS:
            self.csem[e] = self._newsem("c_" + e)
            self.ccount[e] = 0
        self.dsem = {}
        self.dval = {}
        self.dnext = {}
        for e in ("sync", "gpsimd", "scalar"):
            self.dsem[e] = [self._newsem("d_%s%d" % (e, i)) for i in range(n_dma_sems)]
            self.dval[e] = [0] * n_dma_sems
            self.dnext[e] = 0
        self.ccsem = self._newsem("cc")
        self.ccval = 0
        self.known = {e: {} for e in ENGS}
        self.reset_phase()

    def _newsem(self, name):
        cm = self.nc.semaphore(name)
        s = cm.__enter__()
        self.stack.append(cm)
        return s

    def reset_phase(self):
        self.ops = {e: [] for e in ENGS}
        self.last_w = {}
        self.readers = {}

    def add(self, eng, fn, reads=(), writes=(), kind="c"):
        op = Op(eng, fn, kind)
        deps = {}
        for k in reads:
            w = self.last_w.get(k)
            if w is not None:
                deps[id(w)] = w
        for k in writes:
            w = self.last_w.get(k)
            if w is not None:
                deps[id(w)] = w
            for r in self.readers.get(k, ()):
                deps[id(r)] = r
        for d in deps.values():
            if d is op:
                continue
            if d.eng == eng and d.kind == "c" and kind == "c" and eng == "tensor":
                continue
            op.deps.append(d)
            d.needed = True
        for k in reads:
            self.readers.setdefault(k, []).append(op)
        for k in writes:
            self.last_w[k] = op
            self.readers[k] = []
        self.ops[eng].append(op)
        return op

    def emit(self, block):
        for e in ENGS:
            for op in self.ops[e]:
                if op.kind == "c":
                    if op.needed:
                        self.ccount[e] += 1
                        op.sem, op.val = self.csem[e], self.ccount[e]
                elif op.kind == "d":
                    i = self.dnext[e]
                    self.dnext[e] = (i + 1) % len(self.dsem[e])
                    op.idx = i
                    op.sem = self.dsem[e][i]
                    op.prev = self.dval[e][i]
                    self.dval[e][i] += 16
                    op.val = self.dval[e][i]
                elif op.kind == "cc":
                    self.ccval += 1
                    op.sem, op.val = self.ccsem, self.ccval
        prog = self

        def make(e):
            def body(engine):
                known = prog.known[e]

                def wait(sem, val):
                    key = id(sem)
                    if known.get(key, 0) >= val:
                        return
                    known[key] = val
                    engine.wait_ge(sem, val)

                for op in prog.ops[e]:
                    for d in op.deps:
                        wait(d.sem, d.val)
                    if op.kind == "d" and op.prev > 0:
                        wait(op.sem, op.prev)
                    ins = op.fn(engine)
                    if op.kind == "c":
                        if op.needed:
                            ins.then_inc(op.sem, 1)
                    elif op.kind == "d":
                        ins.then_inc(op.sem, 16)
                    else:
                        ins.then_inc(op.sem)
                if e in prog.dsem:
                    for i, s in enumerate(prog.dsem[e]):
                        if prog.dval[e][i] > 0:
                            wait(s, prog.dval[e][i])
                if e == "gpsimd" and prog.ccval > 0:
                    wait(prog.ccsem, prog.ccval)
            return body

        for e in ENGS:
            getattr(block, e)(make(e))
        self.reset_phase()

    def close(self):
        for cm in reversed(self.stack):
            cm.__exit__(None, None, None)


class Ctx:
    pass


def mm(P, out, lhsT, rhs, start, stop, reads, writes):
    return P.add("tensor", lambda e: e.matmul(out, lhsT, rhs, start=start, stop=stop), reads, writes)


def dma(P, q, out, in_, reads, writes, slow=False):
    if slow:
        return P.add(q, lambda e: e.dma_start(out=out, in_=in_, allow_slow_non_contiguous=True), reads, writes,
                     kind="d")
    return P.add(q, lambda e: e.dma_start(out=out, in_=in_), reads, writes, kind="d")


def act(P, out, in_, func, reads, writes, bias=0.0, scale=1.0):
    return P.add("scalar", lambda e: e.activation(out, in_, func, bias=bias, scale=scale), reads, writes)


def tt(P, eng, out, in0, in1, op, reads, writes):
    return P.add(eng, lambda e: e.tensor_tensor(out, in0, in1, op), reads, writes)


def ts(P, eng, out, in0, s1, s2, op0, op1, reads, writes):
    if op1 is None:
        return P.add(eng, lambda e: e.tensor_scalar(out, in0, s1, None, op0), reads, writes)
    return P.add(eng, lambda e: e.tensor_scalar(out, in0, s1, s2, op0, op1), reads, writes)


def stt(P, eng, out, in0, scalar, in1, op0, op1, reads, writes):
    return P.add(eng, lambda e: e.scalar_tensor_tensor(out, in0, scalar, in1, op0, op1), reads, writes)


def cp(P, eng, out, in_, reads, writes):
    if eng == "scalar":
        return P.add(eng, lambda e: e.copy(out, in_), reads, writes)
    return P.add(eng, lambda e: e.tensor_copy(out, in_), reads, writes)


def na_iters(ti):
    out = []
    for c in range(8):
        lr = 8 * ti - 4 + 2 * c
        if lr < 0:
            out += [("top", j, c) for j in range(4)]
        elif lr >= 32:
            out += [("bot", j, c) for j in range(4)]
        else:
            out.append(("own", 0, c))
    return out


TILES = [(0, 512, 0), (512, 512, 0), (1024, 512, 0), (1536, 512, 0), (2048, 256, 1)]


def build_nc(stop_after=None):
    nc = bass.Bass("TRN2", target_bir_lowering=False)
    C = Ctx()
    C.nc = nc

    def din(name, shape, dt=F32):
        return nc.dram_tensor(name, list(shape), dt, kind="ExternalInput").ap()

    def dscr(name, shape, dt=F32):
        return nc.dram_tensor(name, list(shape), dt)

    I = Ctx()
    I.xT = din("xT", [D, NT])
    I.ctxT = din("ctxT", [D, NC])
    I.cT = din("cT", [128, 8, 2])
    I.w_ada = din("w_ada_p", [DEPTH, 9, 128, 8, 1024])
    I.b_ada = din("b_adaT", [DEPTH, 128, 72])
    I.g3 = din("g3T", [DEPTH, 128, 3, 8])
    I.wgu = [din("w_ffn1_gu", [DEPTH, NFC, 128, 2, 8, 128]), din("w_ffn2_gu", [DEPTH, NFC, 128, 2, 8, 128])]
    I.wd = [din("w_ffn1_dn", [DEPTH, 8, 128, NFC, 128]), din("w_ffn2_dn", [DEPTH, 8, 128, NFC, 128])]
    I.w_in = din("w_in_p", [DEPTH, 128, 8, 2304])
    I.w_out = din("w_out_p", [DEPTH, 64, 16, D])
    I.gqk = din("gqkT", [DEPTH, 64, 4])
    I.rotT = din("rotT", [64, 64])
    I.cos = din("ropecos", [64, NT])
    I.sin = din("ropesin", [64, NT])
    I.tpad = din("na_tpad", [DEPTH, 4, 23, 64, 64])
    I.rmA = din("na_rmA", [8, 44, 128])
    I.bq = din("na_bq", [8, 512])
    I.ident = din("ident", [128, 128])
    I.sel = din("sel", [64, 4])
    I.convw = din("hy_convw", [DEPTH, 64, 3, 3])
    I.convb = din("hy_convb", [DEPTH, 64, 3])
    I.convwc = din("hyc_convw", [DEPTH, 64, 12, 3])
    I.convbc = din("hyc_convb", [DEPTH, 64, 12])
    I.fw1 = din("fw1", [DEPTH, 33, 64])
    I.fw2 = din("fw2", [DEPTH, 64, 64])
    I.fvec = din("fvec", [DEPTH, 64, 3])
    I.w3m = din("w3m", [DEPTH, 64, 2, 64])
    I.w3c = din("w3c", [DEPTH, 64, 512])
    I.ndel_m = din("ndel_m", [64, 1])
    I.ndel_c = din("ndel_c", [64, 4])
    I.skip_m = din("skip_m", [DEPTH, 64, 1])
    I.skip_c = din("skip_c", [DEPTH, 64, 4])
    I.feats = {128: din("featsT_m", [33, 2 * SEQ]), 4: din("featsT_c", [33, 2 * NC])}
    I.tlist = {128: din("tlist_m", [1, 2 * SEQ]), 4: din("tlist_c", [1, 2 * NC])}
    I.ftab = {}
    for A_ in (128, 4):
        I.ftab[A_] = dict(F1v=din("F1v_%d" % A_, [A_, 4 * A_]), TwA=din("TwA_%d" % A_, [128, 2 * A_]),
                          TwB=din("TwB_%d" % A_, [128, 2 * A_]), TwTA=din("TwTA_%d" % A_, [A_, 256]),
                          TwTB=din("TwTB_%d" % A_, [A_, 256]), CA=din("CA_%d" % A_, [A_, A_ // 2]),
                          SA=din("SA_%d" % A_, [A_, A_ // 2]))
    I.R4 = din("fftR4", [4, 128, 512])
    S = Ctx()
    S.zT = [dscr("zT%d" % l, [64, SEQ]) for l in range(DEPTH)]
    S.x0T = [dscr("x0T%d" % l, [64, SEQ]) for l in range(DEPTH)]
    S.kfT = [dscr("kfT%d" % l, [64, 2 * SEQ]) for l in range(DEPTH)]
    S.ycv = [dscr("ycv%d" % l, [64, SEQ]) for l in range(DEPTH)]
    S.yb_in = [dscr("yb_in%d" % l, [64, SEQ], BF16) for l in range(DEPTH)]
    S.yb_all = [dscr("yb_all%d" % l, [256, SEQ], BF16) for l in range(DEPTH)]
    S.zTc1 = [dscr("zTc1_%d" % l, [64, NC]) for l in range(DEPTH)]
    S.x0Tc1 = [dscr("x0Tc1_%d" % l, [64, NC]) for l in range(DEPTH)]
    S.kfTc1 = [dscr("kfTc1_%d" % l, [64, 2 * NC]) for l in range(DEPTH)]
    S.ycvc1 = [dscr("ycvc1_%d" % l, [64, NC]) for l in range(DEPTH)]
    S.ybc_in = [dscr("ybc_in%d" % l, [64, NC], BF16) for l in range(DEPTH)]
    S.ybc_all = [dscr("ybc_all%d" % l, [256, NC], BF16) for l in range(DEPTH)]
    S.zTc = [dscr("zTc%d" % l, [4, 64, NC]) for l in range(DEPTH)]
    S.x0Tc = [dscr("x0Tc%d" % l, [4, 64, NC]) for l in range(DEPTH)]
    S.kfTc = [dscr("kfTc%d" % l, [4, 64, 2 * NC]) for l in range(DEPTH)]
    S.ycvc = [dscr("ycvc%d" % l, [4, 64, NC]) for l in range(DEPTH)]
    S.qT = [dscr("qT_scr%d" % l, [12, 64, NTOT], BF16) for l in range(DEPTH)]
    S.kT_in = [[dscr("kT_in%d_%d" % (l, p), [192, NT], BF16) for p in range(2)] for l in range(DEPTH)]
    S.kT_all = [[dscr("kT_all%d_%d" % (l, p), [4 * 192, NT], BF16) for p in range(2)] for l in range(DEPTH)]
    S.v_in = [[dscr("v_in%d_%d" % (l, p), [1024, 390], BF16) for p in range(2)] for l in range(DEPTH)]
    S.v_all = [[dscr("v_all%d_%d" % (l, p), [4 * 1024, 390], BF16) for p in range(2)] for l in range(DEPTH)]
    S.uT_in = [[dscr("uT_in%d_%d" % (l, p), [128, NT], F32) for p in range(6)] for l in range(DEPTH)]
    S.uT_all = [[dscr("uT_all%d_%d" % (l, p), [4 * 128, NT], F32) for p in range(6)] for l in range(DEPTH)]
    S.kTc = [dscr("kTc%d" % l, [384, NC], BF16) for l in range(DEPTH)]
    S.vc = [dscr("vc%d" % l, [NC, 390], BF16) for l in range(DEPTH)]
    S.uTc = [dscr("uTc%d" % l, [768, NC], F32) for l in range(DEPTH)]
    S.yT = [dscr("yT_scr%d" % l, [16, 64, NTOT], BF16) for l in range(DEPTH)]
    out_hT = nc.dram_tensor("out_hT", [D, NT], F32, kind="ExternalOutput").ap()
    dbg = nc.dram_tensor("dbg", [D, NTOT], F32, kind="ExternalOutput").ap() if DEBUG_OUT else None

    P = Prog(nc)
    cms = []

    def sb(name, shape, dt=F32):
        cm = nc.sbuf_tensor(name, list(shape), dt)
        t = cm.__enter__()
        cms.append(cm)
        return t

    h = sb("h", [128, 8, NTOT])
    ones_bf = sb("ones_bf", [128, 128], BF16)
    mods = sb("mods", [128, 9, 8, 2])
    A32 = sb("A32", [128, 3, 8, 2])
    Gm = sb("Gm", [128, 3, 8, 2])
    ones_f = sb("ones_f", [128, 128])
    ident_bf = sb("ident_bf", [128, 128], BF16)
    ident_f = sb("ident_f", [128, 128])
    rn_m = sb("rn_m", [64, 1])
    rn_c = sb("rn_c", [64, 1])
    sel = sb("sel_sb", [64, 4])
    nbias = sb("nbias", [128, 1])
    rotT = sb("rotT_sb", [64, 64])
    ropec = sb("ropec", [64, NT])
    ropes = sb("ropes", [64, NT])

    def phase(fn):
        local = []
        C.uid = getattr(C, "uid", 0) + 1
        u = "_%d" % C.uid

        def lsb(name, shape, dt=F32):
            cm = nc.sbuf_tensor(name + u, list(shape), dt)
            t = cm.__enter__()
            local.append(cm)
            return t

        def lps(name, shape=(128, 512), dt=F32):
            cm = nc.psum_tensor(name + u, list(shape), dt)
            t = cm.__enter__()
            local.append(cm)
            return t

        with nc.Block() as block:
            fn(lsb, lps)
            P.emit(block)
        for cm in reversed(local):
            cm.__exit__(None, None, None)

    def ph_init(lsb, lps):
        P.add("vector", lambda e: e.memset(ones_bf[:, :], 1.0), (), [("ones",)])
        P.add("vector", lambda e: e.memset(ones_f[:, :], 1.0), (), [("onesf",)])
        P.add("vector", lambda e: e.memset(nbias[:, :], -math.pi), (), ["nbias"])
        dma(P, "sync", rotT[:, :], I.rotT[:, :], (), ["rotT"])
        dma(P, "gpsimd", ident_bf[:, :], I.ident[:, :], (), ["ident"])
        dma(P, "sync", ident_f[:, :], I.ident[:, :], (), ["identf"])
        dma(P, "sync", sel[:, :], I.sel[:, :], (), ["sel"])
        dma(P, "sync", ropec[:, :], I.cos[:, :], (), ["ropec"])
        dma(P, "sync", ropes[:, :], I.sin[:, :], (), ["ropes"])
        for k in range(8):
            dma(P, "sync", h[:, k, 0:NT], I.xT[128 * k:128 * (k + 1), :], (), [("h", k, i) for i in range(4)])
            dma(P, "sync", h[:, k, NT:NTOT], I.ctxT[128 * k:128 * (k + 1), :], (), [("h", k, 4)])

    phase(ph_init)

    def ph_adaln(l):
        def f(lsb, lps):
            cts = lsb("cts", [128, 8, 2])
            cbf = lsb("cbf", [128, 8, 2], BF16)
            wp = [lsb("wp0", [128, 8, 1024], BF16), lsb("wp1", [128, 8, 1024], BF16)]
            bias = lsb("bias", [128, 72])
            g32 = lsb("g32", [128, 3, 8])
            ps = lps("ps_ada", [128, 9, 8, 2])
            dma(P, "sync", cts[:, :, :], I.cT[:, :, :], (), ["cts"])
            dma(P, "sync", bias[:, :], I.b_ada[l], (), ["bias"])
            dma(P, "sync", g32[:, :, :], I.g3[l], (), ["g32"])
            act(P, cbf[:, :, :], cts[:, :, :], AF.Silu, ["cts"], ["cbf"])
            for m in range(9):
                w = wp[m % 2]
                dma(P, "gpsimd", w[:, :, :], I.w_ada[l, m], (), [("wp", m % 2)])
                for dc in range(8):
                    for k in range(8):
                        mm(P, ps[:, m, dc, :], w[:, k, 128 * dc:128 * (dc + 1)], cbf[:, k, :], k == 0, k == 7,
                           [("wp", m % 2), "cbf"], [("psada", m)])
            bv = bias[:, :].rearrange("p (m k) -> p m k", k=8)
            for c in range(2):
                tt(P, "vector", mods[:, :, :, c], ps[:, :, :, c], bv, ALU.add,
                   [("psada", m) for m in range(9)] + ["bias"], [("mods", c)])
            ts(P, "vector", g32[:, :, :], g32[:, :, :], 32.0, None, ALU.mult, None, ["g32"], ["g32"])
            for j in range(3):
                for c in range(2):
                    stt(P, "vector", A32[:, j, :, c], mods[:, 3 * j + 1, :, c], 1.0, g32[:, j, :], ALU.add, ALU.mult,
                        [("mods", c), "g32"], [("A32", j, c)])
                    ts(P, "vector", Gm[:, j, :, c], mods[:, 3 * j + 2, :, c], (1.0 if j == 1 else 0.5), None, ALU.mult,
                       None, [("mods", c)], [("Gm", j, c)])
        return f

    def norm_tile(lsb_bufs, j, ti, xm_view, xm_key):
        off, n, c = TILES[ti]
        sq, rstd, tmp, ps_n = lsb_bufs
        for k in range(8):
            act(P, sq[:, k, 0:n], h[:, k, off:off + n], AF.Square, [("h", k, ti)], [("sq", k)])
        for k in range(8):
            mm(P, ps_n[:, 0:n], ones_bf[:, :], sq[:, k, 0:n], k == 0, k == 7, [("sq", k), ("ones",)], ["ps_n"])
        act(P, rstd[:, 0:n], ps_n[:, 0:n], AF.Sqrt, ["ps_n"], ["rstd"], bias=float(D * EPS))
        P.add("vector", lambda e: e.reciprocal(rstd[:, 0:n], rstd[:, 0:n]), ["rstd"], ["rstd"])
        for k in range(8):
            tt(P, "vector", tmp[:, k % 2, 0:n], h[:, k, off:off + n], rstd[:, 0:n], ALU.mult,
               [("h", k, ti), "rstd"], [("ntmp", k % 2)])
            act(P, xm_view(k), tmp[:, k % 2, 0:n], AF.Identity, [("ntmp", k % 2), ("A32", j, c), ("mods", c)],
                [(xm_key, k, ti)], bias=mods[:, 3 * j, k, c:c + 1], scale=A32[:, j, k, c:c + 1])

    def ph_ffn(l, j, tiles):
        which = 0 if j == 0 else 1

        def f(lsb, lps):
            ntok = sum(TILES[t][1] for t in tiles)
            base = TILES[tiles[0]][0]
            xm = lsb("xm", [128, 8, ntok], BF16)
            hid = lsb("hid", [128, NFC, ntok], BF16)
            sq = lsb("sq", [128, 8, 512], BF16)
            rstd = lsb("rstd", [128, 512])
            tmp = lsb("ntmp", [128, 2, 512])
            wgu = [lsb("wgu0", [128, 2, 8, 128], BF16), lsb("wgu1", [128, 2, 8, 128], BF16)]
            wdb = [lsb("wd0", [128, NFC, 128], BF16), lsb("wd1", [128, NFC, 128], BF16)]
            sg = [lsb("sg0", [128, 512], BF16), lsb("sg1", [128, 512], BF16)]
            ps_n = lps("ps_n")
            ps_g = [lps("ps_g0"), lps("ps_g1")]
            ps_u = [lps("ps_u0"), lps("ps_u1")]
            ps_d = [lps("ps_d0"), lps("ps_d1")]
            for ti in tiles:
                off, n, c = TILES[ti]
                norm_tile((sq, rstd, tmp, ps_n), j, ti,
                          lambda k, off=off, n=n: xm[:, k, off - base:off - base + n], "xm")
            it = 0
            for fc in range(NFC):
                w = wgu[fc % 2]
                dma(P, "gpsimd", w[:, :, :, :], I.wgu[which][l, fc], (), [("wgu", fc % 2, 0), ("wgu", fc % 2, 1)])
                for ti in tiles:
                    off, n, c = TILES[ti]
                    o = off - base
                    b = it % 2
                    it += 1
                    for k in range(8):
                        mm(P, ps_g[b][:, 0:n], w[:, 0, k, :], xm[:, k, o:o + n], k == 0, k == 7,
                           [("wgu", fc % 2, 0), ("xm", k, ti)], [("ps_g", b)])
                    for k in range(8):
                        mm(P, ps_u[b][:, 0:n], w[:, 1, k, :], xm[:, k, o:o + n], k == 0, k == 7,
                           [("wgu", fc % 2, 1), ("xm", k, ti)], [("ps_u", b)])
                    act(P, sg[b][:, 0:n], ps_g[b][:, 0:n], AF.Silu, [("ps_g", b)], [("sg", b)])
                    tt(P, "vector", hid[:, fc, o:o + n], sg[b][:, 0:n], ps_u[b][:, 0:n], ALU.mult,
                       [("sg", b), ("ps_u", b)], [("hid", fc, ti)])
            it = 0
            for dc in range(8):
                w = wdb[dc % 2]
                dma(P, "gpsimd", w[:, :, :], I.wd[which][l, dc], (), [("wd", dc % 2)])
                for ti in tiles:
                    off, n, c = TILES[ti]
                    o = off - base
                    b = it % 2
                    it += 1
                    for fc in range(NFC):
                        mm(P, ps_d[b][:, 0:n], w[:, fc, :], hid[:, fc, o:o + n], fc == 0, fc == NFC - 1,
                           [("wd", dc % 2), ("hid", fc, ti)], [("ps_d", b)])
                    stt(P, "vector", h[:, dc, off:off + n], ps_d[b][:, 0:n], Gm[:, j, dc, c:c + 1],
                        h[:, dc, off:off + n], ALU.mult, ALU.add, [("ps_d", b), ("Gm", j, c), ("h", dc, ti)],
                        [("h", dc, ti)])
        return f


    def ph_proj(l):
        last = l == DEPTH - 1

        def f(lsb, lps):
            Win = lsb("Win", [128, 8, 2304], BF16)
            xm = lsb("xm2", [128, 8, 512], BF16)
            sq = lsb("sq", [128, 8, 512], BF16)
            rstd = lsb("rstd", [128, 512])
            tmp = lsb("ntmp", [128, 2, 512])
            g8 = lsb("g8", [64, 4])
            sqh = [lsb("sqh0", [64, 512], BF16), lsb("sqh1", [64, 512], BF16)]
            rs = [lsb("rs0", [64, 512]), lsb("rs1", [64, 512])]
            qn = [lsb("qn0", [64, 512]), lsb("qn1", [64, 512])]
            t1 = [lsb("t10", [64, 512]), lsb("t11", [64, 512])]
            t2 = [lsb("t20", [64, 512]), lsb("t21", [64, 512])]
            ob = [lsb("ob%d" % i, [64, 512], BF16) for i in range(4)]
            hb = [lsb("hb0", [128, 512]), lsb("hb1", [128, 512])]
            vt = [lsb("vt0", [128, 6, 65], BF16), lsb("vt1", [128, 6, 65], BF16)]
            ps_n = lps("ps_n")
            ps_q = [lps("ps_q0"), lps("ps_q1")]
            ps_s = lps("ps_s")
            ps_r = lps("ps_r")
            ps_h = lps("ps_h")
            ps_v = lps("ps_v")
            for half in range(2):
                dma(P, "gpsimd", Win[:, 4 * half:4 * half + 4, :], I.w_in[l, :, 4 * half:4 * half + 4, :], (),
                    [("Win", half)])
            dma(P, "sync", g8[:, :], I.gqk[l], (), ["g8"])
            ts(P, "vector", g8[:, :], g8[:, :], 8.0, None, ALU.mult, None, ["g8"], ["g8"])
            for b in range(2):
                P.add("gpsimd", lambda e, b=b: e.memset(vt[b][:, :, 64:65], 1.0), (), [("vt1", b)])
            WinK = [("Win", 0), ("Win", 1)]
            it = 0
            for ti in range(5):
                off, n, c = TILES[ti]
                norm_tile((sq, rstd, tmp, ps_n), 1, ti, lambda k, n=n: xm[:, k, 0:n], "xm2")
                xk = [("xm2", k, ti) for k in range(8)]
                for g in range(18):
                    b = it % 2
                    it += 1
                    kind = 0 if g < 8 else (2 if g < 12 else (1 if g < 14 else 3))
                    rope = (c == 0) and kind in (0, 1)
                    for k in range(8):
                        mm(P, ps_q[b][0:64, 0:n], Win[:, k, 64 * g:64 * g + 64], xm[:, k, 0:n], k == 0, k == 7,
                           WinK + [xk[k]], [("ps_q", b)])
                    act(P, sqh[b][:, 0:n], ps_q[b][0:64, 0:n], AF.Square, [("ps_q", b)], [("sqh", b)])
                    mm(P, ps_s[0:64, 0:n], ones_bf[0:64, 0:64], sqh[b][:, 0:n], True, True,
                       [("sqh", b), ("ones",)], ["ps_s"])
                    act(P, rs[b][:, 0:n], ps_s[0:64, 0:n], AF.Sqrt, ["ps_s"], [("rs", b)], bias=float(64 * EPS))
                    P.add("vector", lambda e, b=b, n=n: e.reciprocal(rs[b][:, 0:n], rs[b][:, 0:n]),
                          [("rs", b)], [("rs", b)])
                    o = ob[it % 4]
                    okey = ("ob", it % 4)
                    if rope:
                        stt(P, "vector", qn[b][:, 0:n], ps_q[b][0:64, 0:n], g8[:, kind:kind + 1], rs[b][:, 0:n],
                            ALU.mult, ALU.mult, [("ps_q", b), ("rs", b), "g8"], [("qn", b)])
                        mm(P, ps_r[0:64, 0:n], rotT[:, :], qn[b][:, 0:n], True, True, [("qn", b), "rotT"], ["ps_r"])
                        tt(P, "gpsimd", t1[b][:, 0:n], qn[b][:, 0:n], ropec[:, off:off + n], ALU.mult,
                           [("qn", b), "ropec"], [("t1", b)])
                        tt(P, "vector", t2[b][:, 0:n], ps_r[0:64, 0:n], ropes[:, off:off + n], ALU.mult,
                           ["ps_r", "ropes"], [("t2", b)])
                        tt(P, "gpsimd", o[:, 0:n], t1[b][:, 0:n], t2[b][:, 0:n], ALU.add,
                           [("t1", b), ("t2", b)], [okey])
                    else:
                        stt(P, "vector", o[:, 0:n], ps_q[b][0:64, 0:n], g8[:, kind:kind + 1], rs[b][:, 0:n],
                            ALU.mult, ALU.mult, [("ps_q", b), ("rs", b), "g8"], [okey])
                    if g < 12:
                        dma(P, "sync", S.qT[l][g, :, off:off + n], o[:, 0:n], [okey], [("qT", g, ti)])
                    elif c == 0:
                        kg = g - 12
                        dma(P, "sync", S.kT_in[l][kg // 3][64 * (kg % 3):64 * (kg % 3) + 64, off:off + n], o[:, 0:n],
                            [okey], [("kT_in", kg // 3)])
                    else:
                        dma(P, "sync", S.kTc[l][64 * (g - 12):64 * (g - 11), :], o[:, 0:n], [okey], ["kTc"])
                for pr in range(6):
                    b = pr % 2
                    for k in range(8):
                        mm(P, ps_h[:, 0:n], Win[:, k, 1152 + 128 * pr:1152 + 128 * (pr + 1)], xm[:, k, 0:n],
                           k == 0, k == 7, WinK + [xk[k]], ["ps_h"])
                    cp(P, "scalar", hb[b][:, 0:n], ps_h[:, 0:n], ["ps_h"], [("hb", b)])
                    if c == 0:
                        dma(P, "sync", S.uT_in[l][pr][:, off:off + n], hb[b][:, 0:n], [("hb", b)],
                            [("uT_in", pr)])
                    else:
                        dma(P, "sync", S.uTc[l][128 * pr:128 * (pr + 1), :], hb[b][:, 0:n], [("hb", b)], ["uTc"])
                for s_ in range(n // 128):
                    b = s_ % 2
                    for k in range(8):
                        mm(P, ps_v[:, 0:384], xm[:, k, 128 * s_:128 * (s_ + 1)], Win[:, k, 1920:2304], k == 0, k == 7,
                           WinK + [xk[k]], ["ps_v"])
                    cp(P, "scalar", vt[b][:, :, 0:64], ps_v[:, 0:384].rearrange("p (g f) -> p g f", f=64), ["ps_v"],
                       [("vt", b)])
                    if c == 0:
                        tk = off + 128 * s_
                        dma(P, "sync", S.v_in[l][tk // 1024][tk % 1024:tk % 1024 + 128, :],
                            vt[b][:, :, :].rearrange("p g f -> p (g f)"), [("vt", b), ("vt1", b)],
                            [("v_in", tk // 1024)])
                    else:
                        dma(P, "sync", S.vc[l][128 * s_:128 * (s_ + 1), :],
                            vt[b][:, :, :].rearrange("p g f -> p (g f)"), [("vt", b), ("vt1", b)], ["vc"])
            ags = [(S.kT_in[l][p], S.kT_all[l][p], ("kT_in", p)) for p in range(2)]
            ags += [(S.v_in[l][p], S.v_all[l][p], ("v_in", p)) for p in range(2)]
            ags += [(S.uT_in[l][p], S.uT_all[l][p], ("uT_in", p)) for p in range(6)]
            for (src, dst, key) in ags:
                P.add("gpsimd", lambda e, src=src, dst=dst: e.collective_compute(
                    "AllGather", ALU.bypass, replica_groups=G4, ins=[src.ap().opt()], outs=[dst.ap().opt()]),
                    [key], [("all",) + key], kind="cc")
        return f

    def attn_head(B, qsrc, ydst, n, chunks, extra=None):
        i = B.cnt
        B.cnt += 1
        qb = B.qb[i % 2]
        ps_o = B.ps_o[i % 2]
        dma(P, "sync", qb[:, 0:n], qsrc, (), [("qb", i % 2)])
        nch = len(chunks)
        for ci, (kT_ap, v_ap, rds, xf) in enumerate(chunks):
            j = B.it % 2
            B.it += 1
            mm(P, B.ps_s[j][:, 0:n], kT_ap, qb[:, 0:n], True, xf is None, rds + [("qb", i % 2)], [("ps_s", j)])
            if xf is not None:
                xf(B.ps_s[j][:, 0:n], ("ps_s", j))
            act(P, B.pT[j][:, 0:n], B.ps_s[j][:, 0:n], AF.Exp, [("ps_s", j)], [("pT", j)], scale=0.125)
            mm(P, ps_o[0:65, 0:n], v_ap, B.pT[j][:, 0:n], ci == 0, ci == nch - 1, rds + [("pT", j)],
               [("ps_o", i % 2)])
        osb = B.osb[i % 2]
        cp(P, "vector", osb[0:65, 0:n], ps_o[0:65, 0:n], [("ps_o", i % 2)], [("osb", i % 2)])
        P.add("vector", lambda e: e.reciprocal(osb[64:65, 0:n], osb[64:65, 0:n]), [("osb", i % 2)],
              [("osb", i % 2)])
        mm(P, B.ps_b[0:64, 0:n], ones_f[64:65, 0:64], osb[64:65, 0:n], True, True, [("osb", i % 2), ("onesf",)],
           ["ps_b"])
        yb = B.yb[i % 2]
        tt(P, "vector", yb[:, 0:n], osb[0:64, 0:n], B.ps_b[0:64, 0:n], ALU.mult, [("osb", i % 2), "ps_b"],
           [("yb", i % 2)])
        dma(P, "sync", ydst, yb[:, 0:n], [("yb", i % 2)], ())

    def attn_bufs(lsb, lps):
        B = Ctx()
        B.cnt = 0
        B.it = 0
        B.qb = [lsb("qb0", [64, 512], BF16), lsb("qb1", [64, 512], BF16)]
        B.pT = [lsb("pT0", [128, 512], BF16), lsb("pT1", [128, 512], BF16)]
        B.osb = [lsb("osb0", [65, 512]), lsb("osb1", [65, 512])]
        B.yb = [lsb("yb0", [64, 512], BF16), lsb("yb1", [64, 512], BF16)]
        B.ps_s = [lps("ps_s0"), lps("ps_s1")]
        B.ps_o = [lps("ps_o0"), lps("ps_o1")]
        B.ps_b = lps("ps_b")
        return B

    def ph_attn(l):
        last = l == DEPTH - 1

        def f(lsb, lps):
            KT = lsb("KT", [64, 2, SEQ + NC], BF16)
            V = lsb("V", [128, 66, 130], BF16)
            B = attn_bufs(lsb, lps)
            for g in range(2):
                for j in range(4):
                    dma(P, "sync", KT[:, g, NT * j:NT * (j + 1)], S.kT_all[l][0][192 * j + 64 * g:192 * j + 64 * g + 64, :],
                        (), [("KT", g)])
                dma(P, "sync", KT[:, g, SEQ:SEQ + NC], S.kTc[l][64 * g:64 * g + 64, :], (), [("KT", g)])
            for j in range(4):
                for p in range(2):
                    dma(P, "sync", V[:, 16 * j + 8 * p:16 * j + 8 * p + 8, :],
                        S.v_all[l][p][1024 * j:1024 * (j + 1), 0:130].rearrange("(c p) f -> p c f", p=128), (), ["V"])
            dma(P, "sync", V[:, 64:66, :], S.vc[l][:, 0:130].rearrange("(c p) f -> p c f", p=128), (), ["V"])
            for ti in range(5):
                off, n, c = TILES[ti]
                if c == 1 and last:
                    continue
                for hh in range(8):
                    g = hh // 4
                    cl = range(66) if c == 0 else (64, 65)
                    chunks = [(KT[:, g, 128 * ck:128 * (ck + 1)], V[:, ck, 65 * g:65 * g + 65], [("KT", g), "V"], None)
                              for ck in cl]
                    attn_head(B, S.qT[l][hh, :, off:off + n], S.yT[l][hh, :, off:off + n], n, chunks)
        return f


    def ph_na(l):
        last = l == DEPTH - 1

        def f(lsb, lps):
            KTn = lsb("KTn", [64, 4, 4352], BF16)
            Vn = lsb("Vn", [128, 34, 260], BF16)
            stage = lsb("bstage", [128, 8, 512])
            bias8 = [lsb("bias8_0", [128, 8, 512], BF16), lsb("bias8_1", [128, 8, 512], BF16)]
            rmA = lsb("rmA", [8, 44, 128], BF16)
            bq = lsb("bq", [8, 512], BF16)
            B = attn_bufs(lsb, lps)
            dma(P, "gpsimd", rmA[:, :, :], I.rmA[:, :, :], (), ["rmA"])
            dma(P, "gpsimd", bq[:, :], I.bq[:, :], (), ["bq"])
            for hd in range(4):
                kg = 2 + hd
                pc, ro = kg // 3, 64 * (kg % 3)
                dma(P, "sync", KTn[:, hd, 0:NT], S.kT_in[l][pc][ro:ro + 64, :], (), [("KTn", hd)])
                for j in range(4):
                    dma(P, "sync", KTn[:, hd, 2048 + 256 * j:2304 + 256 * j],
                        S.kT_all[l][pc][192 * j + ro:192 * j + ro + 64, 1792:2048], (), [("KTn", hd)])
                    dma(P, "sync", KTn[:, hd, 3072 + 256 * j:3328 + 256 * j],
                        S.kT_all[l][pc][192 * j + ro:192 * j + ro + 64, 0:256], (), [("KTn", hd)])
                dma(P, "sync", KTn[:, hd, 4096:4352], S.kTc[l][64 * kg:64 * kg + 64, :], (), [("KTn", hd)])
            for p in range(2):
                dma(P, "sync", Vn[:, 8 * p:8 * p + 8, :],
                    S.v_in[l][p][:, 130:390].rearrange("(c p) f -> p c f", p=128), (), ["Vn"])
            for j in range(4):
                dma(P, "sync", Vn[:, 16 + 2 * j:18 + 2 * j, :],
                    S.v_all[l][1][1024 * j + 768:1024 * j + 1024, 130:390].rearrange("(c p) f -> p c f", p=128), (),
                    ["Vn"])
                dma(P, "sync", Vn[:, 24 + 2 * j:26 + 2 * j, :],
                    S.v_all[l][0][1024 * j:1024 * j + 256, 130:390].rearrange("(c p) f -> p c f", p=128), (), ["Vn"])
            dma(P, "sync", Vn[:, 32:34, :], S.vc[l][:, 130:390].rearrange("(c p) f -> p c f", p=128), (), ["Vn"])
            for hd in range(4):
                b8 = bias8[hd % 2]
                for c in range(8):
                    for kl in range(2):
                        dp0 = 15 - 2 * c - kl
                        dma(P, "sync", stage[64 * kl:64 * kl + 64, c, :].rearrange("p (r q) -> p r q", q=64),
                            I.tpad[l, hd, dp0:dp0 + 8, :, :].rearrange("r k q -> k r q"), (), ["stage"])
                ts(P, "gpsimd", b8[:, :, :], stage[:, :, :], 8.0, None, ALU.mult, None, ["stage"], [("b8", hd % 2)])
                it = 0
                for ti in range(5):
                    off, n, c_ = TILES[ti]
                    if c_ == 1 and last:
                        continue
                    chunks = []
                    ctxch = [(KTn[:, hd, 4096 + 128 * x:4224 + 128 * x], Vn[:, 32 + x, 65 * hd:65 * hd + 65],
                              [("KTn", hd), "Vn"], None) for x in range(2)]
                    if c_ == 0:
                        for (kind, j, c) in na_iters(ti):
                            if kind == "own":
                                ko = 512 * ti - 256 + 128 * c
                                kT_ap = KTn[:, hd, ko:ko + 128]
                                v_ap = Vn[:, ko // 128, 65 * hd:65 * hd + 65]
                            elif kind == "top":
                                ko = 2048 + 256 * j + 128 * c
                                kT_ap = KTn[:, hd, ko:ko + 128]
                                v_ap = Vn[:, 16 + 2 * j + c, 65 * hd:65 * hd + 65]
                            else:
                                ko = 3072 + 256 * j + 128 * (c - 6)
                                kT_ap = KTn[:, hd, ko:ko + 128]
                                v_ap = Vn[:, 24 + 2 * j + (c - 6), 65 * hd:65 * hd + 65]

                            def xf(ps_ap, pkey, c=c, it=it, b8=b8, hd=hd):
                                mm(P, ps_ap, ident_bf[:, :], b8[:, c, :], False, False, ["ident", ("b8", hd % 2)], [pkey])
                                mm(P, ps_ap, rmA[:, it, :], bq[:, :], False, True, ["rmA", "bq"], [pkey])
                            chunks.append((kT_ap, v_ap, [("KTn", hd), "Vn"], xf))
                            it += 1
                    chunks += ctxch
                    attn_head(B, S.qT[l][8 + hd, :, off:off + n], S.yT[l][12 + hd, :, off:off + n], n, chunks)
        return f


    def conv3(eng, out_ap, acc, w, bcol, n, rkeys, wkey):
        eng = "vector"
        ts(P, eng, out_ap, acc[:, 0:n], w[:, 0:1], bcol, ALU.mult, ALU.add, rkeys, [wkey])
        stt(P, eng, out_ap, acc[:, 1:n + 1], w[:, 1:2], out_ap, ALU.mult, ALU.add, rkeys + [wkey], [wkey])
        stt(P, eng, out_ap, acc[:, 2:n + 2], w[:, 2:3], out_ap, ALU.mult, ALU.add, rkeys + [wkey], [wkey])

    def ph_hypre(l):
        def f(lsb, lps):
            cw = lsb("cw", [64, 3, 3])
            cb = lsb("cb", [64, 3])
            cand = [lsb("cand0", [64, NT + 2]), lsb("cand1", [64, NT + 2])]
            acc = [lsb("acc%d" % i, [64, NT + 2]) for i in range(3)]
            uc = [lsb("uc%d" % i, [64, NT]) for i in range(3)]
            dma(P, "sync", cw[:, :, :], I.convw[l], (), ["cw"])
            dma(P, "sync", cb[:, :], I.convb[l], (), ["cb"])
            ci = 0
            for jb in range(4):
                for s_ in range(3):
                    for g in range(4):
                        cd = cand[ci % 2]
                        ck = ("cand", ci % 2)
                        ci += 1
                        src = S.uT_all[l][2 * s_ + g // 2]
                        r0 = 64 * (g % 2)
                        dma(P, "sync", cd[:, 1:NT + 1], src[128 * jb + r0:128 * jb + r0 + 64, :], (), [ck])
                        if jb > 0:
                            dma(P, "sync", cd[:, 0:1], src[128 * (jb - 1) + r0:128 * (jb - 1) + r0 + 64, NT - 1:NT], (),
                                [ck], slow=True)
                        else:
                            P.add("gpsimd", lambda e, cd=cd: e.memset(cd[:, 0:1], 0.0), (), [ck])
                        if jb < 3:
                            dma(P, "sync", cd[:, NT + 1:NT + 2], src[128 * (jb + 1) + r0:128 * (jb + 1) + r0 + 64, 0:1], (),
                                [ck], slow=True)
                        else:
                            P.add("gpsimd", lambda e, cd=cd: e.memset(cd[:, NT + 1:NT + 2], 0.0), (), [ck])
                        eng = "vector" if g % 2 == 0 else "gpsimd"
                        if g == 0:
                            ts(P, "vector", acc[s_][:, :], cd[:, :], sel[:, 0:1], None, ALU.mult, None, [ck, "sel"],
                               [("acc", s_)])
                        else:
                            stt(P, "vector", acc[s_][:, :], cd[:, :], sel[:, g:g + 1], acc[s_][:, :], ALU.mult, ALU.add,
                                [ck, "sel", ("acc", s_)], [("acc", s_)])
                    conv3("gpsimd" if s_ == 1 else "vector", uc[s_][:, :], acc[s_], cw[:, s_, :], cb[:, s_:s_ + 1], NT,
                          [("acc", s_), "cw", "cb"], ("uc", s_))
                tt(P, "gpsimd", uc[2][:, :], uc[2][:, :], uc[1][:, :], ALU.mult, [("uc", 1), ("uc", 2)], [("uc", 2)])
                dma(P, "sync", S.zT[l][:, NT * jb:NT * (jb + 1)], uc[2][:, :], [("uc", 2)], ["zT"])
                dma(P, "sync", S.x0T[l][:, NT * jb:NT * (jb + 1)], uc[0][:, :], [("uc", 0)], ["x0T"])
        return f

    def ph_hyprec(l):
        def f(lsb, lps):
            cw = lsb("cwc", [64, 12, 3])
            cb = lsb("cbc", [64, 12])
            u = lsb("uc_in", [64, 12, NC + 2])
            uc = lsb("uc_out", [64, 12, NC])
            dma(P, "sync", cw[:, :, :], I.convwc[l], (), ["cw"])
            dma(P, "sync", cb[:, :], I.convbc[l], (), ["cb"])
            P.add("gpsimd", lambda e: e.memset(u[:, :, 0:1], 0.0), (), ["u0"])
            P.add("gpsimd", lambda e: e.memset(u[:, :, NC + 1:NC + 2], 0.0), (), ["u1"])
            dma(P, "sync", u[:, :, 1:NC + 1], S.uTc[l].ap().rearrange("(q c) t -> c q t", c=64), (), ["u"])
            for q in range(12):
                conv3("vector" if q % 2 == 0 else "gpsimd", uc[:, q, :], u[:, q, :], cw[:, q, :], cb[:, q:q + 1], NC,
                      ["u", "u0", "u1", "cw", "cb"], ("ucc", q))
            for g in range(4):
                tt(P, "vector", uc[:, 8 + g, :], uc[:, 8 + g, :], uc[:, 4 + g, :], ALU.mult, [("ucc", 8 + g), ("ucc", 4 + g)],
                   [("ucc", 8 + g)])
                dma(P, "sync", S.zTc[l][g], uc[:, 8 + g, :], [("ucc", 8 + g)], ["zTc"])
                dma(P, "sync", S.x0Tc[l][g], uc[:, g, :], [("ucc", g)], ["x0Tc"])
        return f

    def ph_filt(l, A, groups, ndel, rn, dst):
        n2 = 128 * A
        n = n2 // 2
        TW = min(512, n)
        ntile = n2 // TW

        def f(lsb, lps):
            w1 = lsb("fw1", [33, 64])
            w2 = lsb("fw2", [64, 64])
            fv = lsb("fvec", [64, 3])
            w3 = lsb("fw3", [64, len(groups), 2, 64])
            nd = lsb("ndel", [64, len(groups)])
            ft = [lsb("ft0", [33, TW]), lsb("ft1", [33, TW])]
            tb = [lsb("tb0", [64, TW]), lsb("tb1", [64, TW])]
            pre = [lsb("pre0", [64, TW]), lsb("pre1", [64, TW])]
            h1 = [lsb("h10", [64, TW]), lsb("h11", [64, TW])]
            h2 = [lsb("h20", [64, TW]), lsb("h21", [64, TW])]
            dec = [lsb("dec0", [64, TW]), lsb("dec1", [64, TW])]
            kk = [lsb("kk0", [64, TW]), lsb("kk1", [64, TW])]
            ka = [lsb("ka0", [64, TW]), lsb("ka1", [64, TW])]
            part = lsb("part", [64, len(groups), ntile])
            assert TW <= 512
            ps1 = [lps("psf1_0"), lps("psf1_1")]
            ps2 = [lps("psf2_0"), lps("psf2_1")]
            ps3 = [lps("psf3_0"), lps("psf3_1")]
            pri = [lsb("pri0", [64, TW], mybir.dt.int32), lsb("pri1", [64, TW], mybir.dt.int32)]
            prf = [lsb("prf0", [64, TW]), lsb("prf1", [64, TW])]

            def sin_reduced(ps_ap, pkey, bcol, out_ap, okey, b):
                ts(P, "vector", pre[b][:, :], ps_ap, fv[:, bcol:bcol + 1], f2p[:, 0:1], ALU.add, ALU.mult,
                   [pkey, "f2p", "fv"], [("pre", b)])
                cp(P, "vector", pri[b][:, :], pre[b][:, :], [("pre", b)], [("pri", b)])
                cp(P, "gpsimd", prf[b][:, :], pri[b][:, :], [("pri", b)], [("prf", b)])
                tt(P, "vector", pre[b][:, :], pre[b][:, :], prf[b][:, :], ALU.subtract, [("pre", b), ("prf", b)],
                   [("pre", b)])
                act(P, out_ap, pre[b][:, :], AF.Sin, [("pre", b)], [okey], scale=2.0 * math.pi * (1.0 - 3e-7))

            f2p = lsb("f2p", [64, 1])
            dma(P, "sync", w1[:, :], I.fw1[l], (), ["w1"])
            dma(P, "sync", w2[:, :], I.fw2[l], (), ["w2"])
            dma(P, "sync", fv[:, :], I.fvec[l], (), ["fv"])
            ts(P, "vector", f2p[:, :], fv[:, 2:3], 1.0 / (2.0 * math.pi), None, ALU.mult, None, ["fv"], ["f2p"])
            dma(P, "sync", nd[:, :], ndel, (), ["nd"])
            for gi, (wf, wb) in enumerate(groups):
                dma(P, "sync", w3[:, gi, 0, :], wf, (), ["w3"])
                dma(P, "sync", w3[:, gi, 1, :], wb, (), ["w3"])
            it = 0
            for t in range(ntile):
                b = t % 2
                sl = slice(TW * t, TW * (t + 1))
                dma(P, "sync", ft[b][:, :], I.feats[A][:, sl], (), [("ft", b)])
                dma(P, "sync", tb[b][:, :], I.tlist[A][0:1, sl].partition_broadcast(64), (), [("tb", b)])
                mm(P, ps1[b][0:64, 0:TW], w1[:, :], ft[b][:, :], True, True, ["w1", ("ft", b)], [("ps1", b)])
                sin_reduced(ps1[b][0:64, 0:TW], ("ps1", b), 0, h1[b][:, :], ("h1", b), b)
                mm(P, ps2[b][0:64, 0:TW], w2[:, :], h1[b][:, :], True, True, ["w2", ("h1", b)], [("ps2", b)])
                sin_reduced(ps2[b][0:64, 0:TW], ("ps2", b), 1, h2[b][:, :], ("h2", b), b)
                dirn = 0 if TW * t < n else 1
                for gi in range(len(groups)):
                    c = it % 2
                    it += 1
                    mm(P, ps3[c][0:64, 0:TW], w3[:, gi, dirn, :], h2[b][:, :], True, True, ["w3", ("h2", b)],
                       [("ps3", c)])
                    act(P, dec[c][:, :], tb[b][:, :], AF.Exp, [("tb", b), "nd"], [("dec", c)], scale=nd[:, gi:gi + 1])
                    tt(P, "vector", kk[c][:, :], ps3[c][0:64, 0:TW], dec[c][:, :], ALU.mult, [("ps3", c), ("dec", c)],
                       [("kk", c)])
                    act(P, ka[c][:, :], kk[c][:, :], AF.Abs, [("kk", c)], [("ka", c)])
                    P.add("vector", lambda e, c=c, gi=gi, t=t: e.reduce_sum(part[:, gi, t:t + 1], ka[c][:, :], AX.X),
                          [("ka", c)], [("part", gi)])
                    dma(P, "sync", dst(gi)[:, sl], kk[c][:, :], [("kk", c)], ["kf"])
            for gi in range(len(groups)):
                P.add("vector", lambda e, gi=gi: e.reduce_sum(rn[:, gi:gi + 1], part[:, gi, :], AX.X), [("part", gi)],
                      [("rn", A)])
            P.add("vector", lambda e: e.reciprocal(rn[:, :], rn[:, :]), [("rn", A)], [("rn", A)])
        return f

    def ph_fft(l, A, jobs):
        Ah = A // 2
        NCH = max(j[3] for j in jobs)

        def f(lsb, lps):
            T = {}
            for nm, shp in (("F1v", [A, 4 * A]), ("TwA", [128, 2 * A]), ("TwB", [128, 2 * A]), ("TwTA", [A, 256]),
                            ("TwTB", [A, 256]), ("CA", [A, Ah]), ("SA", [A, Ah])):
                T[nm] = lsb("T" + nm, shp)
                dma(P, "sync", T[nm][:, :], I.ftab[A][nm][:, :], (), ["tab"])
            R4 = lsb("R4", [128, 4, 512])
            dma(P, "sync", R4[:, :, :], I.R4.rearrange("r p n -> p r n"), (), ["tab"])
            kmat = lsb("kmat", [A, NCH, 128])
            Kf = lsb("Kf", [A, NCH, 2, 128])
            Z = lsb("Zb", [Ah, NCH, 128])
            Yo = lsb("Yo", [Ah, NCH, 128])
            t1 = [lsb("ft1_%d" % i, [128, 256]) for i in range(2)]
            t2 = [lsb("ft2_%d" % i, [128, 256]) for i in range(2)]
            Yp = [lsb("Yp%d" % i, [128, 2 * A]) for i in range(2)]
            Pm = [lsb("Pm%d" % i, [A, 256]) for i in range(2)]
            PT = [lsb("PT%d" % i, [128, 2, A]) for i in range(2)]
            Gp = [lsb("Gp%d" % i, [A, 256]) for i in range(2)]
            psY = [lps("psY0"), lps("psY1")]
            psX = [lps("psX0"), lps("psX1")]
            psT = lps("psT")
            psG = [lps("psG0"), lps("psG1")]
            psO = lps("psO")
            cnt = [0]

            def fwd(data_ap, krows, dkeys):
                i = cnt[0] % 2
                cnt[0] += 1
                mm(P, psY[i][:, 0:4 * A], data_ap, T["F1v"][0:krows, :], True, True, dkeys + ["tab"], [("psY", i)])
                tt(P, "vector", t1[i][:, 0:2 * A], psY[i][:, 0:2 * A], T["TwA"][:, :], ALU.mult, [("psY", i), "tab"],
                   [("t1", i)])
                tt(P, "vector", t2[i][:, 0:2 * A], psY[i][:, 2 * A:4 * A], T["TwB"][:, :], ALU.mult, [("psY", i), "tab"],
                   [("t2", i)])
                tt(P, "gpsimd", Yp[i][:, :], t1[i][:, 0:2 * A], t2[i][:, 0:2 * A], ALU.add, [("t1", i), ("t2", i)],
                   [("Yp", i)])
                mm(P, psX[i][0:A, :], Yp[i][:, 0:A], R4[:, 0, :], True, False, [("Yp", i), "tab"], [("psX", i)])
                mm(P, psX[i][0:A, :], Yp[i][:, A:2 * A], R4[:, 1, :], False, True, [("Yp", i), "tab"], [("psX", i)])
                return i

            for (ksrc, zsrc, ydst, nch) in jobs:
                dma(P, "sync", kmat[:, 0:nch, :], ksrc.rearrange("c (a b) -> a c b", b=128), (), ["kmat"])
                dma(P, "sync", Z[:, 0:nch, :], zsrc.rearrange("c (a b) -> a c b", b=128), (), ["Z"])
                for ch in range(nch):
                    i = fwd(kmat[:, ch, :], A, ["kmat"])
                    cp(P, "scalar", Kf[:, ch, :, :], psX[i][0:A, 0:256].rearrange("p (r e) -> p r e", r=2),
                       [("psX", i)], [("Kf", ch)])
                for ch in range(nch):
                    i = fwd(Z[:, ch, :], Ah, ["Z"])
                    xv = psX[i][0:A, :]
                    tt(P, "vector", t1[i][0:A, :].rearrange("p (r e) -> p r e", r=2),
                       xv[:, 0:256].rearrange("p (r e) -> p r e", r=2),
                       Kf[:, ch, 0, :].unsqueeze(1).broadcast_to([A, 2, 128]), ALU.mult, [("psX", i), ("Kf", ch)],
                       [("t1", i)])
                    tt(P, "vector", t2[i][0:A, :].rearrange("p (r e) -> p r e", r=2),
                       xv[:, 256:512].rearrange("p (r e) -> p r e", r=2),
                       Kf[:, ch, 1, :].unsqueeze(1).broadcast_to([A, 2, 128]), ALU.mult, [("psX", i), ("Kf", ch)],
                       [("t2", i)])
                    tt(P, "gpsimd", Pm[i][:, :], t1[i][0:A, :], t2[i][0:A, :], ALU.add, [("t1", i), ("t2", i)],
                       [("Pm", i)])
                    for r in range(2):
                        P.add("tensor", lambda e, i=i, r=r: e.transpose(psT[:, r * A:(r + 1) * A],
                                                                         Pm[i][:, 128 * r:128 * (r + 1)],
                                                                         ident_f[0:A, 0:A]),
                              [("Pm", i), "identf"], ["psT"])
                    cp(P, "scalar", PT[i][:, :, :], psT[:, 0:2 * A].rearrange("p (r c) -> p r c", r=2), ["psT"],
                       [("PT", i)])
                    mm(P, psG[i][0:A, :], PT[i][:, 0, :], R4[:, 2, :], True, False, [("PT", i), "tab"], [("psG", i)])
                    mm(P, psG[i][0:A, :], PT[i][:, 1, :], R4[:, 3, :], False, True, [("PT", i), "tab"], [("psG", i)])
                    tt(P, "vector", t1[i][0:A, :], psG[i][0:A, 0:256], T["TwTA"][:, :], ALU.mult, [("psG", i), "tab"],
                       [("t1", i)])
                    tt(P, "vector", t2[i][0:A, :], psG[i][0:A, 256:512], T["TwTB"][:, :], ALU.mult, [("psG", i), "tab"],
                       [("t2", i)])
                    tt(P, "gpsimd", Gp[i][:, :], t1[i][0:A, :], t2[i][0:A, :], ALU.subtract, [("t1", i), ("t2", i)],
                       [("Gp", i)])
                    mm(P, psO[0:Ah, 0:128], T["CA"][:, :], Gp[i][:, 0:128], True, False, [("Gp", i), "tab"], ["psO"])
                    mm(P, psO[0:Ah, 0:128], T["SA"][:, :], Gp[i][:, 128:256], False, True, [("Gp", i), "tab"], ["psO"])
                    cp(P, "scalar", Yo[:, ch, :], psO[0:Ah, 0:128], ["psO"], ["Yo"])
                dma(P, "sync", ydst.rearrange("c (a b) -> a c b", b=128), Yo[:, 0:nch, :], ["Yo"], ["ycv"])
        return f

    def ph_hypost(l):
        def f(lsb, lps):
            sk = lsb("sk", [64, 1])
            yc = [lsb("yc0", [64, NT]), lsb("yc1", [64, NT])]
            zb = [lsb("zb0", [64, NT]), lsb("zb1", [64, NT])]
            xb = [lsb("xb0", [64, NT]), lsb("xb1", [64, NT])]
            ob = [lsb("yob0", [64, NT], BF16), lsb("yob1", [64, NT], BF16)]
            cd = [lsb("ycd0", [64, NT], BF16), lsb("ycd1", [64, NT], BF16)]
            ac = [lsb("yac0", [64, NT], BF16), lsb("yac1", [64, NT], BF16)]
            dma(P, "sync", sk[:, :], I.skip_m[l], (), ["sk"])
            for jb in range(4):
                b = jb % 2
                sl = slice(NT * jb, NT * (jb + 1))
                dma(P, "sync", yc[b][:, :], S.ycv[l][:, sl], (), [("yc", b)])
                dma(P, "sync", zb[b][:, :], S.zT[l][:, sl], (), [("zb", b)])
                dma(P, "sync", xb[b][:, :], S.x0T[l][:, sl], (), [("xb", b)])
                ts(P, "vector", yc[b][:, :], yc[b][:, :], rn_m[:, 0:1], None, ALU.mult, None, [("yc", b), ("rn", 128)],
                   [("yc", b)])
                stt(P, "vector", yc[b][:, :], zb[b][:, :], sk[:, 0:1], yc[b][:, :], ALU.mult, ALU.add,
                    [("zb", b), "sk", ("yc", b)], [("yc", b)])
                tt(P, "gpsimd", ob[b][:, :], yc[b][:, :], xb[b][:, :], ALU.mult, [("yc", b), ("xb", b)], [("ob", b)])
                dma(P, "sync", S.yb_in[l][:, sl], ob[b][:, :], [("ob", b)], ["yb_in"])
            P.add("gpsimd", lambda e: e.collective_compute(
                "AllGather", ALU.bypass, replica_groups=G4, ins=[S.yb_in[l].ap().opt()], outs=[S.yb_all[l].ap().opt()]),
                ["yb_in"], ["yb_all"], kind="cc")
            ci = 0
            for g in range(4):
                a = ac[g % 2]
                for j in range(4):
                    c = cd[ci % 2]
                    ck = ("ycd", ci % 2)
                    ci += 1
                    dma(P, "sync", c[:, :], S.yb_all[l][64 * g:64 * g + 64, NT * j:NT * (j + 1)], ["yb_all"], [ck])
                    if j == 0:
                        ts(P, "vector", a[:, :], c[:, :], sel[:, 0:1], None, ALU.mult, None, [ck, "sel"], [("yac", g % 2)])
                    else:
                        stt(P, "vector", a[:, :], c[:, :], sel[:, j:j + 1], a[:, :], ALU.mult, ALU.add,
                            [ck, "sel", ("yac", g % 2)], [("yac", g % 2)])
                dma(P, "sync", S.yT[l][8 + g, :, 0:NT], a[:, :], [("yac", g % 2)], ())
        return f

    def ph_hypostc(l):
        def f(lsb, lps):
            sk = lsb("skc", [64, 4])
            yc = lsb("ycc", [64, 4, NC])
            zb = lsb("zbc", [64, 4, NC])
            xb = lsb("xbc", [64, 4, NC])
            ob = lsb("obc", [64, 4, NC], BF16)
            dma(P, "sync", sk[:, :], I.skip_c[l], (), ["sk"])
            dma(P, "sync", yc[:, :, :], S.ycvc[l].ap().rearrange("g c t -> c g t"), (), ["yc"])
            dma(P, "sync", zb[:, :, :], S.zTc[l].ap().rearrange("g c t -> c g t"), (), ["zb"])
            dma(P, "sync", xb[:, :, :], S.x0Tc[l].ap().rearrange("g c t -> c g t"), (), ["xb"])
            for g in range(4):
                ts(P, "vector", yc[:, g, :], yc[:, g, :], rn_c[:, g:g + 1], None, ALU.mult, None, ["yc", ("rn", 4)],
                   [("ycg", g)])
                stt(P, "vector", yc[:, g, :], zb[:, g, :], sk[:, g:g + 1], yc[:, g, :], ALU.mult, ALU.add,
                    ["zb", "sk", ("ycg", g)], [("ycg", g)])
                tt(P, "vector", ob[:, g, :], yc[:, g, :], xb[:, g, :], ALU.mult, [("ycg", g), "xb"], [("obc", g)])
                dma(P, "sync", S.yT[l][8 + g, :, NT:NTOT], ob[:, g, :], [("obc", g)], ())
        return f


    def ph_hyprec1(l):
        def f(lsb, lps):
            cw = lsb("cwc", [64, 3, 3])
            cb = lsb("cbc", [64, 3])
            u = lsb("uc_in", [64, 12, NC + 2])
            us = lsb("uc_sel", [64, 3, NC + 2])
            uc = lsb("uc_out", [64, 3, NC])
            dma(P, "sync", cw[:, :, :], I.convw[l], (), ["cw"])
            dma(P, "sync", cb[:, :], I.convb[l], (), ["cb"])
            P.add("gpsimd", lambda e: e.memset(u[:, :, 0:1], 0.0), (), ["u0"])
            P.add("gpsimd", lambda e: e.memset(u[:, :, NC + 1:NC + 2], 0.0), (), ["u1"])
            dma(P, "sync", u[:, :, 1:NC + 1], S.uTc[l].ap().rearrange("(q c) t -> c q t", c=64), (), ["u"])
            for s_ in range(3):
                for g in range(4):
                    if g == 0:
                        ts(P, "vector", us[:, s_, :], u[:, 4 * s_, :], sel[:, 0:1], None, ALU.mult, None,
                           ["u", "u0", "u1", "sel"], [("us", s_)])
                    else:
                        stt(P, "vector", us[:, s_, :], u[:, 4 * s_ + g, :], sel[:, g:g + 1], us[:, s_, :], ALU.mult,
                            ALU.add, ["u", "u0", "u1", "sel", ("us", s_)], [("us", s_)])
                conv3("vector", uc[:, s_, :], us[:, s_, :], cw[:, s_, :], cb[:, s_:s_ + 1], NC,
                      [("us", s_), "cw", "cb"], ("ucc", s_))
            tt(P, "vector", uc[:, 2, :], uc[:, 2, :], uc[:, 1, :], ALU.mult, [("ucc", 2), ("ucc", 1)], [("ucc", 2)])
            dma(P, "sync", S.zTc1[l][:, :], uc[:, 2, :], [("ucc", 2)], ["zTc"])
            dma(P, "sync", S.x0Tc1[l][:, :], uc[:, 0, :], [("ucc", 0)], ["x0Tc"])
        return f

    def ph_hypostc1(l):
        def f(lsb, lps):
            sk = lsb("skc", [64, 1])
            yc = lsb("ycc", [64, NC])
            zb = lsb("zbc", [64, NC])
            xb = lsb("xbc", [64, NC])
            ob = lsb("obc", [64, NC], BF16)
            dma(P, "sync", sk[:, :], I.skip_m[l], (), ["sk"])
            dma(P, "sync", yc[:, :], S.ycvc1[l][:, :], (), ["yc"])
            dma(P, "sync", zb[:, :], S.zTc1[l][:, :], (), ["zb"])
            dma(P, "sync", xb[:, :], S.x0Tc1[l][:, :], (), ["xb"])
            ts(P, "vector", yc[:, :], yc[:, :], rn_c[:, 0:1], None, ALU.mult, None, ["yc", ("rn", 4)], ["yc"])
            stt(P, "vector", yc[:, :], zb[:, :], sk[:, 0:1], yc[:, :], ALU.mult, ALU.add, ["zb", "sk", "yc"], ["yc"])
            tt(P, "vector", ob[:, :], yc[:, :], xb[:, :], ALU.mult, ["yc", "xb"], ["obc"])
            dma(P, "sync", S.ybc_in[l][:, :], ob[:, :], ["obc"], ["ybc_in"])
            P.add("gpsimd", lambda e: e.collective_compute(
                "AllGather", ALU.bypass, replica_groups=G4, ins=[S.ybc_in[l].ap().opt()],
                outs=[S.ybc_all[l].ap().opt()]), ["ybc_in"], ["ybc_all"], kind="cc")
            for g in range(4):
                dma(P, "sync", S.yT[l][8 + g, :, NT:NTOT], S.ybc_all[l][64 * g:64 * g + 64, :], ["ybc_all"], ())
        return f

    def ph_oproj(l, zero_groups=()):
        last = l == DEPTH - 1

        def f(lsb, lps):
            Wo = lsb("Wo", [64, 16, D], BF16)
            Y = [lsb("Y0", [64, 16, 512], BF16), lsb("Y1", [64, 16, 512], BF16)]
            ps = [lps("ps_op0"), lps("ps_op1")]
            dma(P, "gpsimd", Wo[:, :, :], I.w_out[l], (), ["Wo"])
            it = 0
            for ti in range(5):
                off, n, c = TILES[ti]
                if c == 1 and last:
                    continue
                y = Y[ti % 2]
                dma(P, "sync", y[:, :, 0:n], S.yT[l][:, :, off:off + n].rearrange("g f t -> f g t"), (),
                    [("Y", ti % 2)])
                for g in zero_groups:
                    P.add("gpsimd", lambda e, y=y, g=g, n=n: e.memset(y[:, g, 0:n], 0.0), [("Y", ti % 2)],
                          [("Y", ti % 2)])
                for dc in range(8):
                    b = it % 2
                    it += 1
                    for g in range(16):
                        mm(P, ps[b][:, 0:n], Wo[:, g, 128 * dc:128 * (dc + 1)], y[:, g, 0:n], g == 0, g == 15,
                           ["Wo", ("Y", ti % 2)], [("ps_op", b)])
                    stt(P, "vector", h[:, dc, off:off + n], ps[b][:, 0:n], Gm[:, 1, dc, c:c + 1],
                        h[:, dc, off:off + n], ALU.mult, ALU.add, [("ps_op", b), ("Gm", 1, c), ("h", dc, ti)],
                        [("h", dc, ti)])
        return f

    def ph_out(lsb, lps):
        for k in range(8):
            dma(P, "sync", out_hT[128 * k:128 * (k + 1), :], h[:, k, 0:NT], [("h", k, i) for i in range(4)], ())
            if DEBUG_OUT:
                dma(P, "sync", dbg[128 * k:128 * (k + 1), :], h[:, k, :], [("h", k, i) for i in range(5)], ())

    for l in range(DEPTH):
        phase(ph_adaln(l))
        phase(ph_ffn(l, 0, [0, 1]))
        phase(ph_ffn(l, 0, [2, 3, 4]))
        if stop_after == ("ffn1", l):
            break
        phase(ph_proj(l))
        phase(ph_attn(l))
        if stop_after == ("attn", l):
            phase(ph_oproj(l, zero_groups=range(8, 16)))
            break
        phase(ph_na(l))
        if stop_after == ("na", l):
            phase(ph_oproj(l, zero_groups=range(8, 12)))
            break
        phase(ph_hypre(l))
        phase(ph_filt(l, 128, [(I.w3m[l, :, 0, :], I.w3m[l, :, 1, :])], I.ndel_m[:, :], rn_m, lambda g: S.kfT[l]))
        phase(ph_fft(l, 128, [(S.kfT[l][32 * hf:32 * hf + 32, :], S.zT[l][32 * hf:32 * hf + 32, :],
                               S.ycv[l][32 * hf:32 * hf + 32, :], 32) for hf in range(2)]))
        phase(ph_hypost(l))
        if l < DEPTH - 1:
            phase(ph_hyprec1(l))
            phase(ph_filt(l, 4, [(I.w3m[l, :, 0, :], I.w3m[l, :, 1, :])], I.ndel_m[:, :], rn_c, lambda g: S.kfTc1[l]))
            phase(ph_fft(l, 4, [(S.kfTc1[l][32 * hf:32 * hf + 32, :], S.zTc1[l][32 * hf:32 * hf + 32, :],
                                 S.ycvc1[l][32 * hf:32 * hf + 32, :], 32) for hf in range(2)]))
            phase(ph_hypostc1(l))
        phase(ph_oproj(l))
        if stop_after == ("mix", l):
            break
        phase(ph_ffn(l, 2, [0, 1]))
        phase(ph_ffn(l, 2, [2, 3, 4] if l < DEPTH - 1 else [2, 3]))
    phase(ph_out)
    P.close()
    for cm in reversed(cms):
        cm.__exit__(None, None, None)
    return nc


def fft_tables(A):
    N = 128 * A
    a = np.arange(A)[:, None].astype(np.float64)
    c = np.arange(A)[None, :].astype(np.float64)
    FrA = np.cos(2 * np.pi * a * c / A)
    FiA = -np.sin(2 * np.pi * a * c / A)
    b = np.arange(128)[:, None].astype(np.float64)
    Twr = np.cos(2 * np.pi * b * c / N)
    Twi = -np.sin(2 * np.pi * b * c / N)
    a2 = np.arange(A // 2)[None, :].astype(np.float64)
    cp_ = np.arange(A)[:, None].astype(np.float64)
    f = lambda x: np.ascontiguousarray(x.astype(np.float32))
    return dict(F1v=f(np.concatenate([FrA, FiA, -FiA, FrA], 1)), TwA=f(np.concatenate([Twr, Twr], 1)),
                TwB=f(np.concatenate([Twi, Twi], 1)), TwTA=f(np.concatenate([Twr.T, Twr.T], 1)),
                TwTB=f(np.concatenate([Twi.T, Twi.T], 1)), CA=f(np.cos(2 * np.pi * cp_ * a2 / A) / N),
                SA=f(-np.sin(2 * np.pi * cp_ * a2 / A) / N))


def filter_tables(n):
    p = np.arange(2 * n)
    pos = np.where(p < n, p, 2 * n - p).astype(np.float32)
    pos[n] = 0.0
    t = (pos / np.float32(max(n - 1, 1))).astype(np.float32)
    bands = np.linspace(1e-4, 15, 16, dtype=np.float32)
    ang = (np.float32(2.0 * math.pi / n) * pos[:, None] * bands[None, :]).astype(np.float32)
    feats = np.concatenate([t[:, None], np.cos(ang), -np.sin(ang)], axis=-1).astype(np.float32)
    tl = t.copy()
    tl[n] = 1.0e4
    return np.ascontiguousarray(feats.T), np.ascontiguousarray(tl[None, :])


def hyena_deltas():
    return np.abs(np.linspace(math.log(1e-2) / 1.5, math.log(1e-2) / 0.3, 256, dtype=np.float32))


def prep_inputs(inputs):
    x = np.asarray(inputs["x"], np.float32)
    ctx = np.asarray(inputs["ctx"], np.float32)
    c = np.asarray(inputs["c"], np.float32)
    c_ctx = np.asarray(inputs["c_ctx"], np.float32)
    shared = {}
    f32 = lambda k: np.asarray(inputs[k], np.float32)
    shared["w_ada_p"] = np.ascontiguousarray(f32("w_ada").reshape(DEPTH, 8, 128, 9, 1024).transpose(0, 3, 2, 1, 4))
    for i in (1, 2):
        g_ = f32("w_ffn%d_gate" % i).reshape(DEPTH, 8, 128, NFC, 128)
        u_ = f32("w_ffn%d_up" % i).reshape(DEPTH, 8, 128, NFC, 128)
        gu = np.stack([g_, u_], axis=0)
        shared["w_ffn%d_gu" % i] = np.ascontiguousarray(gu.transpose(1, 4, 3, 0, 2, 5))
        d_ = f32("w_ffn%d_down" % i).reshape(DEPTH, NFC, 128, 8, 128)
        shared["w_ffn%d_dn" % i] = np.ascontiguousarray(d_.transpose(0, 3, 2, 1, 4))
    g3 = np.stack([inputs["g_ffn1"], inputs["g_mix"], inputs["g_ffn2"]], axis=1).astype(np.float32)
    shared["g3T"] = np.ascontiguousarray(g3.reshape(DEPTH, 3, 8, 128).transpose(0, 3, 1, 2))
    shared["b_adaT"] = np.ascontiguousarray(
        np.asarray(inputs["b_ada"], np.float32).reshape(DEPTH, 72, 128).transpose(0, 2, 1))
    w_in = np.asarray(inputs["w_in"], np.float32)
    perm = np.concatenate([np.arange(0, 512), np.arange(1536, 1792), np.arange(512, 640), np.arange(1792, 2048),
                           np.arange(768, 1536), np.arange(640, 768), np.arange(2048, 2304)])
    shared["w_in_p"] = np.ascontiguousarray(w_in[:, :, perm].reshape(DEPTH, 8, 128, 2304).transpose(0, 2, 1, 3))
    shared["w_out_p"] = np.ascontiguousarray(f32("w_out").reshape(DEPTH, 16, 64, D).transpose(0, 2, 1, 3))
    shared["gqkT"] = np.ascontiguousarray(np.stack(
        [inputs["g_q_attn"], inputs["g_k_attn"], inputs["g_q_na"], inputs["g_k_na"]], axis=-1).astype(np.float32))
    rot = np.zeros((64, 64), np.float32)
    for i in range(64):
        if i % 32 < 16:
            rot[i, i + 16] = -1.0
        else:
            rot[i, i - 16] = 1.0
    shared["rotT"] = np.ascontiguousarray(rot.T)
    rpb = np.asarray(inputs["na_rpb"], np.float32)
    kc = np.arange(64)[:, None]
    qc = np.arange(64)[None, :]
    cs = np.clip(qc - 8, 0, 48)
    colvalid = (kc >= cs) & (kc < cs + 16)
    cidx = np.clip(kc - qc + 15, 0, 30)
    tpad = np.full((DEPTH, 4, 23, 64, 64), -BIG, np.float32)
    for dp in range(23):
        dr = 18 - dp
        if 0 <= dr <= 14:
            tpad[:, :, dp] = np.where(colvalid[None, None], rpb[:, :, dr][:, :, cidx], np.float32(-BIG))
    shared["na_tpad"] = tpad
    shared["na_bq"] = np.ascontiguousarray((np.arange(512)[None, :] // 64 == np.arange(8)[:, None]).astype(np.float32))
    shared["ident"] = np.eye(128, dtype=np.float32)
    for A_, n_ in ((128, SEQ), (4, NC)):
        for k_, v_ in fft_tables(A_).items():
            shared["%s_%d" % (k_, A_)] = v_
        ft_, tl_ = filter_tables(n_)
        shared["featsT_" + ("m" if A_ == 128 else "c")] = ft_
        shared["tlist_" + ("m" if A_ == 128 else "c")] = tl_
    bb = np.arange(128)[:, None].astype(np.float64)
    ee = np.arange(128)[None, :].astype(np.float64)
    Cm = np.cos(2 * np.pi * bb * ee / 128)
    Sm = np.sin(2 * np.pi * bb * ee / 128)
    shared["fftR4"] = np.ascontiguousarray(np.stack([
        np.concatenate([Cm, -Sm, Sm, Cm], 1), np.concatenate([Sm, Cm, -Cm, Sm], 1),
        np.concatenate([Cm, Sm, -Sm, Cm], 1), np.concatenate([-Sm, Cm, -Cm, -Sm], 1)]).astype(np.float32))
    conv_w = np.asarray(inputs["conv_w"], np.float32)
    conv_b = np.asarray(inputs["conv_b"], np.float32)
    cw4 = conv_w.reshape(DEPTH, 3, 3, 4, 64)
    cb4 = conv_b.reshape(DEPTH, 3, 4, 64)
    shared["hyc_convw"] = np.ascontiguousarray(cw4.transpose(0, 4, 2, 3, 1).reshape(DEPTH, 64, 12, 3))
    shared["hyc_convb"] = np.ascontiguousarray(cb4.transpose(0, 3, 1, 2).reshape(DEPTH, 64, 12))
    shared["fw1"] = np.ascontiguousarray(np.asarray(inputs["filt_w1"], np.float32))
    shared["fw2"] = np.ascontiguousarray(np.asarray(inputs["filt_w2"], np.float32))
    shared["fvec"] = np.ascontiguousarray(np.stack(
        [inputs["filt_b1"], inputs["filt_b2"], inputs["filt_freq"]], axis=-1).astype(np.float32))
    w3 = np.asarray(inputs["filt_w3"], np.float32)
    shared["w3c"] = np.ascontiguousarray(w3)
    skip = np.asarray(inputs["hyena_skip"], np.float32)
    shared["skip_c"] = np.ascontiguousarray(skip.reshape(DEPTH, 4, 64).transpose(0, 2, 1))
    dl = hyena_deltas()
    shared["ndel_c"] = np.ascontiguousarray((-dl).reshape(4, 64).T)
    inv = (10000.0 ** (-np.arange(16, dtype=np.float32) / 16)).astype(np.float32)
    maps = []
    for r in range(8):
        b, qd = r // 4, r % 4
        m = dict(shared)
        m["xT"] = np.ascontiguousarray(x[b, NT * qd:NT * (qd + 1), :].T)
        m["ctxT"] = np.ascontiguousarray(ctx[b].T)
        cc = np.stack([c[b], c_ctx], axis=-1)
        m["cT"] = np.ascontiguousarray(cc.reshape(8, 128, 2).transpose(1, 0, 2))
        sl_ = np.zeros((64, 4), np.float32)
        sl_[:, qd] = 1.0
        m["sel"] = sl_
        m["hy_convw"] = np.ascontiguousarray(cw4[:, :, :, qd, :].transpose(0, 3, 2, 1))
        m["hy_convb"] = np.ascontiguousarray(cb4[:, :, qd, :].transpose(0, 2, 1))
        m["w3m"] = np.ascontiguousarray(w3.reshape(DEPTH, 64, 2, 4, 64)[:, :, :, qd, :])
        m["ndel_m"] = np.ascontiguousarray(-dl[64 * qd:64 * qd + 64, None])
        m["skip_m"] = np.ascontiguousarray(skip[:, 64 * qd:64 * qd + 64, None])
        rm = np.zeros((8, 44, 128), np.float32)
        it = 0
        for ti in range(4):
            for (kind, j, cch) in na_iters(ti):
                for qrl in range(8):
                    qr = 32 * qd + 8 * ti + qrl
                    rs = min(max(qr - 4, 0), 120)
                    for kl in range(2):
                        kr = 32 * qd + 8 * ti - 4 + 2 * cch + kl
                        ok = rs <= kr < rs + 8
                        if kind == "top":
                            ok = ok and (j == qd - 1)
                        elif kind == "bot":
                            ok = ok and (j == qd + 1)
                        if not ok:
                            rm[qrl, it, 64 * kl:64 * kl + 64] = -8.0 * BIG
                it += 1
        assert it == 44
        m["na_rmA"] = rm
        t = np.arange(NT * qd, NT * (qd + 1))
        rows = (t // GRID_W).astype(np.float32)
        cols = (t % GRID_W).astype(np.float32)
        ang = np.concatenate([rows[None, :] * inv[:, None]] * 2 + [cols[None, :] * inv[:, None]] * 2, axis=0)
        m["ropecos"] = np.ascontiguousarray(np.cos(ang).astype(np.float32))
        m["ropesin"] = np.ascontiguousarray(np.sin(ang).astype(np.float32))
        maps.append(m)
    return maps


def kernel(**inputs):
    nc = build_nc()
    maps = prep_inputs(inputs)
    res = run_bass_kernel_spmd(nc, maps, core_ids=list(range(8)))
    out = np.zeros((2, SEQ, D), np.float32)
    for r in range(8):
        b, qd = r // 4, r % 4
        out[b, NT * qd:NT * (qd + 1), :] = res.results[r]["out_hT"].T
    return out
```

```python
import math
import numpy as np
import ml_dtypes
import concourse.bass as bass
import concourse.mybir as mybir
from concourse.bass_utils import run_bass_kernel_spmd

F32 = mybir.dt.float32
BF16 = mybir.dt.bfloat16
AF = mybir.ActivationFunctionType
ALU = mybir.AluOpType
AX = mybir.AxisListType

D = 1024
DEPTH = 4
SEQ = 8192
NT = 2048
NC = 256
NTOT = NT + NC
FF = 2816
NFC = FF // 128
GRID_W = 64
EPS = 1e-6
BIG = 30000.0
G4 = [[0, 1, 2, 3], [4, 5, 6, 7]]

DEBUG_OUT = False
ENGS = ("tensor", "vector", "scalar", "gpsimd", "sync")


class Op:
    __slots__ = ("eng", "fn", "kind", "deps", "needed", "sem", "val", "idx", "prev")

    def __init__(self, eng, fn, kind):
        self.eng = eng
        self.fn = fn
        self.kind = kind
        self.deps = []
        self.needed = False
        self.sem = None
        self.val = None


class Prog:
    def __init__(self, nc, n_dma_sems=8):
        self.nc = nc
        self.stack = []
        self.csem = {}
        self.ccount = {}
        self.csem_pool = {}
        for e in ENGS:
            self.csem[e] = self._newsem("c_" + e)
            self.ccount[e] = 0
        self.dsem = {}
        self.dval = {}
        self.dnext = {}
        for e in ("sync", "gpsimd", "scalar"):
            self.dsem[e] = [self._newsem("d_%s%d" % (e, i)) for i in range(n_dma_sems)]
            self.dval[e] = [0] * n_dma_sems
            self.dnext[e] = 0
        self.ccsem = self._newsem("cc")
        self.ccval = 0
        self.known = {e: {} for e in ENGS}
        self.reset_phase()

    def _newsem(self, name):
        cm = self.nc.semaphore(name)
        s = cm.__enter__()
        self.stack.append(cm)
        return s

    def reset_phase(self):
        self.ops = {e: [] for e in ENGS}
        self.last_w = {}
        self.readers = {}

    def add(self, eng, fn, reads=(), writes=(), kind="c"):
        op = Op(eng, fn, kind)
        deps = {}
        for k in reads:
            w = self.last_w.get(k)
            if w is not None:
                deps[id(w)] = w
        for k in writes:
            w = self.last_w.get(k)
            if w is not None:
                deps[id(w)] = w
            for r in self.readers.get(k, ()):
                deps[id(r)] = r
        for d in deps.values():
            if d is op:
                continue
            if d.eng == eng and d.kind == "c" and kind == "c" and eng == "tensor":
                continue
            op.deps.append(d)
            d.needed = True
        for k in reads:
            self.readers.setdefault(k, []).append(op)
        for k in writes:
            self.last_w[k] = op
            self.readers[k] = []
        self.ops[eng].append(op)
        return op

    def emit(self, block):
        for e in ENGS:
            for op in self.ops[e]:
                if op.kind == "c":
                    if op.needed:
                        self.ccount[e] += 1
                        op.sem, op.val = self.csem[e], self.ccount[e]
                elif op.kind == "d":
                    i = self.dnext[e]
                    self.dnext[e] = (i + 1) % len(self.dsem[e])
                    op.idx = i
                    op.sem = self.dsem[e][i]
                    op.prev = self.dval[e][i]
                    self.dval[e][i] += 16
                    op.val = self.dval[e][i]
                elif op.kind == "cc":
                    self.ccval += 1
                    op.sem, op.val = self.ccsem, self.ccval
        prog = self

        def make(e):
            def body(engine):
                known = prog.known[e]

                def wait(sem, val):
                    key = id(sem)
                    if known.get(key, 0) >= val:
                        return
                    known[key] = val
                    engine.wait_ge(sem, val)

                for op in prog.ops[e]:
                    for d in op.deps:
                        wait(d.sem, d.val)
                    if op.kind == "d" and op.prev > 0:
                        wait(op.sem, op.prev)
                    ins = op.fn(engine)
                    if op.kind == "c":
                        if op.needed:
                            ins.then_inc(op.sem, 1)
                    elif op.kind == "d":
                        ins.then_inc(op.sem, 16)
                    else:
                        ins.then_inc(op.sem)
                if e in prog.dsem:
                    for i, s in enumerate(prog.dsem[e]):
                        if prog.dval[e][i] > 0:
                            wait(s, prog.dval[e][i])
                if e == "gpsimd" and prog.ccval > 0:
                    wait(prog.ccsem, prog.ccval)
            return body

        for e in ENGS:
            getattr(block, e)(make(e))
        self.reset_phase()

    def close(self):
        for cm in reversed(self.stack):
            cm.__exit__(None, None, None)


class Ctx:
    pass


def mm(P, out, lhsT, rhs, start, stop, reads, writes):
    return P.add("tensor", lambda e: e.matmul(out, lhsT, rhs, start=start, stop=stop), reads, writes)


def dma(P, q, out, in_, reads, writes, slow=False):
    if slow:
        return P.add(q, lambda e: e.dma_start(out=out, in_=in_, allow_slow_non_contiguous=True), reads, writes,
                     kind="d")
    return P.add(q, lambda e: e.dma_start(out=out, in_=in_), reads, writes, kind="d")


def act(P, out, in_, func, reads, writes, bias=0.0, scale=1.0):
    return P.add("scalar", lambda e: e.activation(out, in_, func, bias=bias, scale=scale), reads, writes)


def tt(P, eng, out, in0, in1, op, reads, writes):
    return P.add(eng, lambda e: e.tensor_tensor(out, in0, in1, op), reads, writes)


def ts(P, eng, out, in0, s1, s2, op0, op1, reads, writes):
    if op1 is None:
        return P.add(eng, lambda e: e.tensor_scalar(out, in0, s1, None, op0), reads, writes)
    return P.add(eng, lambda e: e.tensor_scalar(out, in0, s1, s2, op0, op1), reads, writes)


def stt(P, eng, out, in0, scalar, in1, op0, op1, reads, writes):
    return P.add(eng, lambda e: e.scalar_tensor_tensor(out, in0, scalar, in1, op0, op1), reads, writes)


def cp(P, eng, out, in_, reads, writes):
    if eng == "scalar":
        return P.add(eng, lambda e: e.copy(out, in_), reads, writes)
    return P.add(eng, lambda e: e.tensor_copy(out, in_), reads, writes)


def na_iters(ti):
    out = []
    for c in range(8):
        lr = 8 * ti - 4 + 2 * c
        if lr < 0:
            out += [("top", j, c) for j in range(4)]
        elif lr >= 32:
            out += [("bot", j, c) for j in range(4)]
        else:
            out.append(("own", 0, c))
    return out


TILES = [(0, 512, 0), (512, 512, 0), (1024, 512, 0), (1536, 512, 0), (2048, 256, 1)]


def build_nc(stop_after=None):
    nc = bass.Bass("TRN2", target_bir_lowering=False)
    C = Ctx()
    C.nc = nc

    def din(name, shape, dt=F32):
        return nc.dram_tensor(name, list(shape), dt, kind="ExternalInput").ap()

    def dscr(name, shape, dt=F32):
        return nc.dram_tensor(name, list(shape), dt)

    I = Ctx()
    I.xT = din("xT", [D, NT])
    I.ctxT = din("ctxT", [D, NC])
    I.cT = din("cT", [128, 8, 2])
    I.w_ada = din("w_ada_p", [DEPTH, 9, 128, 8, 1024])
    I.b_ada = din("b_adaT", [DEPTH, 128, 72])
    I.g3 = din("g3T", [DEPTH, 128, 3, 8])
    I.wgu = [din("w_ffn1_gu", [DEPTH, NFC, 128, 2, 8, 128]), din("w_ffn2_gu", [DEPTH, NFC, 128, 2, 8, 128])]
    I.wd = [din("w_ffn1_dn", [DEPTH, 8, 128, NFC, 128]), din("w_ffn2_dn", [DEPTH, 8, 128, NFC, 128])]
    I.w_in = din("w_in_p", [DEPTH, 128, 8, 2304])
    I.w_out = din("w_out_p", [DEPTH, 64, 16, D])
    I.gqk = din("gqkT", [DEPTH, 64, 4])
    I.rotT = din("rotT", [64, 64])
    I.cos = din("ropecos", [64, NT])
    I.sin = din("ropesin", [64, NT])
    I.tpad = din("na_tpad", [DEPTH, 4, 23, 64, 64])
    I.rmA = din("na_rmA", [8, 44, 128])
    I.bq = din("na_bq", [8, 512])
    I.ident = din("ident", [128, 128])
    I.sel = din("sel", [64, 4])
    I.convw = din("hy_convw", [DEPTH, 64, 3, 3])
    I.convb = din("hy_convb", [DEPTH, 64, 3])
    I.convwc = din("hyc_convw", [DEPTH, 64, 12, 3])
    I.convbc = din("hyc_convb", [DEPTH, 64, 12])
    I.fw1 = din("fw1", [DEPTH, 33, 64])
    I.fw2 = din("fw2", [DEPTH, 64, 64])
    I.fvec = din("fvec", [DEPTH, 64, 3])
    I.w3m = din("w3m", [DEPTH, 64, 2, 64])
    I.w3c = din("w3c", [DEPTH, 64, 512])
    I.ndel_m = din("ndel_m", [64, 1])
    I.ndel_c = din("ndel_c", [64, 4])
    I.skip_m = din("skip_m", [DEPTH, 64, 1])
    I.skip_c = din("skip_c", [DEPTH, 64, 4])
    I.feats = {128: din("featsT_m", [33, 2 * SEQ]), 4: din("featsT_c", [33, 2 * NC])}
    I.tlist = {128: din("tlist_m", [1, 2 * SEQ]), 4: din("tlist_c", [1, 2 * NC])}
    I.ftab = {}
    for A_ in (128, 4):
        I.ftab[A_] = dict(F1v=din("F1v_%d" % A_, [A_, 4 * A_]), TwA=din("TwA_%d" % A_, [128, 2 * A_]),
                          TwB=din("TwB_%d" % A_, [128, 2 * A_]), TwTA=din("TwTA_%d" % A_, [A_, 256]),
                          TwTB=din("TwTB_%d" % A_, [A_, 256]), CA=din("CA_%d" % A_, [A_, A_ // 2]),
                          SA=din("SA_%d" % A_, [A_, A_ // 2]))
    I.R4 = din("fftR4", [4, 128, 512])
    S = Ctx()
    S.zT = [dscr("zT%d" % l, [64, SEQ]) for l in range(DEPTH)]
    S.x0T = [dscr("x0T%d" % l, [64, SEQ]) for l in range(DEPTH)]
    S.kfT = [dscr("kfT%d" % l, [64, 2 * SEQ]) for l in range(DEPTH)]
    S.ycv = [dscr("ycv%d" % l, [64, SEQ]) for l in range(DEPTH)]
    S.yb_in = [dscr("yb_in%d" % l, [64, SEQ], BF16) for l in range(DEPTH)]
    S.yb_all = [dscr("yb_all%d" % l, [256, SEQ], BF16) for l in range(DEPTH)]
    S.zTc1 = [dscr("zTc1_%d" % l, [64, NC]) for l in range(DEPTH)]
    S.x0Tc1 = [dscr("x0Tc1_%d" % l, [64, NC]) for l in range(DEPTH)]
    S.kfTc1 = [dscr("kfTc1_%d" % l, [64, 2 * NC]) for l in range(DEPTH)]
    S.ycvc1 = [dscr("ycvc1_%d" % l, [64, NC]) for l in range(DEPTH)]
    S.ybc_in = [dscr("ybc_in%d" % l, [64, NC], BF16) for l in range(DEPTH)]
    S.ybc_all = [dscr("ybc_all%d" % l, [256, NC], BF16) for l in range(DEPTH)]
    S.zTc = [dscr("zTc%d" % l, [4, 64, NC]) for l in range(DEPTH)]
    S.x0Tc = [dscr("x0Tc%d" % l, [4, 64, NC]) for l in range(DEPTH)]
    S.kfTc = [dscr("kfTc%d" % l, [4, 64, 2 * NC]) for l in range(DEPTH)]
    S.ycvc = [dscr("ycvc%d" % l, [4, 64, NC]) for l in range(DEPTH)]
    S.qT = [dscr("qT_scr%d" % l, [12, 64, NTOT], BF16) for l in range(DEPTH)]
    S.kT_in = [[dscr("kT_in%d_%d" % (l, p), [192, NT], BF16) for p in range(2)] for l in range(DEPTH)]
    S.kT_all = [[dscr("kT_all%d_%d" % (l, p), [4 * 192, NT], BF16) for p in range(2)] for l in range(DEPTH)]
    S.v_in = [[dscr("v_in%d_%d" % (l, p), [1024, 390], BF16) for p in range(2)] for l in range(DEPTH)]
    S.v_all = [[dscr("v_all%d_%d" % (l, p), [4 * 1024, 390], BF16) for p in range(2)] for l in range(DEPTH)]
    S.uT_in = [[dscr("uT_in%d_%d" % (l, p), [128, NT], F32) for p in range(6)] for l in range(DEPTH)]
    S.uT_all = [[dscr("uT_all%d_%d" % (l, p), [4 * 128, NT], F32) for p in range(6)] for l in range(DEPTH)]
    S.kTc = [dscr("kTc%d" % l, [384, NC], BF16) for l in range(DEPTH)]
    S.vc = [dscr("vc%d" % l, [NC, 390], BF16) for l in range(DEPTH)]
    S.uTc = [dscr("uTc%d" % l, [768, NC], F32) for l in range(DEPTH)]
    S.yT = [dscr("yT_scr%d" % l, [16, 64, NTOT], BF16) for l in range(DEPTH)]
    out_hT = nc.dram_tensor("out_hT", [D, NT], F32, kind="ExternalOutput").ap()
    dbg = nc.dram_tensor("dbg", [D, NTOT], F32, kind="ExternalOutput").ap() if DEBUG_OUT else None

    P = Prog(nc)
    cms = []

    def sb(name, shape, dt=F32):
        cm = nc.sbuf_tensor(name, list(shape), dt)
        t = cm.__enter__()
        cms.append(cm)
        return t

    h = sb("h", [128, 8, NTOT])
    ones_bf = sb("ones_bf", [128, 128], BF16)
    mods = sb("mods", [128, 9, 8, 2])
    A32 = sb("A32", [128, 3, 8, 2])
    Gm = sb("Gm", [128, 3, 8, 2])
    ones_f = sb("ones_f", [128, 128])
    ident_bf = sb("ident_bf", [128, 128], BF16)
    ident_f = sb("ident_f", [128, 128])
    rn_m = sb("rn_m", [64, 1])
    rn_c = sb("rn_c", [64, 1])
    sel = sb("sel_sb", [64, 4])
    nbias = sb("nbias", [128, 1])
    rotT = sb("rotT_sb", [64, 64])
    ropec = sb("ropec", [64, NT])
    ropes = sb("ropes", [64, NT])

    def phase(fn):
        local = []
        C.uid = getattr(C, "uid", 0) + 1
        u = "_%d" % C.uid

        def lsb(name, shape, dt=F32):
            cm = nc.sbuf_tensor(name + u, list(shape), dt)
            t = cm.__enter__()
            local.append(cm)
            return t

        def lps(name, shape=(128, 512), dt=F32):
            cm = nc.psum_tensor(name + u, list(shape), dt)
            t = cm.__enter__()
            local.append(cm)
            return t

        with nc.Block() as block:
            fn(lsb, lps)
            P.emit(block)
        for cm in reversed(local):
            cm.__exit__(None, None, None)

    def ph_init(lsb, lps):
        P.add("vector", lambda e: e.memset(ones_bf[:, :], 1.0), (), [("ones",)])
        P.add("vector", lambda e: e.memset(ones_f[:, :], 1.0), (), [("onesf",)])
        P.add("vector", lambda e: e.memset(nbias[:, :], -math.pi), (), ["nbias"])
        dma(P, "sync", rotT[:, :], I.rotT[:, :], (), ["rotT"])
        dma(P, "gpsimd", ident_bf[:, :], I.ident[:, :], (), ["ident"])
        dma(P, "sync", ident_f[:, :], I.ident[:, :], (), ["identf"])
        dma(P, "sync", sel[:, :], I.sel[:, :], (), ["sel"])
        dma(P, "sync", ropec[:, :], I.cos[:, :], (), ["ropec"])
        dma(P, "sync", ropes[:, :], I.sin[:, :], (), ["ropes"])
        for k in range(8):
            dma(P, "sync", h[:, k, 0:NT], I.xT[128 * k:128 * (k + 1), :], (), [("h", k, i) for i in range(4)])
            dma(P, "sync", h[:, k, NT:NTOT], I.ctxT[128 * k:128 * (k + 1), :], (), [("h", k, 4)])

    phase(ph_init)

    def ph_adaln(l):
        def f(lsb, lps):
            cts = lsb("cts", [128, 8, 2])
            cbf = lsb("cbf", [128, 8, 2], BF16)
            wp = [lsb("wp0", [128, 8, 1024], BF16), lsb("wp1", [128, 8, 1024], BF16)]
            bias = lsb("bias", [128, 72])
            g32 = lsb("g32", [128, 3, 8])
            ps = lps("ps_ada", [128, 9, 8, 2])
            dma(P, "sync", cts[:, :, :], I.cT[:, :, :], (), ["cts"])
            dma(P, "sync", bias[:, :], I.b_ada[l], (), ["bias"])
            dma(P, "sync", g32[:, :, :], I.g3[l], (), ["g32"])
            act(P, cbf[:, :, :], cts[:, :, :], AF.Silu, ["cts"], ["cbf"])
            for m in range(9):
                w = wp[m % 2]
                dma(P, "gpsimd", w[:, :, :], I.w_ada[l, m], (), [("wp", m % 2)])
                for dc in range(8):
                    for k in range(8):
                        mm(P, ps[:, m, dc, :], w[:, k, 128 * dc:128 * (dc + 1)], cbf[:, k, :], k == 0, k == 7,
                           [("wp", m % 2), "cbf"], [("psada", m)])
            bv = bias[:, :].rearrange("p (m k) -> p m k", k=8)
            for c in range(2):
                tt(P, "vector", mods[:, :, :, c], ps[:, :, :, c], bv, ALU.add,
                   [("psada", m) for m in range(9)] + ["bias"], [("mods", c)])
            ts(P, "vector", g32[:, :, :], g32[:, :, :], 32.0, None, ALU.mult, None, ["g32"], ["g32"])
            for j in range(3):
                for c in range(2):
                    stt(P, "vector", A32[:, j, :, c], mods[:, 3 * j + 1, :, c], 1.0, g32[:, j, :], ALU.add, ALU.mult,
                        [("mods", c), "g32"], [("A32", j, c)])
                    ts(P, "vector", Gm[:, j, :, c], mods[:, 3 * j + 2, :, c], (1.0 if j == 1 else 0.5), None, ALU.mult,
                       None, [("mods", c)], [("Gm", j, c)])
        return f

    def norm_tile(lsb_bufs, j, ti, xm_view, xm_key):
        off, n, c = TILES[ti]
        sq, rstd, tmp, ps_n = lsb_bufs
        for k in range(8):
            act(P, sq[:, k, 0:n], h[:, k, off:off + n], AF.Square, [("h", k, ti)], [("sq", k)])
        for k in range(8):
            mm(P, ps_n[:, 0:n], ones_bf[:, :], sq[:, k, 0:n], k == 0, k == 7, [("sq", k), ("ones",)], ["ps_n"])
        act(P, rstd[:, 0:n], ps_n[:, 0:n], AF.Sqrt, ["ps_n"], ["rstd"], bias=float(D * EPS))
        P.add("vector", lambda e: e.reciprocal(rstd[:, 0:n], rstd[:, 0:n]), ["rstd"], ["rstd"])
        for k in range(8):
            tt(P, "vector", tmp[:, k % 2, 0:n], h[:, k, off:off + n], rstd[:, 0:n], ALU.mult,
               [("h", k, ti), "rstd"], [("ntmp", k % 2)])
            act(P, xm_view(k), tmp[:, k % 2, 0:n], AF.Identity, [("ntmp", k % 2), ("A32", j, c), ("mods", c)],
                [(xm_key, k, ti)], bias=mods[:, 3 * j, k, c:c + 1], scale=A32[:, j, k, c:c + 1])

    def ph_ffn(l, j, tiles):
        which = 0 if j == 0 else 1

        def f(lsb, lps):
            ntok = sum(TILES[t][1] for t in tiles)
            base = TILES[tiles[0]][0]
            xm = lsb("xm", [128, 8, ntok], BF16)
            hid = lsb("hid", [128, NFC, ntok], BF16)
            sq = lsb("sq", [128, 8, 512], BF16)
            rstd = lsb("rstd", [128, 512])
            tmp = lsb("ntmp", [128, 2, 512])
            wgu = [lsb("wgu0", [128, 2, 8, 128], BF16), lsb("wgu1", [128, 2, 8, 128], BF16)]
            wdb = [lsb("wd0", [128, NFC, 128], BF16), lsb("wd1", [128, NFC, 128], BF16)]
            sg = [lsb("sg0", [128, 512], BF16), lsb("sg1", [128, 512], BF16)]
            ps_n = lps("ps_n")
            ps_g = [lps("ps_g0"), lps("ps_g1")]
            ps_u = [lps("ps_u0"), lps("ps_u1")]
            ps_d = [lps("ps_d0"), lps("ps_d1")]
            for ti in tiles:
                off, n, c = TILES[ti]
                norm_tile((sq, rstd, tmp, ps_n), j, ti,
                          lambda k, off=off, n=n: xm[:, k, off - base:off - base + n], "xm")
            it = 0
            for fc in range(NFC):
                w = wgu[fc % 2]
                dma(P, "gpsimd", w[:, :, :, :], I.wgu[which][l, fc], (), [("wgu", fc % 2, 0), ("wgu", fc % 2, 1)])
                for ti in tiles:
                    off, n, c = TILES[ti]
                    o = off - base
                    b = it % 2
                    it += 1
                    for k in range(8):
                        mm(P, ps_g[b][:, 0:n], w[:, 0, k, :], xm[:, k, o:o + n], k == 0, k == 7,
                           [("wgu", fc % 2, 0), ("xm", k, ti)], [("ps_g", b)])
                    for k in range(8):
                        mm(P, ps_u[b][:, 0:n], w[:, 1, k, :], xm[:, k, o:o + n], k == 0, k == 7,
                           [("wgu", fc % 2, 1), ("xm", k, ti)], [("ps_u", b)])
                    act(P, sg[b][:, 0:n], ps_g[b][:, 0:n], AF.Silu, [("ps_g", b)], [("sg", b)])
                    tt(P, "vector", hid[:, fc, o:o + n], sg[b][:, 0:n], ps_u[b][:, 0:n], ALU.mult,
                       [("sg", b), ("ps_u", b)], [("hid", fc, ti)])
            it = 0
            for dc in range(8):
                w = wdb[dc % 2]
                dma(P, "gpsimd", w[:, :, :], I.wd[which][l, dc], (), [("wd", dc % 2)])
                for ti in tiles:
                    off, n, c = TILES[ti]
                    o = off - base
                    b = it % 2
                    it += 1
                    for fc in range(NFC):
                        mm(P, ps_d[b][:, 0:n], w[:, fc, :], hid[:, fc, o:o + n], fc == 0, fc == NFC - 1,
                           [("wd", dc % 2), ("hid", fc, ti)], [("ps_d", b)])
                    stt(P, "vector", h[:, dc, off:off + n], ps_d[b][:, 0:n], Gm[:, j, dc, c:c + 1],
                        h[:, dc, off:off + n], ALU.mult, ALU.add, [("ps_d", b), ("Gm", j, c), ("h", dc, ti)],
                        [("h", dc, ti)])
        return f


    def ph_proj(l):
        last = l == DEPTH - 1

        def f(lsb, lps):
            Win = lsb("Win", [128, 8, 2304], BF16)
            xm = lsb("xm2", [128, 8, 512], BF16)
            sq = lsb("sq", [128, 8, 512], BF16)
            rstd = lsb("rstd", [128, 512])
            tmp = lsb("ntmp", [128, 2, 512])
            g8 = lsb("g8", [64, 4])
            sqh = [lsb("sqh0", [64, 512], BF16), lsb("sqh1", [64, 512], BF16)]
            rs = [lsb("rs0", [64, 512]), lsb("rs1", [64, 512])]
            qn = [lsb("qn0", [64, 512]), lsb("qn1", [64, 512])]
            t1 = [lsb("t10", [64, 512]), lsb("t11", [64, 512])]
            t2 = [lsb("t20", [64, 512]), lsb("t21", [64, 512])]
            ob = [lsb("ob%d" % i, [64, 512], BF16) for i in range(4)]
            hb = [lsb("hb0", [128, 512]), lsb("hb1", [128, 512])]
            vt = [lsb("vt0", [128, 6, 65], BF16), lsb("vt1", [128, 6, 65], BF16)]
            ps_n = lps("ps_n")
            ps_q = [lps("ps_q0"), lps("ps_q1")]
            ps_s = lps("ps_s")
            ps_r = lps("ps_r")
            ps_h = lps("ps_h")
            ps_v = lps("ps_v")
            for half in range(2):
                dma(P, "gpsimd", Win[:, 4 * half:4 * half + 4, :], I.w_in[l, :, 4 * half:4 * half + 4, :], (),
                    [("Win", half)])
            dma(P, "sync", g8[:, :], I.gqk[l], (), ["g8"])
            ts(P, "vector", g8[:, :], g8[:, :], 8.0, None, ALU.mult, None, ["g8"], ["g8"])
            for b in range(2):
                P.add("gpsimd", lambda e, b=b: e.memset(vt[b][:, :, 64:65], 1.0), (), [("vt1", b)])
            WinK = [("Win", 0), ("Win", 1)]
            it = 0
            for ti in range(5):
                off, n, c = TILES[ti]
                norm_tile((sq, rstd, tmp, ps_n), 1, ti, lambda k, n=n: xm[:, k, 0:n], "xm2")
                xk = [("xm2", k, ti) for k in range(8)]
                for g in range(18):
                    b = it % 2
                    it += 1
                    kind = 0 if g < 8 else (2 if g < 12 else (1 if g < 14 else 3))
                    rope = (c == 0) and kind in (0, 1)
                    for k in range(8):
                        mm(P, ps_q[b][0:64, 0:n], Win[:, k, 64 * g:64 * g + 64], xm[:, k, 0:n], k == 0, k == 7,
                           WinK + [xk[k]], [("ps_q", b)])
                    act(P, sqh[b][:, 0:n], ps_q[b][0:64, 0:n], AF.Square, [("ps_q", b)], [("sqh", b)])
                    mm(P, ps_s[0:64, 0:n], ones_bf[0:64, 0:64], sqh[b][:, 0:n], True, True,
                       [("sqh", b), ("ones",)], ["ps_s"])
                    act(P, rs[b][:, 0:n], ps_s[0:64, 0:n], AF.Sqrt, ["ps_s"], [("rs", b)], bias=float(64 * EPS))
                    P.add("vector", lambda e, b=b, n=n: e.reciprocal(rs[b][:, 0:n], rs[b][:, 0:n]),
                          [("rs", b)], [("rs", b)])
                    o = ob[it % 4]
                    okey = ("ob", it % 4)
                    if rope:
                        stt(P, "vector", qn[b][:, 0:n], ps_q[b][0:64, 0:n], g8[:, kind:kind + 1], rs[b][:, 0:n],
                            ALU.mult, ALU.mult, [("ps_q", b), ("rs", b), "g8"], [("qn", b)])
                        mm(P, ps_r[0:64, 0:n], rotT[:, :], qn[b][:, 0:n], True, True, [("qn", b), "rotT"], ["ps_r"])
                        tt(P, "gpsimd", t1[b][:, 0:n], qn[b][:, 0:n], ropec[:, off:off + n], ALU.mult,
                           [("qn", b), "ropec"], [("t1", b)])
                        tt(P, "vector", t2[b][:, 0:n], ps_r[0:64, 0:n], ropes[:, off:off + n], ALU.mult,
                           ["ps_r", "ropes"], [("t2", b)])
                        tt(P, "gpsimd", o[:, 0:n], t1[b][:, 0:n], t2[b][:, 0:n], ALU.add,
                           [("t1", b), ("t2", b)], [okey])
                    else:
                        stt(P, "vector", o[:, 0:n], ps_q[b][0:64, 0:n], g8[:, kind:kind + 1], rs[b][:, 0:n],
                            ALU.mult, ALU.mult, [("ps_q", b), ("rs", b), "g8"], [okey])
                    if g < 12:
                        dma(P, "sync", S.qT[l][g, :, off:off + n], o[:, 0:n], [okey], [("qT", g, ti)])
                    elif c == 0:
                        kg = g - 12
                        dma(P, "sync", S.kT_in[l][kg // 3][64 * (kg % 3):64 * (kg % 3) + 64, off:off + n], o[:, 0:n],
                            [okey], [("kT_in", kg // 3)])
                    else:
                        dma(P, "sync", S.kTc[l][64 * (g - 12):64 * (g - 11), :], o[:, 0:n], [okey], ["kTc"])
                for pr in range(6):
                    b = pr % 2
                    for k in range(8):
                        mm(P, ps_h[:, 0:n], Win[:, k, 1152 + 128 * pr:1152 + 128 * (pr + 1)], xm[:, k, 0:n],
                           k == 0, k == 7, WinK + [xk[k]], ["ps_h"])
                    cp(P, "scalar", hb[b][:, 0:n], ps_h[:, 0:n], ["ps_h"], [("hb", b)])
                    if c == 0:
                        dma(P, "sync", S.uT_in[l][pr][:, off:off + n], hb[b][:, 0:n], [("hb", b)],
                            [("uT_in", pr)])
                    else:
                        dma(P, "sync", S.uTc[l][128 * pr:128 * (pr + 1), :], hb[b][:, 0:n], [("hb", b)], ["uTc"])
                for s_ in range(n // 128):
                    b = s_ % 2
                    for k in range(8):
                        mm(P, ps_v[:, 0:384], xm[:, k, 128 * s_:128 * (s_ + 1)], Win[:, k, 1920:2304], k == 0, k == 7,
                           WinK + [xk[k]], ["ps_v"])
                    cp(P, "scalar", vt[b][:, :, 0:64], ps_v[:, 0:384].rearrange("p (g f) -> p g f", f=64), ["ps_v"],
                       [("vt", b)])
                    if c == 0:
                        tk = off + 128 * s_
                        dma(P, "sync", S.v_in[l][tk // 1024][tk % 1024:tk % 1024 + 128, :],
                            vt[b][:, :, :].rearrange("p g f -> p (g f)"), [("vt", b), ("vt1", b)],
                            [("v_in", tk // 1024)])
                    else:
                        dma(P, "sync", S.vc[l][128 * s_:128 * (s_ + 1), :],
                            vt[b][:, :, :].rearrange("p g f -> p (g f)"), [("vt", b), ("vt1", b)], ["vc"])
            ags = [(S.kT_in[l][p], S.kT_all[l][p], ("kT_in", p)) for p in range(2)]
            ags += [(S.v_in[l][p], S.v_all[l][p], ("v_in", p)) for p in range(2)]
            ags += [(S.uT_in[l][p], S.uT_all[l][p], ("uT_in", p)) for p in range(6)]
            for (src, dst, key) in ags:
                P.add("gpsimd", lambda e, src=src, dst=dst: e.collective_compute(
                    "AllGather", ALU.bypass, replica_groups=G4, ins=[src.ap().opt()], outs=[dst.ap().opt()]),
                    [key], [("all",) + key], kind="cc")
        return f

    def attn_head(B, qsrc, ydst, n, chunks, extra=None):
        i = B.cnt
        B.cnt += 1
        qb = B.qb[i % 2]
        ps_o = B.ps_o[i % 2]
        dma(P, "sync", qb[:, 0:n], qsrc, (), [("qb", i % 2)])
        nch = len(chunks)
        base = B.it
        B.it += nch
        LA = 2

        def qk(ci):
            kT_ap, v_ap, rds, xf = chunks[ci]
            j = (base + ci) % 3
            mm(P, B.ps_s[j][:, 0:n], kT_ap, qb[:, 0:n], True, xf is None, rds + [("qb", i % 2)], [("ps_s", j)])
            if xf is not None:
                xf(B.ps_s[j][:, 0:n], ("ps_s", j))

        for ci in range(min(LA, nch)):
            qk(ci)
        for ci in range(nch):
            if ci + LA < nch:
                qk(ci + LA)
            kT_ap, v_ap, rds, xf = chunks[ci]
            j = (base + ci) % 3
            act(P, B.pT[j][:, 0:n], B.ps_s[j][:, 0:n], AF.Exp, [("ps_s", j)], [("pT", j)], scale=0.125)
            mm(P, ps_o[0:65, 0:n], v_ap, B.pT[j][:, 0:n], ci == 0, ci == nch - 1, rds + [("pT", j)],
               [("ps_o", i % 2)])
        osb = B.osb[i % 2]
        cp(P, "vector", osb[0:65, 0:n], ps_o[0:65, 0:n], [("ps_o", i % 2)], [("osb", i % 2)])
        P.add("vector", lambda e: e.reciprocal(osb[64:65, 0:n], osb[64:65, 0:n]), [("osb", i % 2)],
              [("osb", i % 2)])
        mm(P, B.ps_b[0:64, 0:n], ones_f[64:65, 0:64], osb[64:65, 0:n], True, True, [("osb", i % 2), ("onesf",)],
           ["ps_b"])
        yb = B.yb[i % 2]
        tt(P, "vector", yb[:, 0:n], osb[0:64, 0:n], B.ps_b[0:64, 0:n], ALU.mult, [("osb", i % 2), "ps_b"],
           [("yb", i % 2)])
        dma(P, "sync", ydst, yb[:, 0:n], [("yb", i % 2)], ())

    def attn_bufs(lsb, lps):
        B = Ctx()
        B.cnt = 0
        B.it = 0
        B.qb = [lsb("qb0", [64, 512], BF16), lsb("qb1", [64, 512], BF16)]
        B.pT = [lsb("pT%d" % i_, [128, 512], BF16) for i_ in range(3)]
        B.osb = [lsb("osb0", [65, 512]), lsb("osb1", [65, 512])]
        B.yb = [lsb("yb0", [64, 512], BF16), lsb("yb1", [64, 512], BF16)]
        B.ps_s = [lps("ps_s%d" % i_) for i_ in range(3)]
        B.ps_o = [lps("ps_o0"), lps("ps_o1")]
        B.ps_b = lps("ps_b")
        return B

    def ph_attn(l):
        last = l == DEPTH - 1

        def f(lsb, lps):
            KT = lsb("KT", [64, 2, SEQ + NC], BF16)
            V = lsb("V", [128, 66, 130], BF16)
            B = attn_bufs(lsb, lps)
            for g in range(2):
                for j in range(4):
                    dma(P, "sync", KT[:, g, NT * j:NT * (j + 1)], S.kT_all[l][0][192 * j + 64 * g:192 * j + 64 * g + 64, :],
                        (), [("KT", g)])
                dma(P, "sync", KT[:, g, SEQ:SEQ + NC], S.kTc[l][64 * g:64 * g + 64, :], (), [("KT", g)])
            for j in range(4):
                for p in range(2):
                    dma(P, "sync", V[:, 16 * j + 8 * p:16 * j + 8 * p + 8, :],
                        S.v_all[l][p][1024 * j:1024 * (j + 1), 0:130].rearrange("(c p) f -> p c f", p=128), (), ["V"])
            dma(P, "sync", V[:, 64:66, :], S.vc[l][:, 0:130].rearrange("(c p) f -> p c f", p=128), (), ["V"])
            for ti in range(5):
                off, n, c = TILES[ti]
                if c == 1 and last:
                    continue
                for hh in range(8):
                    g = hh // 4
                    cl = range(66) if c == 0 else (64, 65)
                    chunks = [(KT[:, g, 128 * ck:128 * (ck + 1)], V[:, ck, 65 * g:65 * g + 65], [("KT", g), "V"], None)
                              for ck in cl]
                    attn_head(B, S.qT[l][hh, :, off:off + n], S.yT[l][hh, :, off:off + n], n, chunks)
        return f


    def ph_na(l):
        last = l == DEPTH - 1

        def f(lsb, lps):
            KTn = lsb("KTn", [64, 4, 4352], BF16)
            Vn = lsb("Vn", [128, 34, 260], BF16)
            stage = lsb("bstage", [128, 8, 512])
            bias8 = [lsb("bias8_0", [128, 8, 512], BF16), lsb("bias8_1", [128, 8, 512], BF16)]
            rmA = lsb("rmA", [8, 44, 128], BF16)
            bq = lsb("bq", [8, 512], BF16)
            B = attn_bufs(lsb, lps)
            dma(P, "gpsimd", rmA[:, :, :], I.rmA[:, :, :], (), ["rmA"])
            dma(P, "gpsimd", bq[:, :], I.bq[:, :], (), ["bq"])
            for hd in range(4):
                kg = 2 + hd
                pc, ro = kg // 3, 64 * (kg % 3)
                dma(P, "sync", KTn[:, hd, 0:NT], S.kT_in[l][pc][ro:ro + 64, :], (), [("KTn", hd)])
                for j in range(4):
                    dma(P, "sync", KTn[:, hd, 2048 + 256 * j:2304 + 256 * j],
                        S.kT_all[l][pc][192 * j + ro:192 * j + ro + 64, 1792:2048], (), [("KTn", hd)])
                    dma(P, "sync", KTn[:, hd, 3072 + 256 * j:3328 + 256 * j],
                        S.kT_all[l][pc][192 * j + ro:192 * j + ro + 64, 0:256], (), [("KTn", hd)])
                dma(P, "sync", KTn[:, hd, 4096:4352], S.kTc[l][64 * kg:64 * kg + 64, :], (), [("KTn", hd)])
            for p in range(2):
                dma(P, "sync", Vn[:, 8 * p:8 * p + 8, :],
                    S.v_in[l][p][:, 130:390].rearrange("(c p) f -> p c f", p=128), (), ["Vn"])
            for j in range(4):
                dma(P, "sync", Vn[:, 16 + 2 * j:18 + 2 * j, :],
                    S.v_all[l][1][1024 * j + 768:1024 * j + 1024, 130:390].rearrange("(c p) f -> p c f", p=128), (),
                    ["Vn"])
                dma(P, "sync", Vn[:, 24 + 2 * j:26 + 2 * j, :],
                    S.v_all[l][0][1024 * j:1024 * j + 256, 130:390].rearrange("(c p) f -> p c f", p=128), (), ["Vn"])
            dma(P, "sync", Vn[:, 32:34, :], S.vc[l][:, 130:390].rearrange("(c p) f -> p c f", p=128), (), ["Vn"])
            for hd in range(4):
                b8 = bias8[hd % 2]
                for c in range(8):
                    for kl in range(2):
                        dp0 = 15 - 2 * c - kl
                        dma(P, "sync", stage[64 * kl:64 * kl + 64, c, :].rearrange("p (r q) -> p r q", q=64),
                            I.tpad[l, hd, dp0:dp0 + 8, :, :].rearrange("r k q -> k r q"), (), ["stage"])
                ts(P, "gpsimd", b8[:, :, :], stage[:, :, :], 8.0, None, ALU.mult, None, ["stage"], [("b8", hd % 2)])
                it = 0
                for ti in range(5):
                    off, n, c_ = TILES[ti]
                    if c_ == 1 and last:
                        continue
                    chunks = []
                    ctxch = [(KTn[:, hd, 4096 + 128 * x:4224 + 128 * x], Vn[:, 32 + x, 65 * hd:65 * hd + 65],
                              [("KTn", hd), "Vn"], None) for x in range(2)]
                    if c_ == 0:
                        for (kind, j, c) in na_iters(ti):
                            if kind == "own":
                                ko = 512 * ti - 256 + 128 * c
                                kT_ap = KTn[:, hd, ko:ko + 128]
                                v_ap = Vn[:, ko // 128, 65 * hd:65 * hd + 65]
                            elif kind == "top":
                                ko = 2048 + 256 * j + 128 * c
                                kT_ap = KTn[:, hd, ko:ko + 128]
                                v_ap = Vn[:, 16 + 2 * j + c, 65 * hd:65 * hd + 65]
                            else:
                                ko = 3072 + 256 * j + 128 * (c - 6)
                                kT_ap = KTn[:, hd, ko:ko + 128]
                                v_ap = Vn[:, 24 + 2 * j + (c - 6), 65 * hd:65 * hd + 65]

                            def xf(ps_ap, pkey, c=c, it=it, b8=b8, hd=hd):
                                mm(P, ps_ap, ident_bf[:, :], b8[:, c, :], False, False, ["ident", ("b8", hd % 2)], [pkey])
                                mm(P, ps_ap, rmA[:, it, :], bq[:, :], False, True, ["rmA", "bq"], [pkey])
                            chunks.append((kT_ap, v_ap, [("KTn", hd), "Vn"], xf))
                            it += 1
                    chunks += ctxch
                    attn_head(B, S.qT[l][8 + hd, :, off:off + n], S.yT[l][12 + hd, :, off:off + n], n, chunks)
        return f


    def conv3(eng, out_ap, acc, w, bcol, n, rkeys, wkey):
        eng = "vector"
        ts(P, eng, out_ap, acc[:, 0:n], w[:, 0:1], bcol, ALU.mult, ALU.add, rkeys, [wkey])
        stt(P, eng, out_ap, acc[:, 1:n + 1], w[:, 1:2], out_ap, ALU.mult, ALU.add, rkeys + [wkey], [wkey])
        stt(P, eng, out_ap, acc[:, 2:n + 2], w[:, 2:3], out_ap, ALU.mult, ALU.add, rkeys + [wkey], [wkey])

    def ph_hypre(l):
        def f(lsb, lps):
            cw = lsb("cw", [64, 3, 3])
            cb = lsb("cb", [64, 3])
            cand = [lsb("cand0", [64, NT + 2]), lsb("cand1", [64, NT + 2])]
            acc = [lsb("acc%d" % i, [64, NT + 2]) for i in range(3)]
            uc = [lsb("uc%d" % i, [64, NT]) for i in range(3)]
            dma(P, "sync", cw[:, :, :], I.convw[l], (), ["cw"])
            dma(P, "sync", cb[:, :], I.convb[l], (), ["cb"])
            ci = 0
            for jb in range(4):
                for s_ in range(3):
                    for g in range(4):
                        cd = cand[ci % 2]
                        ck = ("cand", ci % 2)
                        ci += 1
                        src = S.uT_all[l][2 * s_ + g // 2]
                        r0 = 64 * (g % 2)
                        dma(P, "sync", cd[:, 1:NT + 1], src[128 * jb + r0:128 * jb + r0 + 64, :], (), [ck])
                        if jb > 0:
                            dma(P, "sync", cd[:, 0:1], src[128 * (jb - 1) + r0:128 * (jb - 1) + r0 + 64, NT - 1:NT], (),
                                [ck], slow=True)
                        else:
                            P.add("gpsimd", lambda e, cd=cd: e.memset(cd[:, 0:1], 0.0), (), [ck])
                        if jb < 3:
                            dma(P, "sync", cd[:, NT + 1:NT + 2], src[128 * (jb + 1) + r0:128 * (jb + 1) + r0 + 64, 0:1], (),
                                [ck], slow=True)
                        else:
                            P.add("gpsimd", lambda e, cd=cd: e.memset(cd[:, NT + 1:NT + 2], 0.0), (), [ck])
                        eng = "vector" if g % 2 == 0 else "gpsimd"
                        if g == 0:
                            ts(P, "vector", acc[s_][:, :], cd[:, :], sel[:, 0:1], None, ALU.mult, None, [ck, "sel"],
                               [("acc", s_)])
                        else:
                            stt(P, "vector", acc[s_][:, :], cd[:, :], sel[:, g:g + 1], acc[s_][:, :], ALU.mult, ALU.add,
                                [ck, "sel", ("acc", s_)], [("acc", s_)])
                    conv3("gpsimd" if s_ == 1 else "vector", uc[s_][:, :], acc[s_], cw[:, s_, :], cb[:, s_:s_ + 1], NT,
                          [("acc", s_), "cw", "cb"], ("uc", s_))
                tt(P, "gpsimd", uc[2][:, :], uc[2][:, :], uc[1][:, :], ALU.mult, [("uc", 1), ("uc", 2)], [("uc", 2)])
                dma(P, "sync", S.zT[l][:, NT * jb:NT * (jb + 1)], uc[2][:, :], [("uc", 2)], ["zT"])
                dma(P, "sync", S.x0T[l][:, NT * jb:NT * (jb + 1)], uc[0][:, :], [("uc", 0)], ["x0T"])
        return f

    def ph_hyprec(l):
        def f(lsb, lps):
            cw = lsb("cwc", [64, 12, 3])
            cb = lsb("cbc", [64, 12])
            u = lsb("uc_in", [64, 12, NC + 2])
            uc = lsb("uc_out", [64, 12, NC])
            dma(P, "sync", cw[:, :, :], I.convwc[l], (), ["cw"])
            dma(P, "sync", cb[:, :], I.convbc[l], (), ["cb"])
            P.add("gpsimd", lambda e: e.memset(u[:, :, 0:1], 0.0), (), ["u0"])
            P.add("gpsimd", lambda e: e.memset(u[:, :, NC + 1:NC + 2], 0.0), (), ["u1"])
            dma(P, "sync", u[:, :, 1:NC + 1], S.uTc[l].ap().rearrange("(q c) t -> c q t", c=64), (), ["u"])
            for q in range(12):
                conv3("vector" if q % 2 == 0 else "gpsimd", uc[:, q, :], u[:, q, :], cw[:, q, :], cb[:, q:q + 1], NC,
                      ["u", "u0", "u1", "cw", "cb"], ("ucc", q))
            for g in range(4):
                tt(P, "vector", uc[:, 8 + g, :], uc[:, 8 + g, :], uc[:, 4 + g, :], ALU.mult, [("ucc", 8 + g), ("ucc", 4 + g)],
                   [("ucc", 8 + g)])
                dma(P, "sync", S.zTc[l][g], uc[:, 8 + g, :], [("ucc", 8 + g)], ["zTc"])
                dma(P, "sync", S.x0Tc[l][g], uc[:, g, :], [("ucc", g)], ["x0Tc"])
        return f

    def ph_filt(l, A, groups, ndel, rn, dst):
        n2 = 128 * A
        n = n2 // 2
        TW = min(512, n)
        ntile = n2 // TW

        def f(lsb, lps):
            w1 = lsb("fw1", [33, 64])
            w2 = lsb("fw2", [64, 64])
            fv = lsb("fvec", [64, 3])
            w3 = lsb("fw3", [64, len(groups), 2, 64])
            nd = lsb("ndel", [64, len(groups)])
            ft = [lsb("ft0", [33, TW]), lsb("ft1", [33, TW])]
            tb = [lsb("tb0", [64, TW]), lsb("tb1", [64, TW])]
            pre = [lsb("pre0", [64, TW]), lsb("pre1", [64, TW])]
            h1 = [lsb("h10", [64, TW]), lsb("h11", [64, TW])]
            h2 = [lsb("h20", [64, TW]), lsb("h21", [64, TW])]
            dec = [lsb("dec0", [64, TW]), lsb("dec1", [64, TW])]
            kk = [lsb("kk0", [64, TW]), lsb("kk1", [64, TW])]
            ka = [lsb("ka0", [64, TW]), lsb("ka1", [64, TW])]
            part = lsb("part", [64, len(groups), ntile])
            assert TW <= 512
            ps1 = [lps("psf1_0"), lps("psf1_1")]
            ps2 = [lps("psf2_0"), lps("psf2_1")]
            ps3 = [lps("psf3_0"), lps("psf3_1")]
            pri = [lsb("pri0", [64, TW], mybir.dt.int32), lsb("pri1", [64, TW], mybir.dt.int32)]
            prf = [lsb("prf0", [64, TW]), lsb("prf1", [64, TW])]

            def sin_reduced(ps_ap, pkey, bcol, out_ap, okey, b):
                ts(P, "vector", pre[b][:, :], ps_ap, fv[:, bcol:bcol + 1], f2p[:, 0:1], ALU.add, ALU.mult,
                   [pkey, "f2p", "fv"], [("pre", b)])
                cp(P, "vector", pri[b][:, :], pre[b][:, :], [("pre", b)], [("pri", b)])
                cp(P, "gpsimd", prf[b][:, :], pri[b][:, :], [("pri", b)], [("prf", b)])
                tt(P, "vector", pre[b][:, :], pre[b][:, :], prf[b][:, :], ALU.subtract, [("pre", b), ("prf", b)],
                   [("pre", b)])
                act(P, out_ap, pre[b][:, :], AF.Sin, [("pre", b)], [okey], scale=2.0 * math.pi * (1.0 - 3e-7))

            f2p = lsb("f2p", [64, 1])
            dma(P, "sync", w1[:, :], I.fw1[l], (), ["w1"])
            dma(P, "sync", w2[:, :], I.fw2[l], (), ["w2"])
            dma(P, "sync", fv[:, :], I.fvec[l], (), ["fv"])
            ts(P, "vector", f2p[:, :], fv[:, 2:3], 1.0 / (2.0 * math.pi), None, ALU.mult, None, ["fv"], ["f2p"])
            dma(P, "sync", nd[:, :], ndel, (), ["nd"])
            for gi, (wf, wb) in enumerate(groups):
                dma(P, "sync", w3[:, gi, 0, :], wf, (), ["w3"])
                dma(P, "sync", w3[:, gi, 1, :], wb, (), ["w3"])
            it = 0
            for t in range(ntile):
                b = t % 2
                sl = slice(TW * t, TW * (t + 1))
                dma(P, "sync", ft[b][:, :], I.feats[A][:, sl], (), [("ft", b)])
                dma(P, "sync", tb[b][:, :], I.tlist[A][0:1, sl].partition_broadcast(64), (), [("tb", b)])
                mm(P, ps1[b][0:64, 0:TW], w1[:, :], ft[b][:, :], True, True, ["w1", ("ft", b)], [("ps1", b)])
                sin_reduced(ps1[b][0:64, 0:TW], ("ps1", b), 0, h1[b][:, :], ("h1", b), b)
                mm(P, ps2[b][0:64, 0:TW], w2[:, :], h1[b][:, :], True, True, ["w2", ("h1", b)], [("ps2", b)])
                sin_reduced(ps2[b][0:64, 0:TW], ("ps2", b), 1, h2[b][:, :], ("h2", b), b)
                dirn = 0 if TW * t < n else 1
                for gi in range(len(groups)):
                    c = it % 2
                    it += 1
                    mm(P, ps3[c][0:64, 0:TW], w3[:, gi, dirn, :], h2[b][:, :], True, True, ["w3", ("h2", b)],
                       [("ps3", c)])
                    act(P, dec[c][:, :], tb[b][:, :], AF.Exp, [("tb", b), "nd"], [("dec", c)], scale=nd[:, gi:gi + 1])
                    tt(P, "vector", kk[c][:, :], ps3[c][0:64, 0:TW], dec[c][:, :], ALU.mult, [("ps3", c), ("dec", c)],
                       [("kk", c)])
                    act(P, ka[c][:, :], kk[c][:, :], AF.Abs, [("kk", c)], [("ka", c)])
                    P.add("vector", lambda e, c=c, gi=gi, t=t: e.reduce_sum(part[:, gi, t:t + 1], ka[c][:, :], AX.X),
                          [("ka", c)], [("part", gi)])
                    dma(P, "sync", dst(gi)[:, sl], kk[c][:, :], [("kk", c)], ["kf"])
            for gi in range(len(groups)):
                P.add("vector", lambda e, gi=gi: e.reduce_sum(rn[:, gi:gi + 1], part[:, gi, :], AX.X), [("part", gi)],
                      [("rn", A)])
            P.add("vector", lambda e: e.reciprocal(rn[:, :], rn[:, :]), [("rn", A)], [("rn", A)])
        return f

    def ph_fft(l, A, jobs):
        Ah = A // 2
        NCH = max(j[3] for j in jobs)

        def f(lsb, lps):
            T = {}
            for nm, shp in (("F1v", [A, 4 * A]), ("TwA", [128, 2 * A]), ("TwB", [128, 2 * A]), ("TwTA", [A, 256]),
                            ("TwTB", [A, 256]), ("CA", [A, Ah]), ("SA", [A, Ah])):
                T[nm] = lsb("T" + nm, shp)
                dma(P, "sync", T[nm][:, :], I.ftab[A][nm][:, :], (), ["tab"])
            R4 = lsb("R4", [128, 4, 512])
            dma(P, "sync", R4[:, :, :], I.R4.rearrange("r p n -> p r n"), (), ["tab"])
            kmat = lsb("kmat", [A, NCH, 128])
            Kf = lsb("Kf", [A, NCH, 2, 128])
            Z = lsb("Zb", [Ah, NCH, 128])
            Yo = lsb("Yo", [Ah, NCH, 128])
            t1 = [lsb("ft1_%d" % i, [128, 256]) for i in range(2)]
            t2 = [lsb("ft2_%d" % i, [128, 256]) for i in range(2)]
            Yp = [lsb("Yp%d" % i, [128, 2 * A]) for i in range(2)]
            Pm = [lsb("Pm%d" % i, [A, 256]) for i in range(2)]
            PT = [lsb("PT%d" % i, [128, 2, A]) for i in range(2)]
            Gp = [lsb("Gp%d" % i, [A, 256]) for i in range(2)]
            psY = [lps("psY0"), lps("psY1")]
            psX = [lps("psX0"), lps("psX1")]
            psT = lps("psT")
            psG = [lps("psG0"), lps("psG1")]
            psO = lps("psO")
            def fwdA(i, data_ap, krows, dkeys):
                mm(P, psY[i][:, 0:4 * A], data_ap, T["F1v"][0:krows, :], True, True, dkeys + ["tab"], [("psY", i)])
                tt(P, "vector", t1[i][:, 0:2 * A], psY[i][:, 0:2 * A], T["TwA"][:, :], ALU.mult, [("psY", i), "tab"],
                   [("t1", i)])
                tt(P, "vector", t2[i][:, 0:2 * A], psY[i][:, 2 * A:4 * A], T["TwB"][:, :], ALU.mult, [("psY", i), "tab"],
                   [("t2", i)])
                tt(P, "gpsimd", Yp[i][:, :], t1[i][:, 0:2 * A], t2[i][:, 0:2 * A], ALU.add, [("t1", i), ("t2", i)],
                   [("Yp", i)])

            def fwdB(i):
                mm(P, psX[i][0:A, :], Yp[i][:, 0:A], R4[:, 0, :], True, False, [("Yp", i), "tab"], [("psX", i)])
                mm(P, psX[i][0:A, :], Yp[i][:, A:2 * A], R4[:, 1, :], False, True, [("Yp", i), "tab"], [("psX", i)])

            for (ksrc, zsrc, ydst, nch) in jobs:
                dma(P, "sync", kmat[:, 0:nch, :], ksrc.rearrange("c (a b) -> a c b", b=128), (), ["kmat"])
                dma(P, "sync", Z[:, 0:nch, :], zsrc.rearrange("c (a b) -> a c b", b=128), (), ["Z"])
                assert nch % 2 == 0
                for c0 in range(0, nch, 2):
                    for i in range(2):
                        fwdA(i, kmat[:, c0 + i, :], A, ["kmat"])
                    for i in range(2):
                        fwdB(i)
                    for i in range(2):
                        cp(P, "scalar", Kf[:, c0 + i, :, :], psX[i][0:A, 0:256].rearrange("p (r e) -> p r e", r=2),
                           [("psX", i)], [("Kf", c0 + i)])
                for c0 in range(0, nch, 2):
                    for i in range(2):
                        fwdA(i, Z[:, c0 + i, :], Ah, ["Z"])
                    for i in range(2):
                        fwdB(i)
                    for i in range(2):
                        ch = c0 + i
                        xv = psX[i][0:A, :]
                        tt(P, "vector", t1[i][0:A, :].rearrange("p (r e) -> p r e", r=2),
                           xv[:, 0:256].rearrange("p (r e) -> p r e", r=2),
                           Kf[:, ch, 0, :].unsqueeze(1).broadcast_to([A, 2, 128]), ALU.mult, [("psX", i), ("Kf", ch)],
                           [("t1", i)])
                        tt(P, "vector", t2[i][0:A, :].rearrange("p (r e) -> p r e", r=2),
                           xv[:, 256:512].rearrange("p (r e) -> p r e", r=2),
                           Kf[:, ch, 1, :].unsqueeze(1).broadcast_to([A, 2, 128]), ALU.mult, [("psX", i), ("Kf", ch)],
                           [("t2", i)])
                        tt(P, "gpsimd", Pm[i][:, :], t1[i][0:A, :], t2[i][0:A, :], ALU.add, [("t1", i), ("t2", i)],
                           [("Pm", i)])
                    for i in range(2):
                        for r in range(2):
                            P.add("tensor", lambda e, i=i, r=r: e.transpose(
                                psT[:, (2 * i + r) * A:(2 * i + r + 1) * A], Pm[i][:, 128 * r:128 * (r + 1)],
                                ident_f[0:A, 0:A]), [("Pm", i), "identf"], [("psT", i)])
                        cp(P, "scalar", PT[i][:, :, :],
                           psT[:, 2 * i * A:(2 * i + 2) * A].rearrange("p (r c) -> p r c", r=2), [("psT", i)],
                           [("PT", i)])
                    for i in range(2):
                        mm(P, psG[i][0:A, :], PT[i][:, 0, :], R4[:, 2, :], True, False, [("PT", i), "tab"], [("psG", i)])
                        mm(P, psG[i][0:A, :], PT[i][:, 1, :], R4[:, 3, :], False, True, [("PT", i), "tab"], [("psG", i)])
                    for i in range(2):
                        tt(P, "vector", t1[i][0:A, :], psG[i][0:A, 0:256], T["TwTA"][:, :], ALU.mult,
                           [("psG", i), "tab"], [("t1", i)])
                        tt(P, "vector", t2[i][0:A, :], psG[i][0:A, 256:512], T["TwTB"][:, :], ALU.mult,
                           [("psG", i), "tab"], [("t2", i)])
                        tt(P, "gpsimd", Gp[i][:, :], t1[i][0:A, :], t2[i][0:A, :], ALU.subtract, [("t1", i), ("t2", i)],
                           [("Gp", i)])
                    for i in range(2):
                        po = psO[0:Ah, 128 * i:128 * (i + 1)]
                        mm(P, po, T["CA"][:, :], Gp[i][:, 0:128], True, False, [("Gp", i), "tab"], [("psO", i)])
                        mm(P, po, T["SA"][:, :], Gp[i][:, 128:256], False, True, [("Gp", i), "tab"], [("psO", i)])
                        cp(P, "scalar", Yo[:, c0 + i, :], po, [("psO", i)], ["Yo"])
                dma(P, "sync", ydst.rearrange("c (a b) -> a c b", b=128), Yo[:, 0:nch, :], ["Yo"], ["ycv"])
        return f

    def ph_hypost(l):
        def f(lsb, lps):
            sk = lsb("sk", [64, 1])
            yc = [lsb("yc0", [64, NT]), lsb("yc1", [64, NT])]
            zb = [lsb("zb0", [64, NT]), lsb("zb1", [64, NT])]
            xb = [lsb("xb0", [64, NT]), lsb("xb1", [64, NT])]
            ob = [lsb("yob0", [64, NT], BF16), lsb("yob1", [64, NT], BF16)]
            cd = [lsb("ycd0", [64, NT], BF16), lsb("ycd1", [64, NT], BF16)]
            ac = [lsb("yac0", [64, NT], BF16), lsb("yac1", [64, NT], BF16)]
            dma(P, "sync", sk[:, :], I.skip_m[l], (), ["sk"])
            for jb in range(4):
                b = jb % 2
                sl = slice(NT * jb, NT * (jb + 1))
                dma(P, "sync", yc[b][:, :], S.ycv[l][:, sl], (), [("yc", b)])
                dma(P, "sync", zb[b][:, :], S.zT[l][:, sl], (), [("zb", b)])
                dma(P, "sync", xb[b][:, :], S.x0T[l][:, sl], (), [("xb", b)])
                ts(P, "vector", yc[b][:, :], yc[b][:, :], rn_m[:, 0:1], None, ALU.mult, None, [("yc", b), ("rn", 128)],
                   [("yc", b)])
                stt(P, "vector", yc[b][:, :], zb[b][:, :], sk[:, 0:1], yc[b][:, :], ALU.mult, ALU.add,
                    [("zb", b), "sk", ("yc", b)], [("yc", b)])
                tt(P, "gpsimd", ob[b][:, :], yc[b][:, :], xb[b][:, :], ALU.mult, [("yc", b), ("xb", b)], [("ob", b)])
                dma(P, "sync", S.yb_in[l][:, sl], ob[b][:, :], [("ob", b)], ["yb_in"])
            P.add("gpsimd", lambda e: e.collective_compute(
                "AllGather", ALU.bypass, replica_groups=G4, ins=[S.yb_in[l].ap().opt()], outs=[S.yb_all[l].ap().opt()]),
                ["yb_in"], ["yb_all"], kind="cc")
            ci = 0
            for g in range(4):
                a = ac[g % 2]
                for j in range(4):
                    c = cd[ci % 2]
                    ck = ("ycd", ci % 2)
                    ci += 1
                    dma(P, "sync", c[:, :], S.yb_all[l][64 * g:64 * g + 64, NT * j:NT * (j + 1)], ["yb_all"], [ck])
                    if j == 0:
                        ts(P, "vector", a[:, :], c[:, :], sel[:, 0:1], None, ALU.mult, None, [ck, "sel"], [("yac", g % 2)])
                    else:
                        stt(P, "vector", a[:, :], c[:, :], sel[:, j:j + 1], a[:, :], ALU.mult, ALU.add,
                            [ck, "sel", ("yac", g % 2)], [("yac", g % 2)])
                dma(P, "sync", S.yT[l][8 + g, :, 0:NT], a[:, :], [("yac", g % 2)], ())
        return f

    def ph_hypostc(l):
        def f(lsb, lps):
            sk = lsb("skc", [64, 4])
            yc = lsb("ycc", [64, 4, NC])
            zb = lsb("zbc", [64, 4, NC])
            xb = lsb("xbc", [64, 4, NC])
            ob = lsb("obc", [64, 4, NC], BF16)
            dma(P, "sync", sk[:, :], I.skip_c[l], (), ["sk"])
            dma(P, "sync", yc[:, :, :], S.ycvc[l].ap().rearrange("g c t -> c g t"), (), ["yc"])
            dma(P, "sync", zb[:, :, :], S.zTc[l].ap().rearrange("g c t -> c g t"), (), ["zb"])
            dma(P, "sync", xb[:, :, :], S.x0Tc[l].ap().rearrange("g c t -> c g t"), (), ["xb"])
            for g in range(4):
                ts(P, "vector", yc[:, g, :], yc[:, g, :], rn_c[:, g:g + 1], None, ALU.mult, None, ["yc", ("rn", 4)],
                   [("ycg", g)])
                stt(P, "vector", yc[:, g, :], zb[:, g, :], sk[:, g:g + 1], yc[:, g, :], ALU.mult, ALU.add,
                    ["zb", "sk", ("ycg", g)], [("ycg", g)])
                tt(P, "vector", ob[:, g, :], yc[:, g, :], xb[:, g, :], ALU.mult, [("ycg", g), "xb"], [("obc", g)])
                dma(P, "sync", S.yT[l][8 + g, :, NT:NTOT], ob[:, g, :], [("obc", g)], ())
        return f


    def ph_hyprec1(l):
        def f(lsb, lps):
            cw = lsb("cwc", [64, 3, 3])
            cb = lsb("cbc", [64, 3])
            u = lsb("uc_in", [64, 12, NC + 2])
            us = lsb("uc_sel", [64, 3, NC + 2])
            uc = lsb("uc_out", [64, 3, NC])
            dma(P, "sync", cw[:, :, :], I.convw[l], (), ["cw"])
            dma(P, "sync", cb[:, :], I.convb[l], (), ["cb"])
            P.add("gpsimd", lambda e: e.memset(u[:, :, 0:1], 0.0), (), ["u0"])
            P.add("gpsimd", lambda e: e.memset(u[:, :, NC + 1:NC + 2], 0.0), (), ["u1"])
            dma(P, "sync", u[:, :, 1:NC + 1], S.uTc[l].ap().rearrange("(q c) t -> c q t", c=64), (), ["u"])
            for s_ in range(3):
                for g in range(4):
                    if g == 0:
                        ts(P, "vector", us[:, s_, :], u[:, 4 * s_, :], sel[:, 0:1], None, ALU.mult, None,
                           ["u", "u0", "u1", "sel"], [("us", s_)])
                    else:
                        stt(P, "vector", us[:, s_, :], u[:, 4 * s_ + g, :], sel[:, g:g + 1], us[:, s_, :], ALU.mult,
                            ALU.add, ["u", "u0", "u1", "sel", ("us", s_)], [("us", s_)])
                conv3("vector", uc[:, s_, :], us[:, s_, :], cw[:, s_, :], cb[:, s_:s_ + 1], NC,
                      [("us", s_), "cw", "cb"], ("ucc", s_))
            tt(P, "vector", uc[:, 2, :], uc[:, 2, :], uc[:, 1, :], ALU.mult, [("ucc", 2), ("ucc", 1)], [("ucc", 2)])
            dma(P, "sync", S.zTc1[l][:, :], uc[:, 2, :], [("ucc", 2)], ["zTc"])
            dma(P, "sync", S.x0Tc1[l][:, :], uc[:, 0, :], [("ucc", 0)], ["x0Tc"])
        return f

    def ph_hypostc1(l):
        def f(lsb, lps):
            sk = lsb("skc", [64, 1])
            yc = lsb("ycc", [64, NC])
            zb = lsb("zbc", [64, NC])
            xb = lsb("xbc", [64, NC])
            ob = lsb("obc", [64, NC], BF16)
            dma(P, "sync", sk[:, :], I.skip_m[l], (), ["sk"])
            dma(P, "sync", yc[:, :], S.ycvc1[l][:, :], (), ["yc"])
            dma(P, "sync", zb[:, :], S.zTc1[l][:, :], (), ["zb"])
            dma(P, "sync", xb[:, :], S.x0Tc1[l][:, :], (), ["xb"])
            ts(P, "vector", yc[:, :], yc[:, :], rn_c[:, 0:1], None, ALU.mult, None, ["yc", ("rn", 4)], ["yc"])
            stt(P, "vector", yc[:, :], zb[:, :], sk[:, 0:1], yc[:, :], ALU.mult, ALU.add, ["zb", "sk", "yc"], ["yc"])
            tt(P, "vector", ob[:, :], yc[:, :], xb[:, :], ALU.mult, ["yc", "xb"], ["obc"])
            dma(P, "sync", S.ybc_in[l][:, :], ob[:, :], ["obc"], ["ybc_in"])
            P.add("gpsimd", lambda e: e.collective_compute(
                "AllGather", ALU.bypass, replica_groups=G4, ins=[S.ybc_in[l].ap().opt()],
                outs=[S.ybc_all[l].ap().opt()]), ["ybc_in"], ["ybc_all"], kind="cc")
            for g in range(4):
                dma(P, "sync", S.yT[l][8 + g, :, NT:NTOT], S.ybc_all[l][64 * g:64 * g + 64, :], ["ybc_all"], ())
        return f

    def ph_oproj(l, zero_groups=()):
        last = l == DEPTH - 1

        def f(lsb, lps):
            Wo = lsb("Wo", [64, 16, D], BF16)
            Y = [lsb("Y0", [64, 16, 512], BF16), lsb("Y1", [64, 16, 512], BF16)]
            ps = [lps("ps_op0"), lps("ps_op1")]
            dma(P, "gpsimd", Wo[:, :, :], I.w_out[l], (), ["Wo"])
            it = 0
            for ti in range(5):
                off, n, c = TILES[ti]
                if c == 1 and last:
                    continue
                y = Y[ti % 2]
                dma(P, "sync", y[:, :, 0:n], S.yT[l][:, :, off:off + n].rearrange("g f t -> f g t"), (),
                    [("Y", ti % 2)])
                for g in zero_groups:
                    P.add("gpsimd", lambda e, y=y, g=g, n=n: e.memset(y[:, g, 0:n], 0.0), [("Y", ti % 2)],
                          [("Y", ti % 2)])
                for dc in range(8):
                    b = it % 2
                    it += 1
                    for g in range(16):
                        mm(P, ps[b][:, 0:n], Wo[:, g, 128 * dc:128 * (dc + 1)], y[:, g, 0:n], g == 0, g == 15,
                           ["Wo", ("Y", ti % 2)], [("ps_op", b)])
                    stt(P, "vector", h[:, dc, off:off + n], ps[b][:, 0:n], Gm[:, 1, dc, c:c + 1],
                        h[:, dc, off:off + n], ALU.mult, ALU.add, [("ps_op", b), ("Gm", 1, c), ("h", dc, ti)],
                        [("h", dc, ti)])
        return f

    def ph_out(lsb, lps):
        for k in range(8):
            dma(P, "sync", out_hT[128 * k:128 * (k + 1), :], h[:, k, 0:NT], [("h", k, i) for i in range(4)], ())
            if DEBUG_OUT:
                dma(P, "sync", dbg[128 * k:128 * (k + 1), :], h[:, k, :], [("h", k, i) for i in range(5)], ())

    for l in range(DEPTH):
        phase(ph_adaln(l))
        phase(ph_ffn(l, 0, [0, 1]))
        phase(ph_ffn(l, 0, [2, 3, 4]))
        if stop_after == ("ffn1", l):
            break
        phase(ph_proj(l))
        phase(ph_attn(l))
        if stop_after == ("attn", l):
            phase(ph_oproj(l, zero_groups=range(8, 16)))
            break
        phase(ph_na(l))
        if stop_after == ("na", l):
            phase(ph_oproj(l, zero_groups=range(8, 12)))
            break
        phase(ph_hypre(l))
        phase(ph_filt(l, 128, [(I.w3m[l, :, 0, :], I.w3m[l, :, 1, :])], I.ndel_m[:, :], rn_m, lambda g: S.kfT[l]))
        phase(ph_fft(l, 128, [(S.kfT[l][32 * hf:32 * hf + 32, :], S.zT[l][32 * hf:32 * hf + 32, :],
                               S.ycv[l][32 * hf:32 * hf + 32, :], 32) for hf in range(2)]))
        phase(ph_hypost(l))
        if l < DEPTH - 1:
            phase(ph_hyprec1(l))
            phase(ph_filt(l, 4, [(I.w3m[l, :, 0, :], I.w3m[l, :, 1, :])], I.ndel_m[:, :], rn_c, lambda g: S.kfTc1[l]))
            phase(ph_fft(l, 4, [(S.kfTc1[l][32 * hf:32 * hf + 32, :], S.zTc1[l][32 * hf:32 * hf + 32, :],
                                 S.ycvc1[l][32 * hf:32 * hf + 32, :], 32) for hf in range(2)]))
            phase(ph_hypostc1(l))
        phase(ph_oproj(l))
        if stop_after == ("mix", l):
            break
        phase(ph_ffn(l, 2, [0, 1]))
        phase(ph_ffn(l, 2, [2, 3, 4] if l < DEPTH - 1 else [2, 3]))
    phase(ph_out)
    P.close()
    for cm in reversed(cms):
        cm.__exit__(None, None, None)
    return nc


def fft_tables(A):
    N = 128 * A
    a = np.arange(A)[:, None].astype(np.float64)
    c = np.arange(A)[None, :].astype(np.float64)
    FrA = np.cos(2 * np.pi * a * c / A)
    FiA = -np.sin(2 * np.pi * a * c / A)
    b = np.arange(128)[:, None].astype(np.float64)
    Twr = np.cos(2 * np.pi * b * c / N)
    Twi = -np.sin(2 * np.pi * b * c / N)
    a2 = np.arange(A // 2)[None, :].astype(np.float64)
    cp_ = np.arange(A)[:, None].astype(np.float64)
    f = lambda x: np.ascontiguousarray(x.astype(np.float32))
    return dict(F1v=f(np.concatenate([FrA, FiA, -FiA, FrA], 1)), TwA=f(np.concatenate([Twr, Twr], 1)),
                TwB=f(np.concatenate([Twi, Twi], 1)), TwTA=f(np.concatenate([Twr.T, Twr.T], 1)),
                TwTB=f(np.concatenate([Twi.T, Twi.T], 1)), CA=f(np.cos(2 * np.pi * cp_ * a2 / A) / N),
                SA=f(-np.sin(2 * np.pi * cp_ * a2 / A) / N))


def filter_tables(n):
    p = np.arange(2 * n)
    pos = np.where(p < n, p, 2 * n - p).astype(np.float32)
    pos[n] = 0.0
    t = (pos / np.float32(max(n - 1, 1))).astype(np.float32)
    bands = np.linspace(1e-4, 15, 16, dtype=np.float32)
    ang = (np.float32(2.0 * math.pi / n) * pos[:, None] * bands[None, :]).astype(np.float32)
    feats = np.concatenate([t[:, None], np.cos(ang), -np.sin(ang)], axis=-1).astype(np.float32)
    tl = t.copy()
    tl[n] = 1.0e4
    return np.ascontiguousarray(feats.T), np.ascontiguousarray(tl[None, :])


def hyena_deltas():
    return np.abs(np.linspace(math.log(1e-2) / 1.5, math.log(1e-2) / 0.3, 256, dtype=np.float32))


def prep_inputs(inputs):
    x = np.asarray(inputs["x"], np.float32)
    ctx = np.asarray(inputs["ctx"], np.float32)
    c = np.asarray(inputs["c"], np.float32)
    c_ctx = np.asarray(inputs["c_ctx"], np.float32)
    shared = {}
    f32 = lambda k: np.asarray(inputs[k], np.float32)
    shared["w_ada_p"] = np.ascontiguousarray(f32("w_ada").reshape(DEPTH, 8, 128, 9, 1024).transpose(0, 3, 2, 1, 4))
    ffn_names = {1: ("w_ffn1_gate", "w_ffn1_up", "w_ffn1_down"), 2: ("w_ffn2_gate", "w_ffn2_up", "w_ffn2_down")}
    for i in (1, 2):
        g_ = f32(ffn_names[i][0]).reshape(DEPTH, 8, 128, NFC, 128)
        u_ = f32(ffn_names[i][1]).reshape(DEPTH, 8, 128, NFC, 128)
        gu = np.stack([g_, u_], axis=0)
        shared["w_ffn%d_gu" % i] = np.ascontiguousarray(gu.transpose(1, 4, 3, 0, 2, 5))
        d_ = f32(ffn_names[i][2]).reshape(DEPTH, NFC, 128, 8, 128)
        shared["w_ffn%d_dn" % i] = np.ascontiguousarray(d_.transpose(0, 3, 2, 1, 4))
    g3 = np.stack([inputs["g_ffn1"], inputs["g_mix"], inputs["g_ffn2"]], axis=1).astype(np.float32)
    shared["g3T"] = np.ascontiguousarray(g3.reshape(DEPTH, 3, 8, 128).transpose(0, 3, 1, 2))
    shared["b_adaT"] = np.ascontiguousarray(
        np.asarray(inputs["b_ada"], np.float32).reshape(DEPTH, 72, 128).transpose(0, 2, 1))
    w_in = np.asarray(inputs["w_in"], np.float32)
    perm = np.concatenate([np.arange(0, 512), np.arange(1536, 1792), np.arange(512, 640), np.arange(1792, 2048),
                           np.arange(768, 1536), np.arange(640, 768), np.arange(2048, 2304)])
    shared["w_in_p"] = np.ascontiguousarray(w_in[:, :, perm].reshape(DEPTH, 8, 128, 2304).transpose(0, 2, 1, 3))
    shared["w_out_p"] = np.ascontiguousarray(f32("w_out").reshape(DEPTH, 16, 64, D).transpose(0, 2, 1, 3))
    shared["gqkT"] = np.ascontiguousarray(np.stack(
        [inputs["g_q_attn"], inputs["g_k_attn"], inputs["g_q_na"], inputs["g_k_na"]], axis=-1).astype(np.float32))
    rot = np.zeros((64, 64), np.float32)
    for i in range(64):
        if i % 32 < 16:
            rot[i, i + 16] = -1.0
        else:
            rot[i, i - 16] = 1.0
    shared["rotT"] = np.ascontiguousarray(rot.T)
    rpb = np.asarray(inputs["na_rpb"], np.float32)
    kc = np.arange(64)[:, None]
    qc = np.arange(64)[None, :]
    cs = np.clip(qc - 8, 0, 48)
    colvalid = (kc >= cs) & (kc < cs + 16)
    cidx = np.clip(kc - qc + 15, 0, 30)
    tpad = np.full((DEPTH, 4, 23, 64, 64), -BIG, np.float32)
    for dp in range(23):
        dr = 18 - dp
        if 0 <= dr <= 14:
            tpad[:, :, dp] = np.where(colvalid[None, None], rpb[:, :, dr][:, :, cidx], np.float32(-BIG))
    shared["na_tpad"] = tpad
    shared["na_bq"] = np.ascontiguousarray((np.arange(512)[None, :] // 64 == np.arange(8)[:, None]).astype(np.float32))
    shared["ident"] = np.eye(128, dtype=np.float32)
    for A_, n_ in ((128, SEQ), (4, NC)):
        for k_, v_ in fft_tables(A_).items():
            shared["%s_%d" % (k_, A_)] = v_
        ft_, tl_ = filter_tables(n_)
        shared["featsT_" + ("m" if A_ == 128 else "c")] = ft_
        shared["tlist_" + ("m" if A_ == 128 else "c")] = tl_
    bb = np.arange(128)[:, None].astype(np.float64)
    ee = np.arange(128)[None, :].astype(np.float64)
    Cm = np.cos(2 * np.pi * bb * ee / 128)
    Sm = np.sin(2 * np.pi * bb * ee / 128)
    shared["fftR4"] = np.ascontiguousarray(np.stack([
        np.concatenate([Cm, -Sm, Sm, Cm], 1), np.concatenate([Sm, Cm, -Cm, Sm], 1),
        np.concatenate([Cm, Sm, -Sm, Cm], 1), np.concatenate([-Sm, Cm, -Cm, -Sm], 1)]).astype(np.float32))
    conv_w = np.asarray(inputs["conv_w"], np.float32)
    conv_b = np.asarray(inputs["conv_b"], np.float32)
    cw4 = conv_w.reshape(DEPTH, 3, 3, 4, 64)
    cb4 = conv_b.reshape(DEPTH, 3, 4, 64)
    shared["hyc_convw"] = np.ascontiguousarray(cw4.transpose(0, 4, 2, 3, 1).reshape(DEPTH, 64, 12, 3))
    shared["hyc_convb"] = np.ascontiguousarray(cb4.transpose(0, 3, 1, 2).reshape(DEPTH, 64, 12))
    shared["fw1"] = np.ascontiguousarray(np.asarray(inputs["filt_w1"], np.float32))
    shared["fw2"] = np.ascontiguousarray(np.asarray(inputs["filt_w2"], np.float32))
    shared["fvec"] = np.ascontiguousarray(np.stack(
        [inputs["filt_b1"], inputs["filt_b2"], inputs["filt_freq"]], axis=-1).astype(np.float32))
    w3 = np.asarray(inputs["filt_w3"], np.float32)
    shared["w3c"] = np.ascontiguousarray(w3)
    skip = np.asarray(inputs["hyena_skip"], np.float32)
    shared["skip_c"] = np.ascontiguousarray(skip.reshape(DEPTH, 4, 64).transpose(0, 2, 1))
    dl = hyena_deltas()
    shared["ndel_c"] = np.ascontiguousarray((-dl).reshape(4, 64).T)
    inv = (10000.0 ** (-np.arange(16, dtype=np.float32) / 16)).astype(np.float32)
    maps = []
    for r in range(8):
        b, qd = r // 4, r % 4
        m = dict(shared)
        m["xT"] = np.ascontiguousarray(x[b, NT * qd:NT * (qd + 1), :].T)
        m["ctxT"] = np.ascontiguousarray(ctx[b].T)
        cc = np.stack([c[b], c_ctx], axis=-1)
        m["cT"] = np.ascontiguousarray(cc.reshape(8, 128, 2).transpose(1, 0, 2))
        sl_ = np.zeros((64, 4), np.float32)
        sl_[:, qd] = 1.0
        m["sel"] = sl_
        m["hy_convw"] = np.ascontiguousarray(cw4[:, :, :, qd, :].transpose(0, 3, 2, 1))
        m["hy_convb"] = np.ascontiguousarray(cb4[:, :, qd, :].transpose(0, 2, 1))
        m["w3m"] = np.ascontiguousarray(w3.reshape(DEPTH, 64, 2, 4, 64)[:, :, :, qd, :])
        m["ndel_m"] = np.ascontiguousarray(-dl[64 * qd:64 * qd + 64, None])
        m["skip_m"] = np.ascontiguousarray(skip[:, 64 * qd:64 * qd + 64, None])
        rm = np.zeros((8, 44, 128), np.float32)
        it = 0
        for ti in range(4):
            for (kind, j, cch) in na_iters(ti):
                for qrl in range(8):
                    qr = 32 * qd + 8 * ti + qrl
                    rs = min(max(qr - 4, 0), 120)
                    for kl in range(2):
                        kr = 32 * qd + 8 * ti - 4 + 2 * cch + kl
                        ok = rs <= kr < rs + 8
                        if kind == "top":
                            ok = ok and (j == qd - 1)
                        elif kind == "bot":
                            ok = ok and (j == qd + 1)
                        if not ok:
                            rm[qrl, it, 64 * kl:64 * kl + 64] = -8.0 * BIG
                it += 1
        assert it == 44
        m["na_rmA"] = rm
        t = np.arange(NT * qd, NT * (qd + 1))
        rows = (t // GRID_W).astype(np.float32)
        cols = (t % GRID_W).astype(np.float32)
        ang = np.concatenate([rows[None, :] * inv[:, None]] * 2 + [cols[None, :] * inv[:, None]] * 2, axis=0)
        m["ropecos"] = np.ascontiguousarray(np.cos(ang).astype(np.float32))
        m["ropesin"] = np.ascontiguousarray(np.sin(ang).astype(np.float32))
        maps.append(m)
    return maps


def kernel(**inputs):
    nc = build_nc()
    maps = prep_inputs(inputs)
    res = run_bass_kernel_spmd(nc, maps, core_ids=list(range(8)))
    out = np.zeros((2, SEQ, D), np.float32)
    for r in range(8):
        b, qd = r // 4, r % 4
        out[b, NT * qd:NT * (qd + 1), :] = res.results[r]["out_hT"].T
    return out
```

```python
import math
import numpy as np
import ml_dtypes
import concourse.bass as bass
import concourse.mybir as mybir
from concourse.bass_utils import run_bass_kernel_spmd

F32 = mybir.dt.float32
BF16 = mybir.dt.bfloat16
AF = mybir.ActivationFunctionType
ALU = mybir.AluOpType
AX = mybir.AxisListType

D = 1024
DEPTH = 4
SEQ = 8192
NT = 2048
NC = 256
NTOT = NT + NC
FF = 2816
NFC = FF // 128
GRID_W = 64
EPS = 1e-6
BIG = 30000.0
G4 = [[0, 1, 2, 3], [4, 5, 6, 7]]

DEBUG_OUT = False
ENGS = ("tensor", "vector", "scalar", "gpsimd", "sync")


class Op:
    __slots__ = ("eng", "fn", "kind", "deps", "needed", "sem", "val", "idx", "prev")

    def __init__(self, eng, fn, kind):
        self.eng = eng
        self.fn = fn
        self.kind = kind
        self.deps = []
        self.needed = False
        self.sem = None
        self.val = None


class Prog:
    def __init__(self, nc, n_dma_sems=8):
        self.nc = nc
        self.stack = []
        self.csem = {}
        self.ccount = {}
        self.csem_pool = {}
        for e in ENGS:
            self.csem[e] = self._newsem("c_" + e)
            self.ccount[e] = 0
        self.dsem = {}
        self.dval = {}
        self.dnext = {}
        for e in ("sync", "gpsimd", "scalar"):
            self.dsem[e] = [self._newsem("d_%s%d" % (e, i)) for i in range(n_dma_sems)]
            self.dval[e] = [0] * n_dma_sems
            self.dnext[e] = 0
        self.ccsem = self._newsem("cc")
        self.ccval = 0
        self.known = {e: {} for e in ENGS}
        self.reset_phase()

    def _newsem(self, name):
        cm = self.nc.semaphore(name)
        s = cm.__enter__()
        self.stack.append(cm)
        return s

    def reset_phase(self):
        self.ops = {e: [] for e in ENGS}
        self.last_w = {}
        self.readers = {}

    def add(self, eng, fn, reads=(), writes=(), kind="c"):
        op = Op(eng, fn, kind)
        deps = {}
        for k in reads:
            w = self.last_w.get(k)
            if w is not None:
                deps[id(w)] = w
        for k in writes:
            w = self.last_w.get(k)
            if w is not None:
                deps[id(w)] = w
            for r in self.readers.get(k, ()):
                deps[id(r)] = r
        for d in deps.values():
            if d is op:
                continue
            if d.eng == eng and d.kind == "c" and kind == "c" and eng == "tensor":
                continue
            op.deps.append(d)
            d.needed = True
        for k in reads:
            self.readers.setdefault(k, []).append(op)
        for k in writes:
            self.last_w[k] = op
            self.readers[k] = []
        self.ops[eng].append(op)
        return op

    def emit(self, block):
        for e in ENGS:
            for op in self.ops[e]:
                if op.kind == "c":
                    if op.needed:
                        self.ccount[e] += 1
                        op.sem, op.val = self.csem[e], self.ccount[e]
                elif op.kind == "d":
                    i = self.dnext[e]
                    self.dnext[e] = (i + 1) % len(self.dsem[e])
                    op.idx = i
                    op.sem = self.dsem[e][i]
                    op.prev = self.dval[e][i]
                    self.dval[e][i] += 16
                    op.val = self.dval[e][i]
                elif op.kind == "cc":
                    self.ccval += 1
                    op.sem, op.val = self.ccsem, self.ccval
        prog = self

        def make(e):
            def body(engine):
                known = prog.known[e]

                def wait(sem, val):
                    key = id(sem)
                    if known.get(key, 0) >= val:
                        return
                    known[key] = val
                    engine.wait_ge(sem, val)

                for op in prog.ops[e]:
                    for d in op.deps:
                        wait(d.sem, d.val)
                    if op.kind == "d" and op.prev > 0:
                        wait(op.sem, op.prev)
                    ins = op.fn(engine)
                    if op.kind == "c":
                        if op.needed:
                            ins.then_inc(op.sem, 1)
                    elif op.kind == "d":
                        ins.then_inc(op.sem, 16)
                    else:
                        ins.then_inc(op.sem)
                if e in prog.dsem:
                    for i, s in enumerate(prog.dsem[e]):
                        if prog.dval[e][i] > 0:
                            wait(s, prog.dval[e][i])
                if e == "gpsimd" and prog.ccval > 0:
                    wait(prog.ccsem, prog.ccval)
            return body

        for e in ENGS:
            getattr(block, e)(make(e))
        self.reset_phase()

    def close(self):
        for cm in reversed(self.stack):
            cm.__exit__(None, None, None)


class Ctx:
    pass


def mm(P, out, lhsT, rhs, start, stop, reads, writes):
    return P.add("tensor", lambda e: e.matmul(out, lhsT, rhs, start=start, stop=stop), reads, writes)


def dma(P, q, out, in_, reads, writes, slow=False):
    if slow:
        return P.add(q, lambda e: e.dma_start(out=out, in_=in_, allow_slow_non_contiguous=True), reads, writes,
                     kind="d")
    return P.add(q, lambda e: e.dma_start(out=out, in_=in_), reads, writes, kind="d")


def act(P, out, in_, func, reads, writes, bias=0.0, scale=1.0):
    return P.add("scalar", lambda e: e.activation(out, in_, func, bias=bias, scale=scale), reads, writes)


def tt(P, eng, out, in0, in1, op, reads, writes):
    return P.add(eng, lambda e: e.tensor_tensor(out, in0, in1, op), reads, writes)


def ts(P, eng, out, in0, s1, s2, op0, op1, reads, writes):
    if op1 is None:
        return P.add(eng, lambda e: e.tensor_scalar(out, in0, s1, None, op0), reads, writes)
    return P.add(eng, lambda e: e.tensor_scalar(out, in0, s1, s2, op0, op1), reads, writes)


def stt(P, eng, out, in0, scalar, in1, op0, op1, reads, writes):
    return P.add(eng, lambda e: e.scalar_tensor_tensor(out, in0, scalar, in1, op0, op1), reads, writes)


def cp(P, eng, out, in_, reads, writes):
    if eng == "scalar":
        return P.add(eng, lambda e: e.copy(out, in_), reads, writes)
    return P.add(eng, lambda e: e.tensor_copy(out, in_), reads, writes)


def na_iters(ti):
    out = []
    for c in range(8):
        lr = 8 * ti - 4 + 2 * c
        if lr < 0:
            out += [("top", j, c) for j in range(4)]
        elif lr >= 32:
            out += [("bot", j, c) for j in range(4)]
        else:
            out.append(("own", 0, c))
    return out


TILES = [(0, 512, 0), (512, 512, 0), (1024, 512, 0), (1536, 512, 0), (2048, 256, 1)]


def build_nc(stop_after=None):
    nc = bass.Bass("TRN2", target_bir_lowering=False)
    C = Ctx()
    C.nc = nc

    def din(name, shape, dt=F32):
        return nc.dram_tensor(name, list(shape), dt, kind="ExternalInput").ap()

    def dscr(name, shape, dt=F32):
        return nc.dram_tensor(name, list(shape), dt)

    I = Ctx()
    I.xT = din("xT", [D, NT])
    I.ctxT = din("ctxT", [D, NC])
    I.cT = din("cT", [128, 8, 2])
    I.w_ada = din("w_ada_p", [DEPTH, 9, 128, 8, 1024])
    I.b_ada = din("b_adaT", [DEPTH, 128, 72])
    I.g3 = din("g3T", [DEPTH, 128, 3, 8])
    I.wgu = [din("w_ffn1_gu", [DEPTH, NFC, 128, 2, 8, 128]), din("w_ffn2_gu", [DEPTH, NFC, 128, 2, 8, 128])]
    I.wd = [din("w_ffn1_dn", [DEPTH, 8, 128, NFC, 128]), din("w_ffn2_dn", [DEPTH, 8, 128, NFC, 128])]
    I.w_in = din("w_in_p", [DEPTH, 128, 8, 2304])
    I.w_out = din("w_out_p", [DEPTH, 64, 16, D])
    I.gqk = din("gqkT", [DEPTH, 64, 4])
    I.rotT = din("rotT", [64, 64])
    I.cos = din("ropecos", [64, NT])
    I.sin = din("ropesin", [64, NT])
    I.tpad = din("na_tpad", [DEPTH, 4, 23, 64, 64])
    I.rmA = din("na_rmA", [8, 44, 128])
    I.bq = din("na_bq", [8, 512])
    I.ident = din("ident", [128, 128])
    I.sel = din("sel", [64, 4])
    I.convw = din("hy_convw", [DEPTH, 64, 3, 3])
    I.convb = din("hy_convb", [DEPTH, 64, 3])
    I.convwc = din("hyc_convw", [DEPTH, 64, 12, 3])
    I.convbc = din("hyc_convb", [DEPTH, 64, 12])
    I.fw1 = din("fw1", [DEPTH, 33, 64])
    I.fw2 = din("fw2", [DEPTH, 64, 64])
    I.fvec = din("fvec", [DEPTH, 64, 3])
    I.w3m = din("w3m", [DEPTH, 64, 2, 64])
    I.w3c = din("w3c", [DEPTH, 64, 512])
    I.ndel_m = din("ndel_m", [64, 1])
    I.ndel_c = din("ndel_c", [64, 4])
    I.skip_m = din("skip_m", [DEPTH, 64, 1])
    I.skip_c = din("skip_c", [DEPTH, 64, 4])
    I.feats = {128: din("featsT_m", [33, 2 * SEQ]), 4: din("featsT_c", [33, 2 * NC])}
    I.tlist = {128: din("tlist_m", [1, 2 * SEQ]), 4: din("tlist_c", [1, 2 * NC])}
    I.ftab = {}
    for A_ in (128, 4):
        I.ftab[A_] = dict(F1v=din("F1v_%d" % A_, [A_, 4 * A_]), TwA=din("TwA_%d" % A_, [128, 2 * A_]),
                          TwB=din("TwB_%d" % A_, [128, 2 * A_]), TwTA=din("TwTA_%d" % A_, [A_, 256]),
                          TwTB=din("TwTB_%d" % A_, [A_, 256]), CA=din("CA_%d" % A_, [A_, A_ // 2]),
                          SA=din("SA_%d" % A_, [A_, A_ // 2]))
    I.R4 = din("fftR4", [4, 128, 512])
    S = Ctx()
    S.zT = [dscr("zT%d" % l, [64, SEQ]) for l in range(DEPTH)]
    S.x0T = [dscr("x0T%d" % l, [64, SEQ]) for l in range(DEPTH)]
    S.kfT = [dscr("kfT%d" % l, [64, 2 * SEQ]) for l in range(DEPTH)]
    S.ycv = [dscr("ycv%d" % l, [64, SEQ]) for l in range(DEPTH)]
    S.yb_in = [dscr("yb_in%d" % l, [64, SEQ], BF16) for l in range(DEPTH)]
    S.yb_all = [dscr("yb_all%d" % l, [256, SEQ], BF16) for l in range(DEPTH)]
    S.zTc1 = [dscr("zTc1_%d" % l, [64, NC]) for l in range(DEPTH)]
    S.x0Tc1 = [dscr("x0Tc1_%d" % l, [64, NC]) for l in range(DEPTH)]
    S.kfTc1 = [dscr("kfTc1_%d" % l, [64, 2 * NC]) for l in range(DEPTH)]
    S.ycvc1 = [dscr("ycvc1_%d" % l, [64, NC]) for l in range(DEPTH)]
    S.ybc_in = [dscr("ybc_in%d" % l, [64, NC], BF16) for l in range(DEPTH)]
    S.ybc_all = [dscr("ybc_all%d" % l, [256, NC], BF16) for l in range(DEPTH)]
    S.zTc = [dscr("zTc%d" % l, [4, 64, NC]) for l in range(DEPTH)]
    S.x0Tc = [dscr("x0Tc%d" % l, [4, 64, NC]) for l in range(DEPTH)]
    S.kfTc = [dscr("kfTc%d" % l, [4, 64, 2 * NC]) for l in range(DEPTH)]
    S.ycvc = [dscr("ycvc%d" % l, [4, 64, NC]) for l in range(DEPTH)]
    S.qT = [dscr("qT_scr%d" % l, [12, 64, NTOT], BF16) for l in range(DEPTH)]
    S.kT_in = [[dscr("kT_in%d_%d" % (l, p), [192, NT], BF16) for p in range(2)] for l in range(DEPTH)]
    S.kT_all = [[dscr("kT_all%d_%d" % (l, p), [4 * 192, NT], BF16) for p in range(2)] for l in range(DEPTH)]
    S.v_in = [[dscr("v_in%d_%d" % (l, p), [1024, 390], BF16) for p in range(2)] for l in range(DEPTH)]
    S.v_all = [[dscr("v_all%d_%d" % (l, p), [4 * 1024, 390], BF16) for p in range(2)] for l in range(DEPTH)]
    S.uT_in = [[dscr("uT_in%d_%d" % (l, p), [128, NT], F32) for p in range(6)] for l in range(DEPTH)]
    S.uT_all = [[dscr("uT_all%d_%d" % (l, p), [4 * 128, NT], F32) for p in range(6)] for l in range(DEPTH)]
    S.kTc = [dscr("kTc%d" % l, [384, NC], BF16) for l in range(DEPTH)]
    S.vc = [dscr("vc%d" % l, [NC, 390], BF16) for l in range(DEPTH)]
    S.uTc = [dscr("uTc%d" % l, [768, NC], F32) for l in range(DEPTH)]
    S.yT = [dscr("yT_scr%d" % l, [16, 64, NTOT], BF16) for l in range(DEPTH)]
    out_hT = nc.dram_tensor("out_hT", [D, NT], F32, kind="ExternalOutput").ap()
    dbg = nc.dram_tensor("dbg", [D, NTOT], F32, kind="ExternalOutput").ap() if DEBUG_OUT else None

    P = Prog(nc)
    cms = []

    def sb(name, shape, dt=F32):
        cm = nc.sbuf_tensor(name, list(shape), dt)
        t = cm.__enter__()
        cms.append(cm)
        return t

    h = sb("h", [128, 8, NTOT])
    ones_bf = sb("ones_bf", [128, 128], BF16)
    mods = sb("mods", [128, 9, 8, 2])
    A32 = sb("A32", [128, 3, 8, 2])
    Gm = sb("Gm", [128, 3, 8, 2])
    ones_f = sb("ones_f", [128, 128])
    ident_bf = sb("ident_bf", [128, 128], BF16)
    ident_f = sb("ident_f", [128, 128])
    rn_m = sb("rn_m", [64, 1])
    rn_c = sb("rn_c", [64, 1])
    sel = sb("sel_sb", [64, 4])
    nbias = sb("nbias", [128, 1])
    rotT = sb("rotT_sb", [64, 64])
    ropec = sb("ropec", [64, NT])
    ropes = sb("ropes", [64, NT])

    def phase(fn):
        local = []
        C.uid = getattr(C, "uid", 0) + 1
        u = "_%d" % C.uid

        def lsb(name, shape, dt=F32):
            cm = nc.sbuf_tensor(name + u, list(shape), dt)
            t = cm.__enter__()
            local.append(cm)
            return t

        def lps(name, shape=(128, 512), dt=F32):
            cm = nc.psum_tensor(name + u, list(shape), dt)
            t = cm.__enter__()
            local.append(cm)
            return t

        with nc.Block() as block:
            fn(lsb, lps)
            P.emit(block)
        for cm in reversed(local):
            cm.__exit__(None, None, None)

    def ph_init(lsb, lps):
        P.add("vector", lambda e: e.memset(ones_bf[:, :], 1.0), (), [("ones",)])
        P.add("vector", lambda e: e.memset(ones_f[:, :], 1.0), (), [("onesf",)])
        P.add("vector", lambda e: e.memset(nbias[:, :], -math.pi), (), ["nbias"])
        dma(P, "sync", rotT[:, :], I.rotT[:, :], (), ["rotT"])
        dma(P, "gpsimd", ident_bf[:, :], I.ident[:, :], (), ["ident"])
        dma(P, "sync", ident_f[:, :], I.ident[:, :], (), ["identf"])
        dma(P, "sync", sel[:, :], I.sel[:, :], (), ["sel"])
        dma(P, "sync", ropec[:, :], I.cos[:, :], (), ["ropec"])
        dma(P, "sync", ropes[:, :], I.sin[:, :], (), ["ropes"])
        for k in range(8):
            dma(P, "sync", h[:, k, 0:NT], I.xT[128 * k:128 * (k + 1), :], (), [("h", k, i) for i in range(4)])
            dma(P, "sync", h[:, k, NT:NTOT], I.ctxT[128 * k:128 * (k + 1), :], (), [("h", k, 4)])

    phase(ph_init)

    def ph_adaln(l):
        def f(lsb, lps):
            cts = lsb("cts", [128, 8, 2])
            cbf = lsb("cbf", [128, 8, 2], BF16)
            wp = [lsb("wp0", [128, 8, 1024], BF16), lsb("wp1", [128, 8, 1024], BF16)]
            bias = lsb("bias", [128, 72])
            g32 = lsb("g32", [128, 3, 8])
            ps = lps("ps_ada", [128, 9, 8, 2])
            dma(P, "sync", cts[:, :, :], I.cT[:, :, :], (), ["cts"])
            dma(P, "sync", bias[:, :], I.b_ada[l], (), ["bias"])
            dma(P, "sync", g32[:, :, :], I.g3[l], (), ["g32"])
            act(P, cbf[:, :, :], cts[:, :, :], AF.Silu, ["cts"], ["cbf"])
            for m in range(9):
                w = wp[m % 2]
                dma(P, "gpsimd", w[:, :, :], I.w_ada[l, m], (), [("wp", m % 2)])
                for dc in range(8):
                    for k in range(8):
                        mm(P, ps[:, m, dc, :], w[:, k, 128 * dc:128 * (dc + 1)], cbf[:, k, :], k == 0, k == 7,
                           [("wp", m % 2), "cbf"], [("psada", m)])
            bv = bias[:, :].rearrange("p (m k) -> p m k", k=8)
            for c in range(2):
                tt(P, "vector", mods[:, :, :, c], ps[:, :, :, c], bv, ALU.add,
                   [("psada", m) for m in range(9)] + ["bias"], [("mods", c)])
            ts(P, "vector", g32[:, :, :], g32[:, :, :], 32.0, None, ALU.mult, None, ["g32"], ["g32"])
            for j in range(3):
                for c in range(2):
                    stt(P, "vector", A32[:, j, :, c], mods[:, 3 * j + 1, :, c], 1.0, g32[:, j, :], ALU.add, ALU.mult,
                        [("mods", c), "g32"], [("A32", j, c)])
                    ts(P, "vector", Gm[:, j, :, c], mods[:, 3 * j + 2, :, c], (1.0 if j == 1 else 0.5), None, ALU.mult,
                       None, [("mods", c)], [("Gm", j, c)])
        return f

    def norm_tile(lsb_bufs, j, ti, xm_view, xm_key):
        off, n, c = TILES[ti]
        sq, rstd, tmp, ps_n = lsb_bufs
        for k in range(8):
            act(P, sq[:, k, 0:n], h[:, k, off:off + n], AF.Square, [("h", k, ti)], [("sq", k)])
        for k in range(8):
            mm(P, ps_n[:, 0:n], ones_bf[:, :], sq[:, k, 0:n], k == 0, k == 7, [("sq", k), ("ones",)], ["ps_n"])
        act(P, rstd[:, 0:n], ps_n[:, 0:n], AF.Sqrt, ["ps_n"], ["rstd"], bias=float(D * EPS))
        P.add("vector", lambda e: e.reciprocal(rstd[:, 0:n], rstd[:, 0:n]), ["rstd"], ["rstd"])
        for k in range(8):
            tt(P, "vector", tmp[:, k % 2, 0:n], h[:, k, off:off + n], rstd[:, 0:n], ALU.mult,
               [("h", k, ti), "rstd"], [("ntmp", k % 2)])
            act(P, xm_view(k), tmp[:, k % 2, 0:n], AF.Identity, [("ntmp", k % 2), ("A32", j, c), ("mods", c)],
                [(xm_key, k, ti)], bias=mods[:, 3 * j, k, c:c + 1], scale=A32[:, j, k, c:c + 1])

    def ph_ffn(l, j, tiles):
        which = 0 if j == 0 else 1

        def f(lsb, lps):
            ntok = sum(TILES[t][1] for t in tiles)
            base = TILES[tiles[0]][0]
            xm = lsb("xm", [128, 8, ntok], BF16)
            hid = lsb("hid", [128, NFC, ntok], BF16)
            sq = lsb("sq", [128, 8, 512], BF16)
            rstd = lsb("rstd", [128, 512])
            tmp = lsb("ntmp", [128, 2, 512])
            wgu = [lsb("wgu0", [128, 2, 8, 128], BF16), lsb("wgu1", [128, 2, 8, 128], BF16)]
            wdb = [lsb("wd0", [128, NFC, 128], BF16), lsb("wd1", [128, NFC, 128], BF16)]
            sg = [lsb("sg0", [128, 512], BF16), lsb("sg1", [128, 512], BF16)]
            ps_n = lps("ps_n")
            ps_g = [lps("ps_g0"), lps("ps_g1")]
            ps_u = [lps("ps_u0"), lps("ps_u1")]
            ps_d = [lps("ps_d0"), lps("ps_d1")]
            for ti in tiles:
                off, n, c = TILES[ti]
                norm_tile((sq, rstd, tmp, ps_n), j, ti,
                          lambda k, off=off, n=n: xm[:, k, off - base:off - base + n], "xm")
            it = 0
            for fc in range(NFC):
                w = wgu[fc % 2]
                dma(P, "gpsimd", w[:, :, :, :], I.wgu[which][l, fc], (), [("wgu", fc % 2, 0), ("wgu", fc % 2, 1)])
                for ti in tiles:
                    off, n, c = TILES[ti]
                    o = off - base
                    b = it % 2
                    it += 1
                    for k in range(8):
                        mm(P, ps_g[b][:, 0:n], w[:, 0, k, :], xm[:, k, o:o + n], k == 0, k == 7,
                           [("wgu", fc % 2, 0), ("xm", k, ti)], [("ps_g", b)])
                    for k in range(8):
                        mm(P, ps_u[b][:, 0:n], w[:, 1, k, :], xm[:, k, o:o + n], k == 0, k == 7,
                           [("wgu", fc % 2, 1), ("xm", k, ti)], [("ps_u", b)])
                    act(P, sg[b][:, 0:n], ps_g[b][:, 0:n], AF.Silu, [("ps_g", b)], [("sg", b)])
                    tt(P, "vector", hid[:, fc, o:o + n], sg[b][:, 0:n], ps_u[b][:, 0:n], ALU.mult,
                       [("sg", b), ("ps_u", b)], [("hid", fc, ti)])
            it = 0
            for dc in range(8):
                w = wdb[dc % 2]
                dma(P, "gpsimd", w[:, :, :], I.wd[which][l, dc], (), [("wd", dc % 2)])
                for ti in tiles:
                    off, n, c = TILES[ti]
                    o = off - base
                    b = it % 2
                    it += 1
                    for fc in range(NFC):
                        mm(P, ps_d[b][:, 0:n], w[:, fc, :], hid[:, fc, o:o + n], fc == 0, fc == NFC - 1,
                           [("wd", dc % 2), ("hid", fc, ti)], [("ps_d", b)])
                    stt(P, "vector", h[:, dc, off:off + n], ps_d[b][:, 0:n], Gm[:, j, dc, c:c + 1],
                        h[:, dc, off:off + n], ALU.mult, ALU.add, [("ps_d", b), ("Gm", j, c), ("h", dc, ti)],
                        [("h", dc, ti)])
        return f


    def ph_proj(l):
        last = l == DEPTH - 1

        def f(lsb, lps):
            Win = lsb("Win", [128, 8, 2304], BF16)
            xm = lsb("xm2", [128, 8, 512], BF16)
            sq = lsb("sq", [128, 8, 512], BF16)
            rstd = lsb("rstd", [128, 512])
            tmp = lsb("ntmp", [128, 2, 512])
            g8 = lsb("g8", [64, 4])
            sqh = [lsb("sqh0", [64, 512], BF16), lsb("sqh1", [64, 512], BF16)]
            rs = [lsb("rs0", [64, 512]), lsb("rs1", [64, 512])]
            qn = [lsb("qn0", [64, 512]), lsb("qn1", [64, 512])]
            t1 = [lsb("t10", [64, 512]), lsb("t11", [64, 512])]
            t2 = [lsb("t20", [64, 512]), lsb("t21", [64, 512])]
            ob = [lsb("ob%d" % i, [64, 512], BF16) for i in range(4)]
            hb = [lsb("hb0", [128, 512]), lsb("hb1", [128, 512])]
            vt = [lsb("vt0", [128, 6, 65], BF16), lsb("vt1", [128, 6, 65], BF16)]
            ps_n = lps("ps_n")
            ps_q = [lps("ps_q0"), lps("ps_q1")]
            ps_s = lps("ps_s")
            ps_r = lps("ps_r")
            ps_h = lps("ps_h")
            ps_v = lps("ps_v")
            for half in range(2):
                dma(P, "gpsimd", Win[:, 4 * half:4 * half + 4, :], I.w_in[l, :, 4 * half:4 * half + 4, :], (),
                    [("Win", half)])
            dma(P, "sync", g8[:, :], I.gqk[l], (), ["g8"])
            ts(P, "vector", g8[:, :], g8[:, :], 8.0, None, ALU.mult, None, ["g8"], ["g8"])
            for b in range(2):
                P.add("gpsimd", lambda e, b=b: e.memset(vt[b][:, :, 64:65], 1.0), (), [("vt1", b)])
            WinK = [("Win", 0), ("Win", 1)]
            it = 0
            for ti in range(5):
                off, n, c = TILES[ti]
                norm_tile((sq, rstd, tmp, ps_n), 1, ti, lambda k, n=n: xm[:, k, 0:n], "xm2")
                xk = [("xm2", k, ti) for k in range(8)]
                for g in range(18):
                    b = it % 2
                    it += 1
                    kind = 0 if g < 8 else (2 if g < 12 else (1 if g < 14 else 3))
                    rope = (c == 0) and kind in (0, 1)
                    for k in range(8):
                        mm(P, ps_q[b][0:64, 0:n], Win[:, k, 64 * g:64 * g + 64], xm[:, k, 0:n], k == 0, k == 7,
                           WinK + [xk[k]], [("ps_q", b)])
                    act(P, sqh[b][:, 0:n], ps_q[b][0:64, 0:n], AF.Square, [("ps_q", b)], [("sqh", b)])
                    mm(P, ps_s[0:64, 0:n], ones_bf[0:64, 0:64], sqh[b][:, 0:n], True, True,
                       [("sqh", b), ("ones",)], ["ps_s"])
                    act(P, rs[b][:, 0:n], ps_s[0:64, 0:n], AF.Sqrt, ["ps_s"], [("rs", b)], bias=float(64 * EPS))
                    P.add("vector", lambda e, b=b, n=n: e.reciprocal(rs[b][:, 0:n], rs[b][:, 0:n]),
                          [("rs", b)], [("rs", b)])
                    o = ob[it % 4]
                    okey = ("ob", it % 4)
                    if rope:
                        stt(P, "vector", qn[b][:, 0:n], ps_q[b][0:64, 0:n], g8[:, kind:kind + 1], rs[b][:, 0:n],
                            ALU.mult, ALU.mult, [("ps_q", b), ("rs", b), "g8"], [("qn", b)])
                        mm(P, ps_r[0:64, 0:n], rotT[:, :], qn[b][:, 0:n], True, True, [("qn", b), "rotT"], ["ps_r"])
                        tt(P, "gpsimd", t1[b][:, 0:n], qn[b][:, 0:n], ropec[:, off:off + n], ALU.mult,
                           [("qn", b), "ropec"], [("t1", b)])
                        tt(P, "vector", t2[b][:, 0:n], ps_r[0:64, 0:n], ropes[:, off:off + n], ALU.mult,
                           ["ps_r", "ropes"], [("t2", b)])
                        tt(P, "gpsimd", o[:, 0:n], t1[b][:, 0:n], t2[b][:, 0:n], ALU.add,
                           [("t1", b), ("t2", b)], [okey])
                    else:
                        stt(P, "vector", o[:, 0:n], ps_q[b][0:64, 0:n], g8[:, kind:kind + 1], rs[b][:, 0:n],
                            ALU.mult, ALU.mult, [("ps_q", b), ("rs", b), "g8"], [okey])
                    if g < 12:
                        dma(P, "sync", S.qT[l][g, :, off:off + n], o[:, 0:n], [okey], [("qT", g, ti)])
                    elif c == 0:
                        kg = g - 12
                        dma(P, "sync", S.kT_in[l][kg // 3][64 * (kg % 3):64 * (kg % 3) + 64, off:off + n], o[:, 0:n],
                            [okey], [("kT_in", kg // 3)])
                    else:
                        dma(P, "sync", S.kTc[l][64 * (g - 12):64 * (g - 11), :], o[:, 0:n], [okey], ["kTc"])
                for pr in range(6):
                    b = pr % 2
                    for k in range(8):
                        mm(P, ps_h[:, 0:n], Win[:, k, 1152 + 128 * pr:1152 + 128 * (pr + 1)], xm[:, k, 0:n],
                           k == 0, k == 7, WinK + [xk[k]], ["ps_h"])
                    cp(P, "scalar", hb[b][:, 0:n], ps_h[:, 0:n], ["ps_h"], [("hb", b)])
                    if c == 0:
                        dma(P, "sync", S.uT_in[l][pr][:, off:off + n], hb[b][:, 0:n], [("hb", b)],
                            [("uT_in", pr)])
                    else:
                        dma(P, "sync", S.uTc[l][128 * pr:128 * (pr + 1), :], hb[b][:, 0:n], [("hb", b)], ["uTc"])
                for s_ in range(n // 128):
                    b = s_ % 2
                    for k in range(8):
                        mm(P, ps_v[:, 0:384], xm[:, k, 128 * s_:128 * (s_ + 1)], Win[:, k, 1920:2304], k == 0, k == 7,
                           WinK + [xk[k]], ["ps_v"])
                    cp(P, "scalar", vt[b][:, :, 0:64], ps_v[:, 0:384].rearrange("p (g f) -> p g f", f=64), ["ps_v"],
                       [("vt", b)])
                    if c == 0:
                        tk = off + 128 * s_
                        dma(P, "sync", S.v_in[l][tk // 1024][tk % 1024:tk % 1024 + 128, :],
                            vt[b][:, :, :].rearrange("p g f -> p (g f)"), [("vt", b), ("vt1", b)],
                            [("v_in", tk // 1024)])
                    else:
                        dma(P, "sync", S.vc[l][128 * s_:128 * (s_ + 1), :],
                            vt[b][:, :, :].rearrange("p g f -> p (g f)"), [("vt", b), ("vt1", b)], ["vc"])
            ags = [(S.kT_in[l][p], S.kT_all[l][p], ("kT_in", p)) for p in range(2)]
            ags += [(S.v_in[l][p], S.v_all[l][p], ("v_in", p)) for p in range(2)]
            ags += [(S.uT_in[l][p], S.uT_all[l][p], ("uT_in", p)) for p in range(6)]
            for (src, dst, key) in ags:
                P.add("gpsimd", lambda e, src=src, dst=dst: e.collective_compute(
                    "AllGather", ALU.bypass, replica_groups=G4, ins=[src.ap().opt()], outs=[dst.ap().opt()]),
                    [key], [("all",) + key], kind="cc")
        return f

    def attn_head(B, qsrc, ydst, n, chunks, extra=None):
        i = B.cnt
        B.cnt += 1
        qb = B.qb[i % 2]
        ps_o = B.ps_o[i % 2]
        dma(P, "sync", qb[:, 0:n], qsrc, (), [("qb", i % 2)])
        nch = len(chunks)
        base = B.it
        B.it += nch
        LA = 2

        def qk(ci):
            kT_ap, v_ap, rds, xf = chunks[ci]
            j = (base + ci) % 3
            mm(P, B.ps_s[j][:, 0:n], kT_ap, qb[:, 0:n], True, xf is None, rds + [("qb", i % 2)], [("ps_s", j)])
            if xf is not None:
                xf(B.ps_s[j][:, 0:n], ("ps_s", j))

        for ci in range(min(LA, nch)):
            qk(ci)
        for ci in range(nch):
            if ci + LA < nch:
                qk(ci + LA)
            kT_ap, v_ap, rds, xf = chunks[ci]
            j = (base + ci) % 3
            act(P, B.pT[j][:, 0:n], B.ps_s[j][:, 0:n], AF.Exp, [("ps_s", j)], [("pT", j)], scale=0.125)
            mm(P, ps_o[0:65, 0:n], v_ap, B.pT[j][:, 0:n], ci == 0, ci == nch - 1, rds + [("pT", j)],
               [("ps_o", i % 2)])
        osb = B.osb[i % 2]
        cp(P, "vector", osb[0:65, 0:n], ps_o[0:65, 0:n], [("ps_o", i % 2)], [("osb", i % 2)])
        P.add("vector", lambda e: e.reciprocal(osb[64:65, 0:n], osb[64:65, 0:n]), [("osb", i % 2)],
              [("osb", i % 2)])
        mm(P, B.ps_b[0:64, 0:n], ones_f[64:65, 0:64], osb[64:65, 0:n], True, True, [("osb", i % 2), ("onesf",)],
           ["ps_b"])
        yb = B.yb[i % 2]
        tt(P, "vector", yb[:, 0:n], osb[0:64, 0:n], B.ps_b[0:64, 0:n], ALU.mult, [("osb", i % 2), "ps_b"],
           [("yb", i % 2)])
        dma(P, "sync", ydst, yb[:, 0:n], [("yb", i % 2)], ())

    def attn_bufs(lsb, lps):
        B = Ctx()
        B.cnt = 0
        B.it = 0
        B.qb = [lsb("qb0", [64, 512], BF16), lsb("qb1", [64, 512], BF16)]
        B.pT = [lsb("pT%d" % i_, [128, 512], BF16) for i_ in range(3)]
        B.osb = [lsb("osb0", [65, 512]), lsb("osb1", [65, 512])]
        B.yb = [lsb("yb0", [64, 512], BF16), lsb("yb1", [64, 512], BF16)]
        B.ps_s = [lps("ps_s%d" % i_) for i_ in range(3)]
        B.ps_o = [lps("ps_o0"), lps("ps_o1")]
        B.ps_b = lps("ps_b")
        return B

    def ph_attn(l):
        last = l == DEPTH - 1

        def f(lsb, lps):
            KT = lsb("KT", [128, SEQ + NC], BF16)
            V = lsb("V", [128, 66, 130], BF16)
            qb = [lsb("qbp0", [128, 512], BF16), lsb("qbp1", [128, 512], BF16)]
            pT = [[lsb("pT%d_%d" % (s_, j), [128, 512], BF16) for j in range(2)] for s_ in range(2)]
            osb = [lsb("osb0", [65, 512]), lsb("osb1", [65, 512])]
            yb = [lsb("yb0", [64, 512], BF16), lsb("yb1", [64, 512], BF16)]
            ps_s = [[lps("ps_s%d_%d" % (s_, j)) for j in range(2)] for s_ in range(2)]
            ps_o = [lps("ps_o0"), lps("ps_o1")]
            ps_b = lps("ps_b")
            for g in range(2):
                for j in range(4):
                    dma(P, "sync", KT[64 * g:64 * g + 64, NT * j:NT * (j + 1)],
                        S.kT_all[l][0][192 * j + 64 * g:192 * j + 64 * g + 64, :], (), [("KT", g)])
                dma(P, "sync", KT[64 * g:64 * g + 64, SEQ:SEQ + NC], S.kTc[l][64 * g:64 * g + 64, :], (), [("KT", g)])
            for j in range(4):
                for p in range(2):
                    dma(P, "sync", V[:, 16 * j + 8 * p:16 * j + 8 * p + 8, :],
                        S.v_all[l][p][1024 * j:1024 * (j + 1), 0:130].rearrange("(c p) f -> p c f", p=128), (), ["V"])
            dma(P, "sync", V[:, 64:66, :], S.vc[l][:, 0:130].rearrange("(c p) f -> p c f", p=128), (), ["V"])
            cnt = 0
            it = 0
            for ti in range(5):
                off, n, c = TILES[ti]
                if c == 1 and last:
                    continue
                cl = list(range(66)) if c == 0 else [64, 65]
                nch = len(cl)
                for hA in range(4):
                    q = qb[cnt % 2]
                    qk_ = ("qbp", cnt % 2)
                    cnt += 1
                    for s_ in range(2):
                        dma(P, "sync", q[64 * s_:64 * s_ + 64, 0:n], S.qT[l][hA + 4 * s_, :, off:off + n], (), [qk_])
                    base = it
                    it += nch

                    def qk(ci):
                        j = (base + ci) % 2
                        ck = cl[ci]
                        for s_ in range(2):
                            mm(P, ps_s[s_][j][:, 0:n], KT[64 * s_:64 * s_ + 64, 128 * ck:128 * (ck + 1)],
                               q[64 * s_:64 * s_ + 64, 0:n], True, True, [("KT", s_), qk_], [("ps_s", s_, j)])

                    qk(0)
                    for ci in range(nch):
                        if ci + 1 < nch:
                            qk(ci + 1)
                        j = (base + ci) % 2
                        ck = cl[ci]
                        for s_ in range(2):
                            act(P, pT[s_][j][:, 0:n], ps_s[s_][j][:, 0:n], AF.Exp, [("ps_s", s_, j)], [("pT", s_, j)],
                                scale=0.125)
                        for s_ in range(2):
                            mm(P, ps_o[s_][0:65, 0:n], V[:, ck, 65 * s_:65 * s_ + 65], pT[s_][j][:, 0:n], ci == 0,
                               ci == nch - 1, ["V", ("pT", s_, j)], [("ps_o", s_)])
                    for s_ in range(2):
                        hh = hA + 4 * s_
                        cp(P, "vector", osb[s_][0:65, 0:n], ps_o[s_][0:65, 0:n], [("ps_o", s_)], [("osb", s_)])
                        P.add("vector", lambda e, s_=s_, n=n: e.reciprocal(osb[s_][64:65, 0:n], osb[s_][64:65, 0:n]),
                              [("osb", s_)], [("osb", s_)])
                        mm(P, ps_b[0:64, 0:n], ones_f[64:65, 0:64], osb[s_][64:65, 0:n], True, True,
                           [("osb", s_), ("onesf",)], ["ps_b"])
                        tt(P, "vector", yb[s_][:, 0:n], osb[s_][0:64, 0:n], ps_b[0:64, 0:n], ALU.mult,
                           [("osb", s_), "ps_b"], [("yb", s_)])
                        dma(P, "sync", S.yT[l][hh, :, off:off + n], yb[s_][:, 0:n], [("yb", s_)], ())
        return f

    def ph_na(l):
        last = l == DEPTH - 1

        def f(lsb, lps):
            KTn = lsb("KTn", [64, 4, 4352], BF16)
            Vn = lsb("Vn", [128, 34, 260], BF16)
            stage = lsb("bstage", [128, 8, 512])
            bias8 = [lsb("bias8_0", [128, 8, 512], BF16), lsb("bias8_1", [128, 8, 512], BF16)]
            rmA = lsb("rmA", [8, 44, 128], BF16)
            bq = lsb("bq", [8, 512], BF16)
            B = attn_bufs(lsb, lps)
            dma(P, "gpsimd", rmA[:, :, :], I.rmA[:, :, :], (), ["rmA"])
            dma(P, "gpsimd", bq[:, :], I.bq[:, :], (), ["bq"])
            for hd in range(4):
                kg = 2 + hd
                pc, ro = kg // 3, 64 * (kg % 3)
                dma(P, "sync", KTn[:, hd, 0:NT], S.kT_in[l][pc][ro:ro + 64, :], (), [("KTn", hd)])
                for j in range(4):
                    dma(P, "sync", KTn[:, hd, 2048 + 256 * j:2304 + 256 * j],
                        S.kT_all[l][pc][192 * j + ro:192 * j + ro + 64, 1792:2048], (), [("KTn", hd)])
                    dma(P, "sync", KTn[:, hd, 3072 + 256 * j:3328 + 256 * j],
                        S.kT_all[l][pc][192 * j + ro:192 * j + ro + 64, 0:256], (), [("KTn", hd)])
                dma(P, "sync", KTn[:, hd, 4096:4352], S.kTc[l][64 * kg:64 * kg + 64, :], (), [("KTn", hd)])
            for p in range(2):
                dma(P, "sync", Vn[:, 8 * p:8 * p + 8, :],
                    S.v_in[l][p][:, 130:390].rearrange("(c p) f -> p c f", p=128), (), ["Vn"])
            for j in range(4):
                dma(P, "sync", Vn[:, 16 + 2 * j:18 + 2 * j, :],
                    S.v_all[l][1][1024 * j + 768:1024 * j + 1024, 130:390].rearrange("(c p) f -> p c f", p=128), (),
                    ["Vn"])
                dma(P, "sync", Vn[:, 24 + 2 * j:26 + 2 * j, :],
                    S.v_all[l][0][1024 * j:1024 * j + 256, 130:390].rearrange("(c p) f -> p c f", p=128), (), ["Vn"])
            dma(P, "sync", Vn[:, 32:34, :], S.vc[l][:, 130:390].rearrange("(c p) f -> p c f", p=128), (), ["Vn"])
            for hd in range(4):
                b8 = bias8[hd % 2]
                for c in range(8):
                    for kl in range(2):
                        dp0 = 15 - 2 * c - kl
                        dma(P, "sync", stage[64 * kl:64 * kl + 64, c, :].rearrange("p (r q) -> p r q", q=64),
                            I.tpad[l, hd, dp0:dp0 + 8, :, :].rearrange("r k q -> k r q"), (), ["stage"])
                ts(P, "vector", b8[:, :, :], stage[:, :, :], 8.0, None, ALU.mult, None, ["stage"], [("b8", hd % 2)])
                it = 0
                for ti in range(5):
                    off, n, c_ = TILES[ti]
                    if c_ == 1 and last:
                        continue
                    chunks = []
                    ctxch = [(KTn[:, hd, 4096 + 128 * x:4224 + 128 * x], Vn[:, 32 + x, 65 * hd:65 * hd + 65],
                              [("KTn", hd), "Vn"], None) for x in range(2)]
                    if c_ == 0:
                        for (kind, j, c) in na_iters(ti):
                            if kind == "own":
                                ko = 512 * ti - 256 + 128 * c
                                kT_ap = KTn[:, hd, ko:ko + 128]
                                v_ap = Vn[:, ko // 128, 65 * hd:65 * hd + 65]
                            elif kind == "top":
                                ko = 2048 + 256 * j + 128 * c
                                kT_ap = KTn[:, hd, ko:ko + 128]
                                v_ap = Vn[:, 16 + 2 * j + c, 65 * hd:65 * hd + 65]
                            else:
                                ko = 3072 + 256 * j + 128 * (c - 6)
                                kT_ap = KTn[:, hd, ko:ko + 128]
                                v_ap = Vn[:, 24 + 2 * j + (c - 6), 65 * hd:65 * hd + 65]

                            def xf(ps_ap, pkey, c=c, it=it, b8=b8, hd=hd):
                                mm(P, ps_ap, ident_bf[:, :], b8[:, c, :], False, False, ["ident", ("b8", hd % 2)], [pkey])
                                mm(P, ps_ap, rmA[:, it, :], bq[:, :], False, True, ["rmA", "bq"], [pkey])
                            chunks.append((kT_ap, v_ap, [("KTn", hd), "Vn"], xf))
                            it += 1
                    chunks += ctxch
                    attn_head(B, S.qT[l][8 + hd, :, off:off + n], S.yT[l][12 + hd, :, off:off + n], n, chunks)
        return f


    def conv3(eng, out_ap, acc, w, bcol, n, rkeys, wkey):
        eng = "vector"
        ts(P, eng, out_ap, acc[:, 0:n], w[:, 0:1], bcol, ALU.mult, ALU.add, rkeys, [wkey])
        stt(P, eng, out_ap, acc[:, 1:n + 1], w[:, 1:2], out_ap, ALU.mult, ALU.add, rkeys + [wkey], [wkey])
        stt(P, eng, out_ap, acc[:, 2:n + 2], w[:, 2:3], out_ap, ALU.mult, ALU.add, rkeys + [wkey], [wkey])

    def ph_hypre(l):
        def f(lsb, lps):
            cw = lsb("cw", [64, 3, 3])
            cb = lsb("cb", [64, 3])
            cand = [lsb("cand0", [64, NT + 2]), lsb("cand1", [64, NT + 2])]
            acc = [lsb("acc%d" % i, [64, NT + 2]) for i in range(3)]
            uc = [lsb("uc%d" % i, [64, NT]) for i in range(3)]
            dma(P, "sync", cw[:, :, :], I.convw[l], (), ["cw"])
            dma(P, "sync", cb[:, :], I.convb[l], (), ["cb"])
            ci = 0
            for jb in range(4):
                for s_ in range(3):
                    for g in range(4):
                        cd = cand[ci % 2]
                        ck = ("cand", ci % 2)
                        ci += 1
                        src = S.uT_all[l][2 * s_ + g // 2]
                        r0 = 64 * (g % 2)
                        dma(P, "sync", cd[:, 1:NT + 1], src[128 * jb + r0:128 * jb + r0 + 64, :], (), [ck])
                        if jb > 0:
                            dma(P, "sync", cd[:, 0:1], src[128 * (jb - 1) + r0:128 * (jb - 1) + r0 + 64, NT - 1:NT], (),
                                [ck], slow=True)
                        else:
                            P.add("gpsimd", lambda e, cd=cd: e.memset(cd[:, 0:1], 0.0), (), [ck])
                        if jb < 3:
                            dma(P, "sync", cd[:, NT + 1:NT + 2], src[128 * (jb + 1) + r0:128 * (jb + 1) + r0 + 64, 0:1], (),
                                [ck], slow=True)
                        else:
                            P.add("gpsimd", lambda e, cd=cd: e.memset(cd[:, NT + 1:NT + 2], 0.0), (), [ck])
                        eng = "vector" if g % 2 == 0 else "gpsimd"
                        if g == 0:
                            ts(P, "vector", acc[s_][:, :], cd[:, :], sel[:, 0:1], None, ALU.mult, None, [ck, "sel"],
                               [("acc", s_)])
                        else:
                            stt(P, "vector", acc[s_][:, :], cd[:, :], sel[:, g:g + 1], acc[s_][:, :], ALU.mult, ALU.add,
                                [ck, "sel", ("acc", s_)], [("acc", s_)])
                    conv3("gpsimd" if s_ == 1 else "vector", uc[s_][:, :], acc[s_], cw[:, s_, :], cb[:, s_:s_ + 1], NT,
                          [("acc", s_), "cw", "cb"], ("uc", s_))
                tt(P, "gpsimd", uc[2][:, :], uc[2][:, :], uc[1][:, :], ALU.mult, [("uc", 1), ("uc", 2)], [("uc", 2)])
                dma(P, "sync", S.zT[l][:, NT * jb:NT * (jb + 1)], uc[2][:, :], [("uc", 2)], ["zT"])
                dma(P, "sync", S.x0T[l][:, NT * jb:NT * (jb + 1)], uc[0][:, :], [("uc", 0)], ["x0T"])
        return f

    def ph_hyprec(l):
        def f(lsb, lps):
            cw = lsb("cwc", [64, 12, 3])
            cb = lsb("cbc", [64, 12])
            u = lsb("uc_in", [64, 12, NC + 2])
            uc = lsb("uc_out", [64, 12, NC])
            dma(P, "sync", cw[:, :, :], I.convwc[l], (), ["cw"])
            dma(P, "sync", cb[:, :], I.convbc[l], (), ["cb"])
            P.add("gpsimd", lambda e: e.memset(u[:, :, 0:1], 0.0), (), ["u0"])
            P.add("gpsimd", lambda e: e.memset(u[:, :, NC + 1:NC + 2], 0.0), (), ["u1"])
            dma(P, "sync", u[:, :, 1:NC + 1], S.uTc[l].ap().rearrange("(q c) t -> c q t", c=64), (), ["u"])
            for q in range(12):
                conv3("vector" if q % 2 == 0 else "gpsimd", uc[:, q, :], u[:, q, :], cw[:, q, :], cb[:, q:q + 1], NC,
                      ["u", "u0", "u1", "cw", "cb"], ("ucc", q))
            for g in range(4):
                tt(P, "vector", uc[:, 8 + g, :], uc[:, 8 + g, :], uc[:, 4 + g, :], ALU.mult, [("ucc", 8 + g), ("ucc", 4 + g)],
                   [("ucc", 8 + g)])
                dma(P, "sync", S.zTc[l][g], uc[:, 8 + g, :], [("ucc", 8 + g)], ["zTc"])
                dma(P, "sync", S.x0Tc[l][g], uc[:, g, :], [("ucc", g)], ["x0Tc"])
        return f

    def ph_filt(l, A, groups, ndel, rn, dst):
        n2 = 128 * A
        n = n2 // 2
        TW = min(512, n)
        ntile = n2 // TW

        def f(lsb, lps):
            w1 = lsb("fw1", [33, 64])
            w2 = lsb("fw2", [64, 64])
            fv = lsb("fvec", [64, 3])
            w3 = lsb("fw3", [64, len(groups), 2, 64])
            nd = lsb("ndel", [64, len(groups)])
            ft = [lsb("ft0", [33, TW]), lsb("ft1", [33, TW])]
            tb = [lsb("tb0", [64, TW]), lsb("tb1", [64, TW])]
            pre = [lsb("pre0", [64, TW]), lsb("pre1", [64, TW])]
            h1 = [lsb("h10", [64, TW]), lsb("h11", [64, TW])]
            h2 = [lsb("h20", [64, TW]), lsb("h21", [64, TW])]
            dec = [lsb("dec0", [64, TW]), lsb("dec1", [64, TW])]
            kk = [lsb("kk0", [64, TW]), lsb("kk1", [64, TW])]
            ka = [lsb("ka0", [64, TW]), lsb("ka1", [64, TW])]
            part = lsb("part", [64, len(groups), ntile])
            assert TW <= 512
            ps1 = [lps("psf1_0"), lps("psf1_1")]
            ps2 = [lps("psf2_0"), lps("psf2_1")]
            ps3 = [lps("psf3_0"), lps("psf3_1")]
            pri = [lsb("pri0", [64, TW], mybir.dt.int32), lsb("pri1", [64, TW], mybir.dt.int32)]
            prf = [lsb("prf0", [64, TW]), lsb("prf1", [64, TW])]

            def sin_reduced(ps_ap, pkey, bcol, out_ap, okey, b):
                ts(P, "vector", pre[b][:, :], ps_ap, fv[:, bcol:bcol + 1], f2p[:, 0:1], ALU.add, ALU.mult,
                   [pkey, "f2p", "fv"], [("pre", b)])
                cp(P, "vector", pri[b][:, :], pre[b][:, :], [("pre", b)], [("pri", b)])
                cp(P, "gpsimd", prf[b][:, :], pri[b][:, :], [("pri", b)], [("prf", b)])
                tt(P, "vector", pre[b][:, :], pre[b][:, :], prf[b][:, :], ALU.subtract, [("pre", b), ("prf", b)],
                   [("pre", b)])
                act(P, out_ap, pre[b][:, :], AF.Sin, [("pre", b)], [okey], scale=2.0 * math.pi * (1.0 - 3e-7))

            f2p = lsb("f2p", [64, 1])
            dma(P, "sync", w1[:, :], I.fw1[l], (), ["w1"])
            dma(P, "sync", w2[:, :], I.fw2[l], (), ["w2"])
            dma(P, "sync", fv[:, :], I.fvec[l], (), ["fv"])
            ts(P, "vector", f2p[:, :], fv[:, 2:3], 1.0 / (2.0 * math.pi), None, ALU.mult, None, ["fv"], ["f2p"])
            dma(P, "sync", nd[:, :], ndel, (), ["nd"])
            for gi, (wf, wb) in enumerate(groups):
                dma(P, "sync", w3[:, gi, 0, :], wf, (), ["w3"])
                dma(P, "sync", w3[:, gi, 1, :], wb, (), ["w3"])
            it = 0
            for t in range(ntile):
                b = t % 2
                sl = slice(TW * t, TW * (t + 1))
                dma(P, "sync", ft[b][:, :], I.feats[A][:, sl], (), [("ft", b)])
                dma(P, "sync", tb[b][:, :], I.tlist[A][0:1, sl].partition_broadcast(64), (), [("tb", b)])
                mm(P, ps1[b][0:64, 0:TW], w1[:, :], ft[b][:, :], True, True, ["w1", ("ft", b)], [("ps1", b)])
                sin_reduced(ps1[b][0:64, 0:TW], ("ps1", b), 0, h1[b][:, :], ("h1", b), b)
                mm(P, ps2[b][0:64, 0:TW], w2[:, :], h1[b][:, :], True, True, ["w2", ("h1", b)], [("ps2", b)])
                sin_reduced(ps2[b][0:64, 0:TW], ("ps2", b), 1, h2[b][:, :], ("h2", b), b)
                dirn = 0 if TW * t < n else 1
                for gi in range(len(groups)):
                    c = it % 2
                    it += 1
                    mm(P, ps3[c][0:64, 0:TW], w3[:, gi, dirn, :], h2[b][:, :], True, True, ["w3", ("h2", b)],
                       [("ps3", c)])
                    act(P, dec[c][:, :], tb[b][:, :], AF.Exp, [("tb", b), "nd"], [("dec", c)], scale=nd[:, gi:gi + 1])
                    tt(P, "vector", kk[c][:, :], ps3[c][0:64, 0:TW], dec[c][:, :], ALU.mult, [("ps3", c), ("dec", c)],
                       [("kk", c)])
                    act(P, ka[c][:, :], kk[c][:, :], AF.Abs, [("kk", c)], [("ka", c)])
                    P.add("vector", lambda e, c=c, gi=gi, t=t: e.reduce_sum(part[:, gi, t:t + 1], ka[c][:, :], AX.X),
                          [("ka", c)], [("part", gi)])
                    dma(P, "sync", dst(gi)[:, sl], kk[c][:, :], [("kk", c)], ["kf"])
            for gi in range(len(groups)):
                P.add("vector", lambda e, gi=gi: e.reduce_sum(rn[:, gi:gi + 1], part[:, gi, :], AX.X), [("part", gi)],
                      [("rn", A)])
            P.add("vector", lambda e: e.reciprocal(rn[:, :], rn[:, :]), [("rn", A)], [("rn", A)])
        return f

    def ph_fft(l, A, jobs):
        Ah = A // 2
        NCH = max(j[3] for j in jobs)

        def f(lsb, lps):
            T = {}
            for nm, shp in (("F1v", [A, 4 * A]), ("TwA", [128, 2 * A]), ("TwB", [128, 2 * A]), ("TwTA", [A, 256]),
                            ("TwTB", [A, 256]), ("CA", [A, Ah]), ("SA", [A, Ah])):
                T[nm] = lsb("T" + nm, shp)
                dma(P, "sync", T[nm][:, :], I.ftab[A][nm][:, :], (), ["tab"])
            R4 = lsb("R4", [128, 4, 512])
            dma(P, "sync", R4[:, :, :], I.R4.rearrange("r p n -> p r n"), (), ["tab"])
            kmat = lsb("kmat", [A, NCH, 128])
            Kf = lsb("Kf", [A, NCH, 2, 128])
            Z = lsb("Zb", [Ah, NCH, 128])
            Yo = lsb("Yo", [Ah, NCH, 128])
            t1 = [lsb("ft1_%d" % i, [128, 256]) for i in range(2)]
            t2 = [lsb("ft2_%d" % i, [128, 256]) for i in range(2)]
            Yp = [lsb("Yp%d" % i, [128, 2 * A]) for i in range(2)]
            Pm = [lsb("Pm%d" % i, [A, 256]) for i in range(2)]
            PT = [lsb("PT%d" % i, [128, 2, A]) for i in range(2)]
            Gp = [lsb("Gp%d" % i, [A, 256]) for i in range(2)]
            psY = [lps("psY0"), lps("psY1")]
            psX = [lps("psX0"), lps("psX1")]
            psT = lps("psT")
            psG = [lps("psG0"), lps("psG1")]
            psO = lps("psO")
            def fwdA(i, data_ap, krows, dkeys):
                mm(P, psY[i][:, 0:4 * A], data_ap, T["F1v"][0:krows, :], True, True, dkeys + ["tab"], [("psY", i)])
                tt(P, "vector", t1[i][:, 0:2 * A], psY[i][:, 0:2 * A], T["TwA"][:, :], ALU.mult, [("psY", i), "tab"],
                   [("t1", i)])
                tt(P, "vector", t2[i][:, 0:2 * A], psY[i][:, 2 * A:4 * A], T["TwB"][:, :], ALU.mult, [("psY", i), "tab"],
                   [("t2", i)])
                tt(P, "gpsimd", Yp[i][:, :], t1[i][:, 0:2 * A], t2[i][:, 0:2 * A], ALU.add, [("t1", i), ("t2", i)],
                   [("Yp", i)])

            def fwdB(i):
                mm(P, psX[i][0:A, :], Yp[i][:, 0:A], R4[:, 0, :], True, False, [("Yp", i), "tab"], [("psX", i)])
                mm(P, psX[i][0:A, :], Yp[i][:, A:2 * A], R4[:, 1, :], False, True, [("Yp", i), "tab"], [("psX", i)])

            for (ksrc, zsrc, ydst, nch) in jobs:
                dma(P, "sync", kmat[:, 0:nch, :], ksrc.rearrange("c (a b) -> a c b", b=128), (), ["kmat"])
                dma(P, "sync", Z[:, 0:nch, :], zsrc.rearrange("c (a b) -> a c b", b=128), (), ["Z"])
                assert nch % 2 == 0
                for c0 in range(0, nch, 2):
                    for i in range(2):
                        fwdA(i, kmat[:, c0 + i, :], A, ["kmat"])
                    for i in range(2):
                        fwdB(i)
                    for i in range(2):
                        cp(P, "scalar", Kf[:, c0 + i, :, :], psX[i][0:A, 0:256].rearrange("p (r e) -> p r e", r=2),
                           [("psX", i)], [("Kf", c0 + i)])
                for c0 in range(0, nch, 2):
                    for i in range(2):
                        fwdA(i, Z[:, c0 + i, :], Ah, ["Z"])
                    for i in range(2):
                        fwdB(i)
                    for i in range(2):
                        ch = c0 + i
                        xv = psX[i][0:A, :]
                        tt(P, "vector", t1[i][0:A, :].rearrange("p (r e) -> p r e", r=2),
                           xv[:, 0:256].rearrange("p (r e) -> p r e", r=2),
                           Kf[:, ch, 0, :].unsqueeze(1).broadcast_to([A, 2, 128]), ALU.mult, [("psX", i), ("Kf", ch)],
                           [("t1", i)])
                        tt(P, "vector", t2[i][0:A, :].rearrange("p (r e) -> p r e", r=2),
                           xv[:, 256:512].rearrange("p (r e) -> p r e", r=2),
                           Kf[:, ch, 1, :].unsqueeze(1).broadcast_to([A, 2, 128]), ALU.mult, [("psX", i), ("Kf", ch)],
                           [("t2", i)])
                        tt(P, "gpsimd", Pm[i][:, :], t1[i][0:A, :], t2[i][0:A, :], ALU.add, [("t1", i), ("t2", i)],
                           [("Pm", i)])
                    for i in range(2):
                        for r in range(2):
                            P.add("tensor", lambda e, i=i, r=r: e.transpose(
                                psT[:, (2 * i + r) * A:(2 * i + r + 1) * A], Pm[i][:, 128 * r:128 * (r + 1)],
                                ident_f[0:A, 0:A]), [("Pm", i), "identf"], [("psT", i)])
                        cp(P, "scalar", PT[i][:, :, :],
                           psT[:, 2 * i * A:(2 * i + 2) * A].rearrange("p (r c) -> p r c", r=2), [("psT", i)],
                           [("PT", i)])
                    for i in range(2):
                        mm(P, psG[i][0:A, :], PT[i][:, 0, :], R4[:, 2, :], True, False, [("PT", i), "tab"], [("psG", i)])
                        mm(P, psG[i][0:A, :], PT[i][:, 1, :], R4[:, 3, :], False, True, [("PT", i), "tab"], [("psG", i)])
                    for i in range(2):
                        tt(P, "vector", t1[i][0:A, :], psG[i][0:A, 0:256], T["TwTA"][:, :], ALU.mult,
                           [("psG", i), "tab"], [("t1", i)])
                        tt(P, "vector", t2[i][0:A, :], psG[i][0:A, 256:512], T["TwTB"][:, :], ALU.mult,
                           [("psG", i), "tab"], [("t2", i)])
                        tt(P, "gpsimd", Gp[i][:, :], t1[i][0:A, :], t2[i][0:A, :], ALU.subtract, [("t1", i), ("t2", i)],
                           [("Gp", i)])
                    for i in range(2):
                        po = psO[0:Ah, 128 * i:128 * (i + 1)]
                        mm(P, po, T["CA"][:, :], Gp[i][:, 0:128], True, False, [("Gp", i), "tab"], [("psO", i)])
                        mm(P, po, T["SA"][:, :], Gp[i][:, 128:256], False, True, [("Gp", i), "tab"], [("psO", i)])
                        cp(P, "scalar", Yo[:, c0 + i, :], po, [("psO", i)], ["Yo"])
                dma(P, "sync", ydst.rearrange("c (a b) -> a c b", b=128), Yo[:, 0:nch, :], ["Yo"], ["ycv"])
        return f

    def ph_hypost(l):
        def f(lsb, lps):
            sk = lsb("sk", [64, 1])
            yc = [lsb("yc0", [64, NT]), lsb("yc1", [64, NT])]
            zb = [lsb("zb0", [64, NT]), lsb("zb1", [64, NT])]
            xb = [lsb("xb0", [64, NT]), lsb("xb1", [64, NT])]
            ob = [lsb("yob0", [64, NT], BF16), lsb("yob1", [64, NT], BF16)]
            cd = [lsb("ycd0", [64, NT], BF16), lsb("ycd1", [64, NT], BF16)]
            ac = [lsb("yac0", [64, NT], BF16), lsb("yac1", [64, NT], BF16)]
            dma(P, "sync", sk[:, :], I.skip_m[l], (), ["sk"])
            for jb in range(4):
                b = jb % 2
                sl = slice(NT * jb, NT * (jb + 1))
                dma(P, "sync", yc[b][:, :], S.ycv[l][:, sl], (), [("yc", b)])
                dma(P, "sync", zb[b][:, :], S.zT[l][:, sl], (), [("zb", b)])
                dma(P, "sync", xb[b][:, :], S.x0T[l][:, sl], (), [("xb", b)])
                ts(P, "vector", yc[b][:, :], yc[b][:, :], rn_m[:, 0:1], None, ALU.mult, None, [("yc", b), ("rn", 128)],
                   [("yc", b)])
                stt(P, "vector", yc[b][:, :], zb[b][:, :], sk[:, 0:1], yc[b][:, :], ALU.mult, ALU.add,
                    [("zb", b), "sk", ("yc", b)], [("yc", b)])
                tt(P, "gpsimd", ob[b][:, :], yc[b][:, :], xb[b][:, :], ALU.mult, [("yc", b), ("xb", b)], [("ob", b)])
                dma(P, "sync", S.yb_in[l][:, sl], ob[b][:, :], [("ob", b)], ["yb_in"])
            P.add("gpsimd", lambda e: e.collective_compute(
                "AllGather", ALU.bypass, replica_groups=G4, ins=[S.yb_in[l].ap().opt()], outs=[S.yb_all[l].ap().opt()]),
                ["yb_in"], ["yb_all"], kind="cc")
            ci = 0
            for g in range(4):
                a = ac[g % 2]
                for j in range(4):
                    c = cd[ci % 2]
                    ck = ("ycd", ci % 2)
                    ci += 1
                    dma(P, "sync", c[:, :], S.yb_all[l][64 * g:64 * g + 64, NT * j:NT * (j + 1)], ["yb_all"], [ck])
                    if j == 0:
                        ts(P, "vector", a[:, :], c[:, :], sel[:, 0:1], None, ALU.mult, None, [ck, "sel"], [("yac", g % 2)])
                    else:
                        stt(P, "vector", a[:, :], c[:, :], sel[:, j:j + 1], a[:, :], ALU.mult, ALU.add,
                            [ck, "sel", ("yac", g % 2)], [("yac", g % 2)])
                dma(P, "sync", S.yT[l][8 + g, :, 0:NT], a[:, :], [("yac", g % 2)], ())
        return f

    def ph_hypostc(l):
        def f(lsb, lps):
            sk = lsb("skc", [64, 4])
            yc = lsb("ycc", [64, 4, NC])
            zb = lsb("zbc", [64, 4, NC])
            xb = lsb("xbc", [64, 4, NC])
            ob = lsb("obc", [64, 4, NC], BF16)
            dma(P, "sync", sk[:, :], I.skip_c[l], (), ["sk"])
            dma(P, "sync", yc[:, :, :], S.ycvc[l].ap().rearrange("g c t -> c g t"), (), ["yc"])
            dma(P, "sync", zb[:, :, :], S.zTc[l].ap().rearrange("g c t -> c g t"), (), ["zb"])
            dma(P, "sync", xb[:, :, :], S.x0Tc[l].ap().rearrange("g c t -> c g t"), (), ["xb"])
            for g in range(4):
                ts(P, "vector", yc[:, g, :], yc[:, g, :], rn_c[:, g:g + 1], None, ALU.mult, None, ["yc", ("rn", 4)],
                   [("ycg", g)])
                stt(P, "vector", yc[:, g, :], zb[:, g, :], sk[:, g:g + 1], yc[:, g, :], ALU.mult, ALU.add,
                    ["zb", "sk", ("ycg", g)], [("ycg", g)])
                tt(P, "vector", ob[:, g, :], yc[:, g, :], xb[:, g, :], ALU.mult, [("ycg", g), "xb"], [("obc", g)])
                dma(P, "sync", S.yT[l][8 + g, :, NT:NTOT], ob[:, g, :], [("obc", g)], ())
        return f


    def ph_hyprec1(l):
        def f(lsb, lps):
            cw = lsb("cwc", [64, 3, 3])
            cb = lsb("cbc", [64, 3])
            u = lsb("uc_in", [64, 12, NC + 2])
            us = lsb("uc_sel", [64, 3, NC + 2])
            uc = lsb("uc_out", [64, 3, NC])
            dma(P, "sync", cw[:, :, :], I.convw[l], (), ["cw"])
            dma(P, "sync", cb[:, :], I.convb[l], (), ["cb"])
            P.add("gpsimd", lambda e: e.memset(u[:, :, 0:1], 0.0), (), ["u0"])
            P.add("gpsimd", lambda e: e.memset(u[:, :, NC + 1:NC + 2], 0.0), (), ["u1"])
            dma(P, "sync", u[:, :, 1:NC + 1], S.uTc[l].ap().rearrange("(q c) t -> c q t", c=64), (), ["u"])
            for s_ in range(3):
                for g in range(4):
                    if g == 0:
                        ts(P, "vector", us[:, s_, :], u[:, 4 * s_, :], sel[:, 0:1], None, ALU.mult, None,
                           ["u", "u0", "u1", "sel"], [("us", s_)])
                    else:
                        stt(P, "vector", us[:, s_, :], u[:, 4 * s_ + g, :], sel[:, g:g + 1], us[:, s_, :], ALU.mult,
                            ALU.add, ["u", "u0", "u1", "sel", ("us", s_)], [("us", s_)])
                conv3("vector", uc[:, s_, :], us[:, s_, :], cw[:, s_, :], cb[:, s_:s_ + 1], NC,
                      [("us", s_), "cw", "cb"], ("ucc", s_))
            tt(P, "vector", uc[:, 2, :], uc[:, 2, :], uc[:, 1, :], ALU.mult, [("ucc", 2), ("ucc", 1)], [("ucc", 2)])
            dma(P, "sync", S.zTc1[l][:, :], uc[:, 2, :], [("ucc", 2)], ["zTc"])
            dma(P, "sync", S.x0Tc1[l][:, :], uc[:, 0, :], [("ucc", 0)], ["x0Tc"])
        return f

    def ph_hypostc1(l):
        def f(lsb, lps):
            sk = lsb("skc", [64, 1])
            yc = lsb("ycc", [64, NC])
            zb = lsb("zbc", [64, NC])
            xb = lsb("xbc", [64, NC])
            ob = lsb("obc", [64, NC], BF16)
            dma(P, "sync", sk[:, :], I.skip_m[l], (), ["sk"])
            dma(P, "sync", yc[:, :], S.ycvc1[l][:, :], (), ["yc"])
            dma(P, "sync", zb[:, :], S.zTc1[l][:, :], (), ["zb"])
            dma(P, "sync", xb[:, :], S.x0Tc1[l][:, :], (), ["xb"])
            ts(P, "vector", yc[:, :], yc[:, :], rn_c[:, 0:1], None, ALU.mult, None, ["yc", ("rn", 4)], ["yc"])
            stt(P, "vector", yc[:, :], zb[:, :], sk[:, 0:1], yc[:, :], ALU.mult, ALU.add, ["zb", "sk", "yc"], ["yc"])
            tt(P, "vector", ob[:, :], yc[:, :], xb[:, :], ALU.mult, ["yc", "xb"], ["obc"])
            dma(P, "sync", S.ybc_in[l][:, :], ob[:, :], ["obc"], ["ybc_in"])
            P.add("gpsimd", lambda e: e.collective_compute(
                "AllGather", ALU.bypass, replica_groups=G4, ins=[S.ybc_in[l].ap().opt()],
                outs=[S.ybc_all[l].ap().opt()]), ["ybc_in"], ["ybc_all"], kind="cc")
            for g in range(4):
                dma(P, "sync", S.yT[l][8 + g, :, NT:NTOT], S.ybc_all[l][64 * g:64 * g + 64, :], ["ybc_all"], ())
        return f

    def ph_oproj(l, zero_groups=()):
        last = l == DEPTH - 1

        def f(lsb, lps):
            Wo = lsb("Wo", [64, 16, D], BF16)
            Y = [lsb("Y0", [64, 16, 512], BF16), lsb("Y1", [64, 16, 512], BF16)]
            ps = [lps("ps_op0"), lps("ps_op1")]
            dma(P, "gpsimd", Wo[:, :, :], I.w_out[l], (), ["Wo"])
            it = 0
            for ti in range(5):
                off, n, c = TILES[ti]
                if c == 1 and last:
                    continue
                y = Y[ti % 2]
                dma(P, "sync", y[:, :, 0:n], S.yT[l][:, :, off:off + n].rearrange("g f t -> f g t"), (),
                    [("Y", ti % 2)])
                for g in zero_groups:
                    P.add("gpsimd", lambda e, y=y, g=g, n=n: e.memset(y[:, g, 0:n], 0.0), [("Y", ti % 2)],
                          [("Y", ti % 2)])
                for dc in range(8):
                    b = it % 2
                    it += 1
                    for g in range(16):
                        mm(P, ps[b][:, 0:n], Wo[:, g, 128 * dc:128 * (dc + 1)], y[:, g, 0:n], g == 0, g == 15,
                           ["Wo", ("Y", ti % 2)], [("ps_op", b)])
                    stt(P, "vector", h[:, dc, off:off + n], ps[b][:, 0:n], Gm[:, 1, dc, c:c + 1],
                        h[:, dc, off:off + n], ALU.mult, ALU.add, [("ps_op", b), ("Gm", 1, c), ("h", dc, ti)],
                        [("h", dc, ti)])
        return f

    def ph_out(lsb, lps):
        for k in range(8):
            dma(P, "sync", out_hT[128 * k:128 * (k + 1), :], h[:, k, 0:NT], [("h", k, i) for i in range(4)], ())
            if DEBUG_OUT:
                dma(P, "sync", dbg[128 * k:128 * (k + 1), :], h[:, k, :], [("h", k, i) for i in range(5)], ())

    for l in range(DEPTH):
        phase(ph_adaln(l))
        phase(ph_ffn(l, 0, [0, 1]))
        phase(ph_ffn(l, 0, [2, 3, 4]))
        if stop_after == ("ffn1", l):
            break
        phase(ph_proj(l))
        phase(ph_attn(l))
        if stop_after == ("attn", l):
            phase(ph_oproj(l, zero_groups=range(8, 16)))
            break
        phase(ph_na(l))
        if stop_after == ("na", l):
            phase(ph_oproj(l, zero_groups=range(8, 12)))
            break
        phase(ph_hypre(l))
        phase(ph_filt(l, 128, [(I.w3m[l, :, 0, :], I.w3m[l, :, 1, :])], I.ndel_m[:, :], rn_m, lambda g: S.kfT[l]))
        phase(ph_fft(l, 128, [(S.kfT[l][32 * hf:32 * hf + 32, :], S.zT[l][32 * hf:32 * hf + 32, :],
                               S.ycv[l][32 * hf:32 * hf + 32, :], 32) for hf in range(2)]))
        phase(ph_hypost(l))
        if l < DEPTH - 1:
            phase(ph_hyprec1(l))
            phase(ph_filt(l, 4, [(I.w3m[l, :, 0, :], I.w3m[l, :, 1, :])], I.ndel_m[:, :], rn_c, lambda g: S.kfTc1[l]))
            phase(ph_fft(l, 4, [(S.kfTc1[l][32 * hf:32 * hf + 32, :], S.zTc1[l][32 * hf:32 * hf + 32, :],
                                 S.ycvc1[l][32 * hf:32 * hf + 32, :], 32) for hf in range(2)]))
            phase(ph_hypostc1(l))
        phase(ph_oproj(l))
        if stop_after == ("mix", l):
            break
        phase(ph_ffn(l, 2, [0, 1]))
        phase(ph_ffn(l, 2, [2, 3, 4] if l < DEPTH - 1 else [2, 3]))
    phase(ph_out)
    P.close()
    for cm in reversed(cms):
        cm.__exit__(None, None, None)
    return nc


def fft_tables(A):
    N = 128 * A
    a = np.arange(A)[:, None].astype(np.float64)
    c = np.arange(A)[None, :].astype(np.float64)
    FrA = np.cos(2 * np.pi * a * c / A)
    FiA = -np.sin(2 * np.pi * a * c / A)
    b = np.arange(128)[:, None].astype(np.float64)
    Twr = np.cos(2 * np.pi * b * c / N)
    Twi = -np.sin(2 * np.pi * b * c / N)
    a2 = np.arange(A // 2)[None, :].astype(np.float64)
    cp_ = np.arange(A)[:, None].astype(np.float64)
    f = lambda x: np.ascontiguousarray(x.astype(np.float32))
    return dict(F1v=f(np.concatenate([FrA, FiA, -FiA, FrA], 1)), TwA=f(np.concatenate([Twr, Twr], 1)),
                TwB=f(np.concatenate([Twi, Twi], 1)), TwTA=f(np.concatenate([Twr.T, Twr.T], 1)),
                TwTB=f(np.concatenate([Twi.T, Twi.T], 1)), CA=f(np.cos(2 * np.pi * cp_ * a2 / A) / N),
                SA=f(-np.sin(2 * np.pi * cp_ * a2 / A) / N))


def filter_tables(n):
    p = np.arange(2 * n)
    pos = np.where(p < n, p, 2 * n - p).astype(np.float32)
    pos[n] = 0.0
    t = (pos / np.float32(max(n - 1, 1))).astype(np.float32)
    bands = np.linspace(1e-4, 15, 16, dtype=np.float32)
    ang = (np.float32(2.0 * math.pi / n) * pos[:, None] * bands[None, :]).astype(np.float32)
    feats = np.concatenate([t[:, None], np.cos(ang), -np.sin(ang)], axis=-1).astype(np.float32)
    tl = t.copy()
    tl[n] = 1.0e4
    return np.ascontiguousarray(feats.T), np.ascontiguousarray(tl[None, :])


def hyena_deltas():
    return np.abs(np.linspace(math.log(1e-2) / 1.5, math.log(1e-2) / 0.3, 256, dtype=np.float32))


def prep_inputs(inputs):
    x = np.asarray(inputs["x"], np.float32)
    ctx = np.asarray(inputs["ctx"], np.float32)
    c = np.asarray(inputs["c"], np.float32)
    c_ctx = np.asarray(inputs["c_ctx"], np.float32)
    shared = {}
    f32 = lambda k: np.asarray(inputs[k], np.float32)
    shared["w_ada_p"] = np.ascontiguousarray(f32("w_ada").reshape(DEPTH, 8, 128, 9, 1024).transpose(0, 3, 2, 1, 4))
    ffn_names = {1: ("w_ffn1_gate", "w_ffn1_up", "w_ffn1_down"), 2: ("w_ffn2_gate", "w_ffn2_up", "w_ffn2_down")}
    for i in (1, 2):
        g_ = f32(ffn_names[i][0]).reshape(DEPTH, 8, 128, NFC, 128)
        u_ = f32(ffn_names[i][1]).reshape(DEPTH, 8, 128, NFC, 128)
        gu = np.stack([g_, u_], axis=0)
        shared["w_ffn%d_gu" % i] = np.ascontiguousarray(gu.transpose(1, 4, 3, 0, 2, 5))
        d_ = f32(ffn_names[i][2]).reshape(DEPTH, NFC, 128, 8, 128)
        shared["w_ffn%d_dn" % i] = np.ascontiguousarray(d_.transpose(0, 3, 2, 1, 4))
    g3 = np.stack([inputs["g_ffn1"], inputs["g_mix"], inputs["g_ffn2"]], axis=1).astype(np.float32)
    shared["g3T"] = np.ascontiguousarray(g3.reshape(DEPTH, 3, 8, 128).transpose(0, 3, 1, 2))
    shared["b_adaT"] = np.ascontiguousarray(
        np.asarray(inputs["b_ada"], np.float32).reshape(DEPTH, 72, 128).transpose(0, 2, 1))
    w_in = np.asarray(inputs["w_in"], np.float32)
    perm = np.concatenate([np.arange(0, 512), np.arange(1536, 1792), np.arange(512, 640), np.arange(1792, 2048),
                           np.arange(768, 1536), np.arange(640, 768), np.arange(2048, 2304)])
    shared["w_in_p"] = np.ascontiguousarray(w_in[:, :, perm].reshape(DEPTH, 8, 128, 2304).transpose(0, 2, 1, 3))
    shared["w_out_p"] = np.ascontiguousarray(f32("w_out").reshape(DEPTH, 16, 64, D).transpose(0, 2, 1, 3))
    shared["gqkT"] = np.ascontiguousarray(np.stack(
        [inputs["g_q_attn"], inputs["g_k_attn"], inputs["g_q_na"], inputs["g_k_na"]], axis=-1).astype(np.float32))
    rot = np.zeros((64, 64), np.float32)
    for i in range(64):
        if i % 32 < 16:
            rot[i, i + 16] = -1.0
        else:
            rot[i, i - 16] = 1.0
    shared["rotT"] = np.ascontiguousarray(rot.T)
    rpb = np.asarray(inputs["na_rpb"], np.float32)
    kc = np.arange(64)[:, None]
    qc = np.arange(64)[None, :]
    cs = np.clip(qc - 8, 0, 48)
    colvalid = (kc >= cs) & (kc < cs + 16)
    cidx = np.clip(kc - qc + 15, 0, 30)
    tpad = np.full((DEPTH, 4, 23, 64, 64), -BIG, np.float32)
    for dp in range(23):
        dr = 18 - dp
        if 0 <= dr <= 14:
            tpad[:, :, dp] = np.where(colvalid[None, None], rpb[:, :, dr][:, :, cidx], np.float32(-BIG))
    shared["na_tpad"] = tpad
    shared["na_bq"] = np.ascontiguousarray((np.arange(512)[None, :] // 64 == np.arange(8)[:, None]).astype(np.float32))
    shared["ident"] = np.eye(128, dtype=np.float32)
    for A_, n_ in ((128, SEQ), (4, NC)):
        for k_, v_ in fft_tables(A_).items():
            shared["%s_%d" % (k_, A_)] = v_
        ft_, tl_ = filter_tables(n_)
        shared["featsT_" + ("m" if A_ == 128 else "c")] = ft_
        shared["tlist_" + ("m" if A_ == 128 else "c")] = tl_
    bb = np.arange(128)[:, None].astype(np.float64)
    ee = np.arange(128)[None, :].astype(np.float64)
    Cm = np.cos(2 * np.pi * bb * ee / 128)
    Sm = np.sin(2 * np.pi * bb * ee / 128)
    shared["fftR4"] = np.ascontiguousarray(np.stack([
        np.concatenate([Cm, -Sm, Sm, Cm], 1), np.concatenate([Sm, Cm, -Cm, Sm], 1),
        np.concatenate([Cm, Sm, -Sm, Cm], 1), np.concatenate([-Sm, Cm, -Cm, -Sm], 1)]).astype(np.float32))
    conv_w = np.asarray(inputs["conv_w"], np.float32)
    conv_b = np.asarray(inputs["conv_b"], np.float32)
    cw4 = conv_w.reshape(DEPTH, 3, 3, 4, 64)
    cb4 = conv_b.reshape(DEPTH, 3, 4, 64)
    shared["hyc_convw"] = np.ascontiguousarray(cw4.transpose(0, 4, 2, 3, 1).reshape(DEPTH, 64, 12, 3))
    shared["hyc_convb"] = np.ascontiguousarray(cb4.transpose(0, 3, 1, 2).reshape(DEPTH, 64, 12))
    shared["fw1"] = np.ascontiguousarray(np.asarray(inputs["filt_w1"], np.float32))
    shared["fw2"] = np.ascontiguousarray(np.asarray(inputs["filt_w2"], np.float32))
    shared["fvec"] = np.ascontiguousarray(np.stack(
        [inputs["filt_b1"], inputs["filt_b2"], inputs["filt_freq"]], axis=-1).astype(np.float32))
    w3 = np.asarray(inputs["filt_w3"], np.float32)
    shared["w3c"] = np.ascontiguousarray(w3)
    skip = np.asarray(inputs["hyena_skip"], np.float32)
    shared["skip_c"] = np.ascontiguousarray(skip.reshape(DEPTH, 4, 64).transpose(0, 2, 1))
    dl = hyena_deltas()
    shared["ndel_c"] = np.ascontiguousarray((-dl).reshape(4, 64).T)
    inv = (10000.0 ** (-np.arange(16, dtype=np.float32) / 16)).astype(np.float32)
    maps = []
    for r in range(8):
        b, qd = r // 4, r % 4
        m = dict(shared)
        m["xT"] = np.ascontiguousarray(x[b, NT * qd:NT * (qd + 1), :].T)
        m["ctxT"] = np.ascontiguousarray(ctx[b].T)
        cc = np.stack([c[b], c_ctx], axis=-1)
        m["cT"] = np.ascontiguousarray(cc.reshape(8, 128, 2).transpose(1, 0, 2))
        sl_ = np.zeros((64, 4), np.float32)
        sl_[:, qd] = 1.0
        m["sel"] = sl_
        m["hy_convw"] = np.ascontiguousarray(cw4[:, :, :, qd, :].transpose(0, 3, 2, 1))
        m["hy_convb"] = np.ascontiguousarray(cb4[:, :, qd, :].transpose(0, 2, 1))
        m["w3m"] = np.ascontiguousarray(w3.reshape(DEPTH, 64, 2, 4, 64)[:, :, :, qd, :])
        m["ndel_m"] = np.ascontiguousarray(-dl[64 * qd:64 * qd + 64, None])
        m["skip_m"] = np.ascontiguousarray(skip[:, 64 * qd:64 * qd + 64, None])
        rm = np.zeros((8, 44, 128), np.float32)
        it = 0
        for ti in range(4):
            for (kind, j, cch) in na_iters(ti):
                for qrl in range(8):
                    qr = 32 * qd + 8 * ti + qrl
                    rs = min(max(qr - 4, 0), 120)
                    for kl in range(2):
                        kr = 32 * qd + 8 * ti - 4 + 2 * cch + kl
                        ok = rs <= kr < rs + 8
                        if kind == "top":
                            ok = ok and (j == qd - 1)
                        elif kind == "bot":
                            ok = ok and (j == qd + 1)
                        if not ok:
                            rm[qrl, it, 64 * kl:64 * kl + 64] = -8.0 * BIG
                it += 1
        assert it == 44
        m["na_rmA"] = rm
        t = np.arange(NT * qd, NT * (qd + 1))
        rows = (t // GRID_W).astype(np.float32)
        cols = (t % GRID_W).astype(np.float32)
        ang = np.concatenate([rows[None, :] * inv[:, None]] * 2 + [cols[None, :] * inv[:, None]] * 2, axis=0)
        m["ropecos"] = np.ascontiguousarray(np.cos(ang).astype(np.float32))
        m["ropesin"] = np.ascontiguousarray(np.sin(ang).astype(np.float32))
        maps.append(m)
    return maps


def kernel(**inputs):
    nc = build_nc()
    maps = prep_inputs(inputs)
    res = run_bass_kernel_spmd(nc, maps, core_ids=list(range(8)))
    out = np.zeros((2, SEQ, D), np.float32)
    for r in range(8):
        b, qd = r // 4, r % 4
        out[b, NT * qd:NT * (qd + 1), :] = res.results[r]["out_hT"].T
    return out
```
